# Optimizing a Trainium2 kernel written in Bass

```python
import jax, jax.numpy as jnp
from jax import lax
import numpy as np

D_MODEL = 2048
BATCH = 4
SEQ = 2048
DEPTH = 4
DEC_BATCH = 128
DEC_SEQ = 1
PAST_LEN = 16384
PAGE_SIZE = 128

D_MIX = 2 * D_MODEL
RET_HEADS = 8
RET_DIM = D_MIX // 4
RET_HEAD_DIM = RET_DIM // RET_HEADS
SC_DIM = D_MIX // 4
SC_WIDTH = 3
SSM_DIM = D_MIX // 2
SSM_HEAD_DIM = 64
SSM_HEADS = SSM_DIM // SSM_HEAD_DIM
SSM_GROUPS = 4
SSM_STATE = 128
SSM_CONV = 4
SSM_CONV_DIM = SSM_DIM + 2 * SSM_GROUPS * SSM_STATE
CHUNK = 128
ROPE_BASE = 10000.0
EPS = 1e-6
SPLIT_SIZES = (RET_DIM, RET_DIM, RET_DIM, RET_DIM, SC_DIM, SC_DIM, SC_DIM, SC_DIM, SSM_DIM, SSM_CONV_DIM, SSM_HEADS)
D_PROJ = sum(SPLIT_SIZES)

kernel_name = "hybrid_retention_shortconv_ssd_decoder_step"


def _split_points():
    return [int(p) for p in np.cumsum(SPLIT_SIZES)[:-1]]


def _rmsnorm(x, g):
    x32 = x.astype(jnp.float32)
    y = x32 * lax.rsqrt(jnp.mean(x32 * x32, axis=-1, keepdims=True) + EPS)
    return (y * g.astype(jnp.float32)).astype(x.dtype)


def _head_norm(o, g):
    mu = jnp.mean(o, axis=-1, keepdims=True)
    oc = o - mu
    var = jnp.mean(oc * oc, axis=-1, keepdims=True)
    on = oc * lax.rsqrt(var + EPS)
    return on.reshape(o.shape[0], o.shape[1], -1) * g.astype(jnp.float32)


def _rope(t, pos):
    half = t.shape[-1] // 2
    inv = ROPE_BASE ** (-jnp.arange(half, dtype=jnp.float32) / half)
    ang = pos.astype(jnp.float32)[:, None] * inv[None, :]
    cos = jnp.cos(ang)[None, :, None, :]
    sin = jnp.sin(ang)[None, :, None, :]
    t1, t2 = t[..., :half], t[..., half:]
    return jnp.concatenate([t1 * cos - t2 * sin, t1 * sin + t2 * cos], axis=-1)


def _chunk_len(L):
    return CHUNK if L % CHUNK == 0 else L


def _to_chunks(a, c):
    b, l = a.shape[:2]
    return jnp.swapaxes(a.reshape(b, l // c, c, *a.shape[2:]), 0, 1)


def _from_chunks(a):
    n, b, c = a.shape[:3]
    return jnp.swapaxes(a, 0, 1).reshape(b, n * c, *a.shape[3:])


def _causal_conv(u, buf, w, b):
    width = w.shape[0]
    L = u.shape[1]
    xp = jnp.concatenate([buf, u], axis=1)
    out = b.astype(jnp.float32)
    for j in range(width):
        out = out + xp[:, j:j + L] * w[j].astype(jnp.float32)
    return out, xp[:, xp.shape[1] - (width - 1):]


def _retention(q, k, v, s0):
    L = q.shape[1]
    c = _chunk_len(L)
    lg = jnp.log(1.0 - 2.0 ** (-5.0 - jnp.arange(RET_HEADS, dtype=jnp.float32)))
    i = jnp.arange(c, dtype=jnp.float32)
    diff = i[:, None] - i[None, :]
    dmat = jnp.exp(jnp.where((diff >= 0)[None], diff[None] * lg[:, None, None], -jnp.inf))
    q_dec = jnp.exp((i + 1.0)[:, None] * lg[None, :])
    k_dec = jnp.exp((c - 1.0 - i)[:, None] * lg[None, :])
    chunk_dec = jnp.exp(c * lg)

    def step(s, inp):
        qc, kc, vc = inp
        scores = jnp.einsum('bihd,bjhd->bhij', qc, kc) * dmat[None]
        o = jnp.einsum('bhij,bjhe->bihe', scores, vc)
        o = o + jnp.einsum('bihd,bhde->bihe', qc, s) * q_dec[None, :, :, None]
        s = chunk_dec[None, :, None, None] * s + jnp.einsum('bjhd,bjhe->bhde', kc * k_dec[None, :, :, None], vc)
        return s, o

    s, o = lax.scan(step, s0, (_to_chunks(q, c), _to_chunks(k, c), _to_chunks(v, c)))
    return _from_chunks(o), s


def _ssd(x, dt, A, bm, cm, s0):
    L = x.shape[1]
    c = _chunk_len(L)
    rep = SSM_HEADS // SSM_GROUPS
    bh = jnp.repeat(bm, rep, axis=2)
    ch = jnp.repeat(cm, rep, axis=2)
    a = dt * A.astype(jnp.float32)
    ii = jnp.arange(c)
    mask = (ii[:, None] >= ii[None, :])[None, :, :, None]

    def step(s, inp):
        xc, dtc, ac, bc, cc = inp
        acum = jnp.cumsum(ac, axis=1)
        seg = acum[:, :, None, :] - acum[:, None, :, :]
        lmat = jnp.exp(jnp.where(mask, seg, -jnp.inf))
        cb = jnp.einsum('bihn,bjhn->bijh', cc, bc)
        y = jnp.einsum('bijh,bjhp->bihp', cb * lmat * dtc[:, None, :, :], xc)
        y = y + jnp.einsum('bihn,bhpn->bihp', cc, s) * jnp.exp(acum)[..., None]
        last = acum[:, -1]
        wj = jnp.exp(last[:, None, :] - acum) * dtc
        s = jnp.exp(last)[:, :, None, None] * s + jnp.einsum('bjhn,bjhp->bhpn', bc * wj[..., None], xc)
        return s, y

    xs = (_to_chunks(x, c), _to_chunks(dt, c), _to_chunks(a, c), _to_chunks(bh, c), _to_chunks(ch, c))
    s, y = lax.scan(step, s0, xs)
    return _from_chunks(y), s


def _layer(x, pos, ret_s, sc_buf, ssm_buf, ssm_s, w_in, w_out, g_pre, g_post, g_ret,
           sc_w, sc_b, ssm_w, ssm_b, dt_bias, a_log, d_skip, g_ssm):
    f32 = jnp.float32
    bsz, L, _ = x.shape
    h = _rmsnorm(x, g_pre)
    proj = jnp.einsum('bld,dp->blp', h, w_in).astype(f32)
    q, k, v, g_r, sc_bg, sc_cg, sc_h, g_sc, z, xbc, dt = jnp.split(proj, _split_points(), axis=-1)

    q = _rope(q.reshape(bsz, L, RET_HEADS, RET_HEAD_DIM), pos)
    k = _rope(k.reshape(bsz, L, RET_HEADS, RET_HEAD_DIM), pos) * (RET_HEAD_DIM ** -0.5)
    v = v.reshape(bsz, L, RET_HEADS, RET_HEAD_DIM)
    o_ret, ret_new = _retention(q, k, v, ret_s.astype(f32))
    o_ret = _head_norm(o_ret, g_ret) * jax.nn.silu(g_r)

    conv, sc_new = _causal_conv(sc_cg * sc_h, sc_buf.astype(f32), sc_w, sc_b)
    o_sc = sc_bg * conv * jax.nn.silu(g_sc)

    xbc_c, ssm_conv_new = _causal_conv(xbc, ssm_buf.astype(f32), ssm_w, ssm_b)
    xbc_c = jax.nn.silu(xbc_c)
    xs, bm, cm = jnp.split(xbc_c, [SSM_DIM, SSM_DIM + SSM_GROUPS * SSM_STATE], axis=-1)
    xs = xs.reshape(bsz, L, SSM_HEADS, SSM_HEAD_DIM)
    bm = bm.reshape(bsz, L, SSM_GROUPS, SSM_STATE)
    cm = cm.reshape(bsz, L, SSM_GROUPS, SSM_STATE)
    dt = jax.nn.softplus(dt + dt_bias.astype(f32))
    A = -jnp.exp(a_log.astype(f32))
    y, ssm_new = _ssd(xs, dt, A, bm, cm, ssm_s.astype(f32))
    y = y + d_skip.astype(f32)[:, None] * xs
    o_ssm = _rmsnorm(y.reshape(bsz, L, SSM_DIM) * jax.nn.silu(z), g_ssm)

    mix = jnp.concatenate([o_ret, o_sc, o_ssm], axis=-1).astype(x.dtype)
    out = _rmsnorm(jnp.einsum('blm,md->bld', mix, w_out), g_post)
    dt_out = x.dtype
    return x + out, (ret_new.astype(dt_out), sc_new.astype(dt_out), ssm_conv_new.astype(dt_out), ssm_new.astype(dt_out))


def setup_inputs(seed: int = 0) -> dict:
    key = jax.random.key(seed)
    ks = jax.random.split(key, 20)
    f32 = jnp.float32
    nrm = lambda k, s: jax.random.normal(k, s, f32)
    dt0 = jnp.exp(jax.random.uniform(ks[14], (DEPTH, SSM_HEADS), f32, np.log(1e-3), np.log(1e-1)))
    return {
        "x_prompt": nrm(ks[0], (BATCH, SEQ, D_MODEL)),
        "x_sample": nrm(ks[1], (DEC_BATCH, DEC_SEQ, D_MODEL)),
        "state_ret": 0.5 * nrm(ks[2], (DEPTH, DEC_BATCH, RET_HEADS, RET_HEAD_DIM, RET_HEAD_DIM)),
        "state_sconv": nrm(ks[3], (DEPTH, DEC_BATCH, SC_WIDTH - 1, SC_DIM)),
        "state_ssm_conv": nrm(ks[4], (DEPTH, DEC_BATCH, SSM_CONV - 1, SSM_CONV_DIM)),
        "state_ssm": 0.1 * nrm(ks[5], (DEPTH, DEC_BATCH, SSM_HEADS, SSM_HEAD_DIM, SSM_STATE)),
        "w_in": nrm(ks[6], (DEPTH, D_MODEL, D_PROJ)) * D_MODEL ** -0.5,
        "w_out": nrm(ks[7], (DEPTH, D_MIX, D_MODEL)) * D_MIX ** -0.5,
        "norm_pre": 1.0 + 0.05 * nrm(ks[8], (DEPTH, D_MODEL)),
        "norm_post": 1.0 + 0.05 * nrm(ks[9], (DEPTH, D_MODEL)),
        "ret_norm": 1.0 + 0.05 * nrm(ks[10], (DEPTH, RET_DIM)),
        "sc_conv_w": nrm(ks[11], (DEPTH, SC_WIDTH, SC_DIM)) * SC_WIDTH ** -0.5,
        "sc_conv_b": 0.02 * nrm(ks[12], (DEPTH, SC_DIM)),
        "ssm_conv_w": nrm(ks[13], (DEPTH, SSM_CONV, SSM_CONV_DIM)) * SSM_CONV ** -0.5,
        "ssm_conv_b": 0.02 * nrm(ks[15], (DEPTH, SSM_CONV_DIM)),
        "ssm_dt_bias": dt0 + jnp.log(-jnp.expm1(-dt0)),
        "ssm_a_log": jnp.log(jax.random.uniform(ks[16], (DEPTH, SSM_HEADS), f32, 1.0, 16.0)),
        "ssm_d": 1.0 + 0.1 * nrm(ks[17], (DEPTH, SSM_HEADS)),
        "ssm_norm": 1.0 + 0.05 * nrm(ks[18], (DEPTH, SSM_DIM)),
    }


def reference(x_prompt, x_sample, state_ret, state_sconv, state_ssm_conv, state_ssm,
              w_in, w_out, norm_pre, norm_post, ret_norm, sc_conv_w, sc_conv_b,
              ssm_conv_w, ssm_conv_b, ssm_dt_bias, ssm_a_log, ssm_d, ssm_norm):
    f32 = jnp.float32
    bp, lp = x_prompt.shape[0], x_prompt.shape[1]
    ls = x_sample.shape[1]
    pos_p = jnp.arange(lp)
    pos_s = PAST_LEN + jnp.arange(ls)
    z_ret = jnp.zeros((bp, RET_HEADS, RET_HEAD_DIM, RET_HEAD_DIM), f32)
    z_sc = jnp.zeros((bp, SC_WIDTH - 1, SC_DIM), f32)
    z_sconv = jnp.zeros((bp, SSM_CONV - 1, SSM_CONV_DIM), f32)
    z_ssm = jnp.zeros((bp, SSM_HEADS, SSM_HEAD_DIM, SSM_STATE), f32)

    yp, ys = x_prompt, x_sample
    np_ret, np_sc, np_sconv, np_ssm = [], [], [], []
    ns_ret, ns_sc, ns_sconv, ns_ssm = [], [], [], []
    for l in range(DEPTH):
        lw = (w_in[l], w_out[l], norm_pre[l], norm_post[l], ret_norm[l], sc_conv_w[l], sc_conv_b[l],
              ssm_conv_w[l], ssm_conv_b[l], ssm_dt_bias[l], ssm_a_log[l], ssm_d[l], ssm_norm[l])
        yp, (r, c, sc, s) = _layer(yp, pos_p, z_ret, z_sc, z_sconv, z_ssm, *lw)
        np_ret.append(r); np_sc.append(c); np_sconv.append(sc); np_ssm.append(s)
        ys, (r, c, sc, s) = _layer(ys, pos_s, state_ret[l], state_sconv[l], state_ssm_conv[l], state_ssm[l], *lw)
        ns_ret.append(r); ns_sc.append(c); ns_sconv.append(sc); ns_ssm.append(s)

    return (yp, ys,
            jnp.stack(np_ret), jnp.stack(np_sc), jnp.stack(np_sconv), jnp.stack(np_ssm),
            jnp.stack(ns_ret), jnp.stack(ns_sc), jnp.stack(ns_sconv), jnp.stack(ns_ssm))
```

```python
import numpy as np
from contextlib import ExitStack
import concourse.bass as bass
import concourse.mybir as mybir
from concourse.bass_utils import run_bass_kernel_spmd

F32 = mybir.dt.float32
BF16 = mybir.dt.bfloat16
ALU = mybir.AluOpType
AF = mybir.ActivationFunctionType
AX = mybir.AxisListType

D_MODEL = 2048
DEPTH = 4
BATCH = 4
SEQ = 2048
DEC_B = 128
PAST_LEN = 16384
D_PROJ = 13344
EPS = 1e-6
NT = 4
SEGT = NT * 128
KC = 16
SB = 16
C_Q, C_K, C_V, C_GR, C_BG, C_CG, C_H, C_GSC, C_Z, C_XBC, C_DT = 0, 1024, 2048, 3072, 4096, 5120, 6144, 7168, 8192, 10240, 13312
GAM = [1.0 - 2.0 ** (-5.0 - h) for h in range(8)]


class Track:
    __slots__ = ("w", "r")

    def __init__(self):
        self.w = None
        self.r = []


class Tile:
    __slots__ = ("ap", "tracks", "arena", "off")

    def __init__(self, ap, tracks, arena=None, off=0):
        self.ap = ap
        self.tracks = tracks
        self.arena = arena
        self.off = off

    def sub(self, ap, woff, wlen):
        a = self.arena
        o = self.off + woff
        return Tile(ap, a.blocks[o // a.G:(o + wlen + a.G - 1) // a.G], a, o)


def _flat(lst):
    out = []
    for x in lst:
        if isinstance(x, Tile):
            out.extend(x.tracks)
        elif isinstance(x, Track):
            out.append(x)
        elif x is None:
            pass
        else:
            out.extend(_flat(x))
    return out


class Arena:
    G = 64

    def __init__(self, tensor, nwords):
        self.t = tensor
        self.n = nwords
        self.off = 0
        self.peak = 0
        self.blocks = [Track() for _ in range(nwords // self.G + 2)]

    def tile(self, words, dt=F32, pat=None, parts=128, **kw):
        req = words
        words = (words + self.G - 1) // self.G * self.G
        off = self.off
        self.off += words
        self.peak = max(self.peak, self.off)
        assert self.off <= self.n, "SBUF arena overflow %d > %d" % (self.off, self.n)
        ap = self.t[0:parts, off:off + req]
        if dt is BF16:
            ap = ap.bitcast(BF16)
        if pat:
            ap = ap.rearrange(pat, **kw)
        return Tile(ap, self.blocks[off // self.G:(off + words) // self.G], self, off)

    def mark(self):
        return self.off

    def release(self, m):
        self.off = m


class Sched:
    COMPUTE = ("pe", "act", "dve", "pool")

    def __init__(self, nc, nslots_sp=14, nslots_pool=8):
        self.nc = nc
        self.ops = []
        self.nslots = {"sp": nslots_sp, "pool": nslots_pool}
        self.dma_count = {"sp": 0, "pool": 0}

    def op(self, eng, fn, reads=(), writes=()):
        self.ops.append((eng, fn, _flat(reads), _flat(writes), None))

    def dma(self, queue, out, in_, reads=(), writes=(), **kw):
        n = self.dma_count[queue]
        self.dma_count[queue] += 1
        slot = (queue, n % self.nslots[queue])
        self.ops.append((queue, None, _flat(reads), _flat(writes), (out, in_, slot, kw)))

    def emit(self, stack):
        nc = self.nc
        ops = self.ops
        n = len(ops)
        deps_all = [None] * n
        slot_last = {}
        needs = set()
        for i, (eng, fn, R, W, dma) in enumerate(ops):
            deps = set()
            for t in R:
                if t.w is not None:
                    deps.add(t.w)
            for t in W:
                if t.w is not None:
                    deps.add(t.w)
                if t.r:
                    deps.update(t.r)
            if dma:
                s = dma[2]
                if s in slot_last:
                    deps.add(slot_last[s])
                slot_last[s] = i
            deps.discard(i)
            deps_all[i] = deps
            needs |= deps
            for t in R:
                t.r.append(i)
            for t in W:
                t.w = i
                t.r = []
        sems = {}
        for e in self.COMPUTE:
            sems[e] = stack.enter_context(nc.semaphore("s_" + e))
        for q, ns in self.nslots.items():
            for k in range(min(ns, self.dma_count[q])):
                sems[(q, k)] = stack.enter_context(nc.semaphore("d_%s%d" % (q, k)))
        cnt = {k: 0 for k in sems}
        sig = [None] * n
        for i, (eng, fn, R, W, dma) in enumerate(ops):
            if dma:
                k = dma[2]
                cnt[k] += 16
                sig[i] = (k, cnt[k])
            elif i in needs:
                cnt[eng] += 1
                sig[i] = (eng, cnt[eng])
        final = dict(cnt)
        streams = {e: [] for e in ("pe", "act", "dve", "pool", "sp")}
        seen = {e: {} for e in streams}
        nw = 0
        for i, (eng, fn, R, W, dma) in enumerate(ops):
            waits = {}
            se = seen[eng]
            for d in deps_all[i]:
                k, v = sig[d]
                if se.get(k, 0) < v and waits.get(k, 0) < v:
                    waits[k] = v
            for k, v in waits.items():
                se[k] = v
            nw += len(waits)
            streams[eng].append((i, waits))
        self.nwaits = nw
        self.final = final
        block = stack.enter_context(nc.Block())

        def run_stream(e, eng):
            for i, waits in streams[e]:
                for k, v in waits.items():
                    eng.wait_ge(sems[k], v)
                _, fn, _, _, dma = ops[i]
                if dma:
                    ins = eng.dma_start(out=dma[0], in_=dma[1], **dma[3])
                    ins.then_inc(sems[sig[i][0]], 16)
                else:
                    ins = fn(eng)
                    if sig[i] is not None:
                        ins.then_inc(sems[sig[i][0]], 1)
            for k, v in final.items():
                if isinstance(k, tuple) and k[0] == e and v > 0:
                    eng.wait_ge(sems[k], v)

        @block.sync
        def _(eng):
            run_stream("sp", eng)

        @block.tensor
        def _(eng):
            run_stream("pe", eng)

        @block.scalar
        def _(eng):
            run_stream("act", eng)

        @block.vector
        def _(eng):
            run_stream("dve", eng)

        @block.gpsimd
        def _(eng):
            run_stream("pool", eng)


def _rope_tables(pos, kscale):
    half = 64
    inv = (np.float32(10000.0) ** (-np.arange(half, dtype=np.float32) / np.float32(half))).astype(np.float32)
    ang = (pos.astype(np.float32)[:, None] * inv[None, :]).astype(np.float32)
    cos = np.cos(ang.astype(np.float64))
    sin = np.sin(ang.astype(np.float64))
    n = len(pos)
    cs = np.zeros((n, 2, 2, 64), np.float64)
    sc = np.zeros((n, 2, 2, 64), np.float64)
    cs[:, 0, 0], cs[:, 0, 1] = cos, sin
    sc[:, 0, 0], sc[:, 0, 1] = sin, cos
    cs[:, 1, 0], cs[:, 1, 1] = cos * kscale, sin * kscale
    sc[:, 1, 0], sc[:, 1, 1] = sin * kscale, cos * kscale
    return cs.reshape(n, 256).astype(np.float32), sc.reshape(n, 256).astype(np.float32)


def _consts(seqlen):
    c = {}
    c["ident"] = np.eye(128, dtype=np.float32)
    j = np.arange(128)[:, None]
    i = np.arange(128)[None, :]
    c["tri"] = (j <= i).astype(np.float32)
    c["ones"] = np.ones((128, 128), np.float32)
    c["maskneg"] = np.where(i >= j, 0.0, -30000.0).astype(np.float32)
    g = np.array(GAM, np.float64)
    m = np.zeros((128, 8, 128), np.float64)
    for h in range(8):
        m[:, h, :] = np.where(i >= j, g[h] ** (-(j + 1.0)), 0.0)
    c["mret"] = m.astype(np.float32)
    c["qdec"] = (g[None, :] ** (np.arange(128)[:, None] + 1.0)).astype(np.float32)
    c["kdec"] = (g[None, :] ** (127.0 - np.arange(128)[:, None])).astype(np.float32)
    c["cdec"] = np.broadcast_to((g ** 128.0)[None, :], (128, 8)).astype(np.float32).copy()
    ks = 128.0 ** -0.5
    c["cs_p"], c["sc_p"] = _rope_tables(np.arange(seqlen), ks)
    cs_s, sc_s = _rope_tables(np.array([PAST_LEN]), ks)
    c["cs_s"] = np.broadcast_to(cs_s, (128, 256)).copy()
    c["sc_s"] = np.broadcast_to(sc_s, (128, 256)).copy()
    c["gam_p"] = np.repeat(np.array(GAM, np.float32), SB)[:, None].copy()
    p = np.arange(128)
    c["selb"] = (p[:, None] % SB == p[None, :] % SB).astype(np.float32)
    sel = np.zeros((4, 64, 128), np.float32)
    for hh in range(8):
        gidx = hh // 2
        for b in range(SB):
            for which, base in ((0, 0), (1, 2)):
                blk = base + gidx // 2
                sel[which * 2 + (gidx % 2), blk * SB + b, hh * SB + b] = 1.0
    c["sel4"] = sel
    return c


def build_program(depth=DEPTH, nseg=SEQ // SEGT, debug=False):
    L = depth
    seqlen = nseg * SEGT
    nc = bass.Bass("TRN2", target_bir_lowering=False)

    def din(name, shape):
        return nc.dram_tensor(name, list(shape), F32, kind="ExternalInput").ap()

    def dout(name, shape):
        return nc.dram_tensor(name, list(shape), F32, kind="ExternalOutput").ap()

    xp = din("xp", [seqlen, D_MODEL])
    xs = din("xs", [SB, D_MODEL])
    st_ret = din("st_ret", [L, 128, 128, 128])
    st_sc = din("st_sc", [L, SB, 2, 1024])
    st_sconv = din("st_sconv", [L, SB, 3, 3072])
    st_ssm = din("st_ssm", [L, 128, 4, 64, 128])
    w_in = din("w_in", [L, D_MODEL, D_PROJ])
    w_out = din("w_out", [L, 4096, D_MODEL])
    norm_pre = din("norm_pre", [L, 2048])
    norm_post = din("norm_post", [L, 2048])
    ret_norm = din("ret_norm", [L, 1024])
    sc_w = din("sc_conv_w", [L, 3, 1024])
    sc_b = din("sc_conv_b", [L, 1024])
    ssm_w = din("ssm_conv_w", [L, 4, 3072])
    ssm_b = din("ssm_conv_b", [L, 3072])
    dt_bias = din("ssm_dt_bias", [L, 32])
    a_log = din("ssm_a_log", [L, 32])
    d_skip = din("ssm_d", [L, 32])
    ssm_norm = din("ssm_norm", [L, 2048])
    cn = {}
    for name, shp in (("ident", [128, 128]), ("tri", [128, 128]), ("ones", [128, 128]), ("maskneg", [128, 128]),
                      ("mret", [128, 8, 128]), ("qdec", [128, 8]), ("kdec", [128, 8]), ("cdec", [128, 8]),
                      ("cs_p", [seqlen, 256]), ("sc_p", [seqlen, 256]), ("cs_s", [128, 256]), ("sc_s", [128, 256]),
                      ("gam_p", [128, 1]), ("selb", [128, 128]), ("sel4", [4, 64, 128])):
        cn[name] = din("c_" + name, shp)

    yp = dout("yp", [seqlen, D_MODEL])
    ys = dout("ys", [SB, D_MODEL])
    ret_p = dout("ret_p", [L, 8, 128, 128])
    sc_p = dout("sc_p", [L, 2, 1024])
    sconv_p = dout("sconv_p", [L, 3, 3072])
    ssm_p = dout("ssm_p", [L, 32, 64, 128])
    ret_s = dout("ret_s", [L, 128, 128, 128])
    sc_s = dout("sc_s", [L, SB, 2, 1024])
    sconv_s = dout("sconv_s", [L, SB, 3, 3072])
    ssm_s = dout("ssm_s", [L, 128, 4, 64, 128])

    xres = nc.dram_tensor("xres", [seqlen, D_MODEL], F32).ap()
    xsres = nc.dram_tensor("xsres", [SB, D_MODEL], F32).ap()
    pj = nc.dram_tensor("pj", [SB, D_PROJ + 96], F32).ap()

    T_xres = [Track() for _ in range(nseg * NT)]
    T_xsres = Track()
    T_pj = [Track() for _ in range(106)]
    T_out = Track()

    with ExitStack() as st:
        S = Sched(nc)
        NW = 53200
        art = st.enter_context(nc.sbuf_tensor("arena", [128, NW], F32))
        A = Arena(art, NW)
        PB = [st.enter_context(nc.psum_tensor("pb%d" % i, [128, 512], F32)) for i in range(8)]
        PT = [Track() for _ in range(8)]

        def pbf(i):
            return PB[i][:, :].bitcast(BF16)

        def TT(eng, out, in0, in1, op, R, W):
            S.op(eng, lambda e: e.tensor_tensor(out=out, in0=in0, in1=in1, op=op), R, W)

        def STT(eng, out, in0, scalar, in1, op0, op1, R, W):
            S.op(eng, lambda e: e.scalar_tensor_tensor(out=out, in0=in0, scalar=scalar, in1=in1, op0=op0, op1=op1), R, W)

        def TS(eng, out, in0, s1, s2, op0, op1, R, W):
            if s2 is None:
                S.op(eng, lambda e: e.tensor_scalar(out=out, in0=in0, scalar1=s1, scalar2=None, op0=op0), R, W)
            else:
                S.op(eng, lambda e: e.tensor_scalar(out=out, in0=in0, scalar1=s1, scalar2=s2, op0=op0, op1=op1), R, W)

        def ACT(out, in_, func, R, W, bias=None, scale=None, accum=None):
            kw = {}
            if bias is not None:
                kw["bias"] = bias
            if scale is not None:
                kw["scale"] = scale
            if accum is not None:
                kw["accum_out"] = accum
            S.op("act", lambda e: e.activation(out=out, in_=in_, func=func, **kw), R, W)

        def CP(eng, out, in_, R, W):
            if eng == "act":
                S.op("act", lambda e: e.activation(out=out, in_=in_, func=AF.Copy), R, W)
            else:
                S.op(eng, lambda e: e.tensor_copy(out=out, in_=in_), R, W)

        def MSET(eng, out, val, W):
            S.op(eng, lambda e: e.memset(out, val), (), W)

        def MM(bank, mms, R, W=()):
            def fn(e):
                ins = None
                for (o, l, r, s0, s1) in mms:
                    ins = e.matmul(o, lhsT=l, rhs=r, start=s0, stop=s1)
                return ins
            S.op("pe", fn, R, [PT[bank]] + list(W))

        def MMACC(bank, out, pairs, R):
            n = len(pairs)
            MM(bank, [(out, l, r, k == 0, k == n - 1) for k, (l, r) in enumerate(pairs)], R)

        def TRS(bank, items, R):
            def fn(e):
                ins = None
                for (o, i_, idn) in items:
                    ins = e.transpose(out=o, in_=i_, identity=idn)
                return ins
            S.op("pe", fn, R, [PT[bank]])

        def RSTD(out, in_, scale, R, W, tmp):
            TS("dve", tmp.ap, in_, scale, EPS, ALU.mult, ALU.add, R, [tmp])
            TT("pool", out, tmp.ap, mhalf.ap[0:tmp.ap.shape[0], 0:1], ALU.pow, [tmp, mhalf], W)

        hT = A.tile(KC * SEGT // 2, BF16, "p (c t) -> p c t", c=KC)
        mixT = A.tile(32 * SEGT // 2, BF16, "p (c t) -> p c t", c=32)
        WB = [A.tile(4096, BF16), A.tile(4096, BF16)]
        identb = A.tile(64, BF16)
        identf = A.tile(128)
        tri = A.tile(128)
        ones = A.tile(128)
        maskneg = A.tile(64, BF16)
        qdec = A.tile(8)
        kdec = A.tile(8)
        mhalf = A.tile(8)
        cdec = A.tile(8)
        gpreT = A.tile(16)
        gretT = A.tile(8)
        gssmT = A.tile(16)
        scwT = A.tile(24, F32, "p (c j) -> p c j", j=3)
        scbT = A.tile(8)
        ssmwT = A.tile(96, F32, "p (c j) -> p c j", j=4)
        ssmbT = A.tile(24)
        dtb_bc = A.tile(32)
        A_bc = A.tile(32)
        D_bc = A.tile(32)
        S_ret = A.tile(1024, F32, "p (h e) -> p h e", h=8)
        S_bf = A.tile(512, BF16, "p (h e) -> p h e", h=8)
        sT = A.tile(2048)
        sT_bf = A.tile(1024, BF16)
        Ucar = A.tile(16, F32, "p (c j) -> p c j", j=2)
        XBcar = A.tile(72, F32, "p (c j) -> p c j", j=3)
        hTs = A.tile(KC * SB // 2, BF16, "p (c t) -> p c t", c=KC)
        mixTs = A.tile(32 * SB // 2, BF16, "p (c t) -> p c t", c=32)
        gam_p = A.tile(1)
        selb = A.tile(128)
        sel4 = A.tile(512, F32, "p (s c) -> p s c", s=4, parts=64)
        wpar = [0]

        def wb_next():
            w = WB[wpar[0]]
            wpar[0] ^= 1
            return w

        S.dma("pool", identb.ap, cn["ident"], writes=[identb])
        S.dma("sp", identf.ap, cn["ident"], writes=[identf])
        S.dma("sp", tri.ap, cn["tri"], writes=[tri])
        S.dma("sp", ones.ap, cn["ones"], writes=[ones])
        S.dma("pool", maskneg.ap, cn["maskneg"], writes=[maskneg])
        S.dma("sp", qdec.ap[:, 0:8], cn["qdec"], writes=[qdec])
        S.dma("sp", kdec.ap[:, 0:8], cn["kdec"], writes=[kdec])
        S.dma("sp", cdec.ap[:, 0:8], cn["cdec"], writes=[cdec])
        S.dma("sp", gam_p.ap[:, 0:1], cn["gam_p"], writes=[gam_p])
        S.dma("sp", selb.ap, cn["selb"], writes=[selb])
        S.dma("sp", sel4.ap, cn["sel4"].rearrange("s p c -> p s c"), writes=[sel4])
        MSET("dve", mhalf.ap, -0.5, [mhalf])

        def load_layer_params(l):
            S.dma("sp", gpreT.ap[:, 0:16], norm_pre[l].rearrange("(c p) -> p c", p=128), writes=[gpreT], allow_slow_non_contiguous=True)
            S.dma("sp", gretT.ap[:, 0:8], ret_norm[l].rearrange("(c p) -> p c", p=128), writes=[gretT], allow_slow_non_contiguous=True)
            S.dma("sp", gssmT.ap[:, 0:16], ssm_norm[l].rearrange("(c p) -> p c", p=128), writes=[gssmT], allow_slow_non_contiguous=True)
            for j in range(3):
                S.dma("sp", scwT.ap[:, :, j], sc_w[l][j].rearrange("(c p) -> p c", p=128), writes=[scwT], allow_slow_non_contiguous=True)
            S.dma("sp", scbT.ap[:, 0:8], sc_b[l].rearrange("(c p) -> p c", p=128), writes=[scbT], allow_slow_non_contiguous=True)
            for j in range(4):
                S.dma("sp", ssmwT.ap[:, :, j], ssm_w[l][j].rearrange("(c p) -> p c", p=128), writes=[ssmwT], allow_slow_non_contiguous=True)
            S.dma("sp", ssmbT.ap[:, 0:24], ssm_b[l].rearrange("(c p) -> p c", p=128), writes=[ssmbT], allow_slow_non_contiguous=True)
            S.dma("sp", dtb_bc.ap[:, 0:32], dt_bias[l:l + 1, :].partition_broadcast(128), writes=[dtb_bc])
            S.dma("sp", A_bc.ap[:, 0:32], a_log[l:l + 1, :].partition_broadcast(128), writes=[A_bc])
            S.dma("sp", D_bc.ap[:, 0:32], d_skip[l:l + 1, :].partition_broadcast(128), writes=[D_bc])
            ACT(A_bc.ap[:, 0:32], A_bc.ap[:, 0:32], AF.Exp, [A_bc], [A_bc])
            TS("dve", A_bc.ap[:, 0:32], A_bc.ap[:, 0:32], -1.0, None, ALU.mult, None, [A_bc], [A_bc])
            MSET("dve", S_ret.ap, 0.0, [S_ret])
            MSET("dve", S_bf.ap, 0.0, [S_bf])
            MSET("dve", sT.ap, 0.0, [sT])
            MSET("dve", sT_bf.ap, 0.0, [sT_bf])
            MSET("dve", Ucar.ap, 0.0, [Ucar])
            MSET("dve", XBcar.ap, 0.0, [XBcar])

        def build_hT(l, seg):
            m = A.mark()
            xt = [A.tile(2048), A.tile(2048)]
            xn = [A.tile(1024, BF16), A.tile(1024, BF16)]
            ssq = [A.tile(1), A.tile(1)]
            rs = [A.tile(1), A.tile(1)]
            tmp = [A.tile(1), A.tile(1)]
            for t in range(NT):
                p = t % 2
                r0 = seg * SEGT + t * 128
                if l == 0:
                    S.dma("sp", xt[p].ap, xp[r0:r0 + 128, :], writes=[xt[p]])
                else:
                    S.dma("sp", xt[p].ap, xres[r0:r0 + 128, :], reads=[T_xres[seg * NT + t]], writes=[xt[p]])
                ACT(xn[p].ap, xt[p].ap, AF.Square, [xt[p]], [xn[p], ssq[p]], accum=ssq[p].ap[:, 0:1])
                RSTD(rs[p].ap[:, 0:1], ssq[p].ap[:, 0:1], 1.0 / D_MODEL, [ssq[p]], [rs[p]], tmp[p])
                ACT(xn[p].ap, xt[p].ap, AF.Copy, [xt[p], rs[p]], [xn[p]], scale=rs[p].ap[:, 0:1])
                for q in range(4):
                    bank = 6 + (q % 2)
                    TRS(bank, [(pbf(bank)[:, k * 128:(k + 1) * 128], xn[p].ap[:, (q * 4 + k) * 128:(q * 4 + k + 1) * 128], identb.ap)
                               for k in range(4)], [xn[p], identb])
                    for k in range(4):
                        c = q * 4 + k
                        dst = hT.sub(hT.ap[:, c, t * 128:(t + 1) * 128], (c * SEGT + t * 128) // 2, 64)
                        if k % 2 == 0:
                            ACT(dst.ap, pbf(bank)[:, k * 128:(k + 1) * 128], AF.Copy, [gpreT], [PT[bank], dst], scale=gpreT.ap[:, c:c + 1])
                        else:
                            TS("dve", dst.ap, pbf(bank)[:, k * 128:(k + 1) * 128], gpreT.ap[:, c:c + 1], None, ALU.mult, None,
                               [gpreT], [PT[bank], dst])
            A.release(m)

        def build_hTs(l):
            m = A.mark()
            xt = A.tile(2048, parts=SB)
            xn = A.tile(1024, BF16, parts=SB)
            ssq = A.tile(1, parts=SB)
            rs = A.tile(1, parts=SB)
            tmp = A.tile(1, parts=SB)
            if l == 0:
                S.dma("sp", xt.ap, xs, writes=[xt])
            else:
                S.dma("sp", xt.ap, xsres, reads=[T_xsres], writes=[xt])
            ACT(xn.ap, xt.ap, AF.Square, [xt], [xn, ssq], accum=ssq.ap[:, 0:1])
            RSTD(rs.ap[:, 0:1], ssq.ap[:, 0:1], 1.0 / D_MODEL, [ssq], [rs], tmp)
            ACT(xn.ap, xt.ap, AF.Copy, [xt, rs], [xn], scale=rs.ap[:, 0:1])
            for q in range(4):
                bank = 6 + (q % 2)
                TRS(bank, [(pbf(bank)[:, k * SB:(k + 1) * SB], xn.ap[:, (q * 4 + k) * 128:(q * 4 + k + 1) * 128], identb.ap[0:SB, 0:SB])
                           for k in range(4)], [xn, identb])
                for k in range(4):
                    c = q * 4 + k
                    ACT(hTs.ap[:, c, :], pbf(bank)[:, k * SB:(k + 1) * SB], AF.Copy, [gpreT], [PT[bank], hTs], scale=gpreT.ap[:, c:c + 1])
            A.release(m)

        def load_win_block(l, wb, cols):
            v = wb.ap.rearrange("p (c n) -> p c n", c=KC)
            src = w_in[l].rearrange("(c p) n -> p c n", p=128)
            o = 0
            for (c0, ncol) in cols:
                S.dma("pool", v[:, :, o:o + ncol], src[:, :, c0:c0 + ncol], writes=[wb])
                o += ncol
            return v

        def sample_proj(wb, v, cols, stage):
            ntot = sum(n for _, n in cols)
            MMACC(5, PB[5][0:SB, 0:ntot], [(hTs.ap[:, c, :], v[:, c, 0:ntot]) for c in range(KC)], [hTs, wb])
            CP("act", stage.ap[:, 0:ntot], PB[5][0:SB, 0:ntot], [], [PT[5], stage])
            o = 0
            for (c0, ncol) in cols:
                S.dma("sp", pj[:, c0:c0 + ncol], stage.ap[:, o:o + ncol], reads=[stage],
                      writes=T_pj[c0 // 128:(c0 + ncol + 127) // 128])
                o += ncol

        def ret_phase(l, seg, do_sample):
            m = A.mark()
            CS = A.tile(NT * 256, F32, "p (t x) -> p t x", t=NT)
            SC = A.tile(NT * 256, F32, "p (t x) -> p t x", t=NT)
            mret = A.tile(1024, F32, "p (h i) -> p h i", h=8)
            gret_bc = A.tile(1024)
            r0 = seg * SEGT
            S.dma("sp", CS.ap, cn["cs_p"][r0:r0 + SEGT, :].rearrange("(t p) x -> p t x", p=128), writes=[CS])
            S.dma("sp", SC.ap, cn["sc_p"][r0:r0 + SEGT, :].rearrange("(t p) x -> p t x", p=128), writes=[SC])
            S.dma("sp", mret.ap, cn["mret"], writes=[mret])
            S.dma("sp", gret_bc.ap, ret_norm[l:l + 1, :].partition_broadcast(128), writes=[gret_bc])
            stage = A.tile(512, parts=SB) if do_sample else None
            H = []
            for h in range(8):
                H.append(dict(
                    qkT=A.tile(2 * SEGT // 2, BF16, "p (a t) -> p a t", a=2),
                    qkr=A.tile(NT * 256 // 2, BF16, "p (t a d) -> p t a d", t=NT, a=2),
                    vbf=A.tile(NT * 128 // 2, BF16, "p (t e) -> p t e", t=NT),
                    vdec=A.tile(NT * 128 // 2, BF16, "p (t e) -> p t e", t=NT)))
            GS = [A.tile(NT * 512 // 2, BF16, "p (t i e) -> p t i e", t=NT, i=4) for _ in range(2)]
            ABCD_t = [A.tile(512) for _ in range(2)]
            ABCD = [dict(AB=t_.sub(t_.ap[:, 0:256], 0, 256), CD=t_.sub(t_.ap[:, 256:512], 256, 256)) for t_ in ABCD_t]
            for h in range(8):
                Hh = H[h]
                wb = wb_next()
                cols = [(C_Q + h * 128, 128), (C_K + h * 128, 128), (C_V + h * 128, 128), (C_GR + h * 128, 128)]
                v = load_win_block(l, wb, cols)
                for t in range(NT):
                    bank = t % 4
                    P = PB[bank]
                    B = ABCD[t % 2]
                    MMACC(bank, P[:, 0:512], [(hT.ap[:, c, t * 128:(t + 1) * 128], v[:, c, :]) for c in range(KC)], [hT, wb])
                    P4 = P[:, 0:256].rearrange("p (a b f) -> p a b f", a=2, b=2)
                    AB4 = B["AB"].ap.rearrange("p (a b f) -> p a b f", a=2, b=2)
                    CD4 = B["CD"].ap.rearrange("p (a b f) -> p a b f", a=2, b=2)
                    TT("dve", AB4, P4, CS.ap[:, t, :].rearrange("p (a b f) -> p a b f", a=2, b=2), ALU.mult, [CS], [PT[bank], B["AB"]])
                    TT("dve", CD4, P4, SC.ap[:, t, :].rearrange("p (a b f) -> p a b f", a=2, b=2), ALU.mult, [SC], [PT[bank], B["CD"]])
                    TT("dve", Hh["qkr"].ap[:, t, :, 0:64], AB4[:, :, 0, :], AB4[:, :, 1, :], ALU.subtract, [B["AB"]], [Hh["qkr"]])
                    TT("dve", Hh["qkr"].ap[:, t, :, 64:128], CD4[:, :, 0, :], CD4[:, :, 1, :], ALU.add, [B["CD"]], [Hh["qkr"]])
                    ACT(Hh["vbf"].ap[:, t, :], P[:, 256:384], AF.Copy, [], [PT[bank], Hh["vbf"]])
                    ACT(Hh["vdec"].ap[:, t, :], P[:, 256:384], AF.Copy, [kdec], [PT[bank], Hh["vdec"]], scale=kdec.ap[:, h:h + 1])
                    ACT(GS[h // 4].ap[:, t, h % 4, :], P[:, 384:512], AF.Silu, [], [PT[bank], GS[h // 4]])
                    tb = 6 + (t % 2)
                    TRS(tb, [(pbf(tb)[:, a * 128:(a + 1) * 128], Hh["qkr"].ap[:, t, a, :], identb.ap) for a in range(2)], [Hh["qkr"], identb])
                    CP("act", Hh["qkT"].ap[:, :, t * 128:(t + 1) * 128], pbf(tb)[:, 0:256].rearrange("p (a i) -> p a i", a=2), [], [PT[tb], Hh["qkT"]])
                if do_sample:
                    sample_proj(wb, v, cols, stage)
            Pm = [A.tile(256, BF16, "p (i j) -> p i j", i=4) for _ in range(2)]
            osb = [A.tile(512, F32, "p (i e) -> p i e", i=4) for _ in range(2)]
            sq = [Tile(t_.ap.rearrange("p (i e) -> p i e", i=4), t_.tracks, t_.arena, t_.off) for t_ in ABCD_t]
            og = [A.tile(256, BF16, "p (i e) -> p i e", i=4) for _ in range(2)]
            st4 = []
            for _ in range(2):
                t_ = A.tile(24)
                st4.append({nm: t_.sub(t_.ap[:, 4 * k_:4 * k_ + 4], 0, 24) for k_, nm in enumerate(("s1", "s2", "mean", "msq", "var", "rs"))})
            b4 = lambda ap: ap.rearrange("p (i e) -> p i e", i=4)

            def finalize(c, qd):
                sl = slice(c * 128, (c + 1) * 128)
                h0 = qd * 4
                bsc = 4 if qd == 0 else 7
                TRS(bsc, [(pbf(bsc)[:, i * 128:(i + 1) * 128], og[qd].ap[:, i, :], identb.ap) for i in range(4)], [og[qd], identb])
                dsts = [mixT.sub(None, ((h0 + i) * SEGT + c * 128) // 2, 64) for i in range(4)]
                CP("act", mixT.ap[:, h0:h0 + 4, sl], pbf(bsc)[:, 0:512].rearrange("p (i e) -> p i e", i=4), [], [PT[bsc]] + dsts)

            def unit(c, qd, prev):
                    if prev is not None:
                        finalize(*prev)
                    sl = slice(c * 128, (c + 1) * 128)
                    h0 = qd * 4
                    HQ = H[h0:h0 + 4]
                    bsc, bst, bo = (4 if qd == 0 else 7), 5, 6
                    MM(bsc, [(PB[bsc][:, i * 128:(i + 1) * 128], HQ[i]["qkT"].ap[:, 1, sl], HQ[i]["qkT"].ap[:, 0, sl], True, True) for i in range(4)],
                       [x["qkT"] for x in HQ])
                    TT("dve", Pm[qd].ap, b4(PB[bsc][:, 0:512]), mret.ap[:, h0:h0 + 4, :], ALU.mult, [mret], [PT[bsc], Pm[qd]])
                    MM(bst, [(PB[bst][:, i * 128:(i + 1) * 128], HQ[i]["qkr"].ap[:, c, 1, :], HQ[i]["vdec"].ap[:, c, :], True, True) for i in range(4)],
                       [x["qkr"] for x in HQ] + [x["vdec"] for x in HQ])
                    mms = []
                    for i in range(4):
                        o = PB[bo][:, i * 128:(i + 1) * 128]
                        mms.append((o, Pm[qd].ap[:, i, :], HQ[i]["vbf"].ap[:, c, :], True, False))
                        mms.append((o, HQ[i]["qkT"].ap[:, 0, sl], S_bf.ap[:, h0 + i, :], False, True))
                    MM(bo, mms, [Pm[qd], S_bf] + [x["vbf"] for x in HQ] + [x["qkT"] for x in HQ])
                    TT("dve", S_ret.ap[:, h0:h0 + 4, :], S_ret.ap[:, h0:h0 + 4, :], cdec.ap[:, h0:h0 + 4].unsqueeze(2).to_broadcast([128, 4, 128]), ALU.mult, [cdec], [S_ret])
                    TT("dve", S_ret.ap[:, h0:h0 + 4, :], S_ret.ap[:, h0:h0 + 4, :], b4(PB[bst][:, 0:512]), ALU.add, [], [PT[bst], S_ret])
                    CP("act", S_bf.ap[:, h0:h0 + 4, :], S_ret.ap[:, h0:h0 + 4, :], [S_ret], [S_bf])
                    s = st4[qd]
                    TT("dve", osb[qd].ap, b4(PB[bo][:, 0:512]), qdec.ap[:, h0:h0 + 4].unsqueeze(2).to_broadcast([128, 4, 128]), ALU.mult, [qdec], [PT[bo], osb[qd]])
                    S.op("dve", lambda e, o=s["s1"].ap, i_=osb[qd].ap: e.tensor_reduce(out=o, in_=i_, axis=AX.X, op=ALU.add), [osb[qd]], [s["s1"]])
                    ACT(sq[qd].ap, osb[qd].ap, AF.Square, [osb[qd]], [sq[qd]])
                    S.op("dve", lambda e, o=s["s2"].ap, i_=sq[qd].ap: e.tensor_reduce(out=o, in_=i_, axis=AX.X, op=ALU.add), [sq[qd]], [s["s2"]])
                    TS("dve", s["mean"].ap, s["s1"].ap, 1.0 / 128, None, ALU.mult, None, [s["s1"]], [s["mean"]])
                    TT("dve", s["msq"].ap, s["mean"].ap, s["mean"].ap, ALU.mult, [s["mean"]], [s["msq"]])
                    STT("dve", s["var"].ap, s["s2"].ap, 1.0 / 128, s["msq"].ap, ALU.mult, ALU.subtract, [s["s2"], s["msq"]], [s["var"]])
                    TS("dve", s["var"].ap, s["var"].ap, EPS, None, ALU.add, None, [], [s["var"]])
                    TT("pool", s["rs"].ap, s["var"].ap, mhalf.ap[:, 0:4], ALU.pow, [s["var"], mhalf], [s["rs"]])
                    TT("dve", osb[qd].ap, osb[qd].ap, s["mean"].ap.unsqueeze(2).to_broadcast([128, 4, 128]), ALU.subtract, [s["mean"]], [osb[qd]])
                    TT("dve", osb[qd].ap, osb[qd].ap, s["rs"].ap.unsqueeze(2).to_broadcast([128, 4, 128]), ALU.mult, [s["rs"]], [osb[qd]])
                    TT("dve", osb[qd].ap, osb[qd].ap, b4(gret_bc.ap[:, h0 * 128:(h0 + 4) * 128]), ALU.mult, [gret_bc], [osb[qd]])
                    TT("dve", og[qd].ap, osb[qd].ap, GS[qd].ap[:, c, :, :], ALU.mult, [osb[qd], GS[qd]], [og[qd]])

            items = [(c, qd) for c in range(NT) for qd in range(2)]
            units = []
            for k, (c, qd) in enumerate(items):
                prev = items[k - 1] if k > 0 else None
                units.append(lambda c=c, qd=qd, prev=prev: unit(c, qd, prev))
            units.append(lambda: finalize(*items[-1]))
            return m, units, stage

        def sc_phase(l, seg, do_sample, units, stage):
            m = A.mark()
            bufs = []
            for par in range(1):
                bufs.append(dict(cg=A.tile(512), U=A.tile(SEGT + 2), acc=A.tile(512), sg=A.tile(256, BF16), bgc=A.tile(256, BF16)))
            for cc in range(8):
                B = bufs[0]
                wb = wb_next()
                cols = [(C_BG + cc * 128, 128), (C_CG + cc * 128, 128), (C_H + cc * 128, 128), (C_GSC + cc * 128, 128)]
                v = load_win_block(l, wb, cols)
                bk = [j for j in range(4)]
                for j in range(4):
                    MMACC(bk[j], PB[bk[j]][:, 0:SEGT], [(v[:, c, j * 128:(j + 1) * 128], hT.ap[:, c, :]) for c in range(KC)], [hT, wb])
                CP("act", B["cg"].ap, PB[bk[1]][:, 0:SEGT], [], [PT[bk[1]], B["cg"]])
                ACT(B["sg"].ap, PB[bk[3]][:, 0:SEGT], AF.Silu, [], [PT[bk[3]], B["sg"]])
                CP("act", B["bgc"].ap, PB[bk[0]][:, 0:SEGT], [], [PT[bk[0]], B["bgc"]])
                CP("dve", B["U"].ap[:, 0:2], Ucar.ap[:, cc, :], [Ucar], [B["U"]])
                TT("dve", B["U"].ap[:, 2:2 + SEGT], B["cg"].ap, PB[bk[2]][:, 0:SEGT], ALU.mult, [B["cg"]], [PT[bk[2]], B["U"]])
                if cc < len(units):
                    units[cc]()
                CP("dve", Ucar.ap[:, cc, :], B["U"].ap[:, SEGT:SEGT + 2], [B["U"]], [Ucar])
                TS("dve", B["acc"].ap, B["U"].ap[:, 0:SEGT], scwT.ap[:, cc, 0:1], scbT.ap[:, cc:cc + 1], ALU.mult, ALU.add, [B["U"], scwT, scbT], [B["acc"]])
                STT("dve", B["acc"].ap, B["U"].ap[:, 1:1 + SEGT], scwT.ap[:, cc, 1:2], B["acc"].ap, ALU.mult, ALU.add, [B["U"], scwT], [B["acc"]])
                STT("dve", B["acc"].ap, B["U"].ap[:, 2:2 + SEGT], scwT.ap[:, cc, 2:3], B["acc"].ap, ALU.mult, ALU.add, [B["U"], scwT], [B["acc"]])
                TT("dve", B["acc"].ap, B["acc"].ap, B["sg"].ap, ALU.mult, [B["sg"]], [B["acc"]])
                dst = mixT.sub(mixT.ap[:, 8 + cc, :], ((8 + cc) * SEGT) // 2, SEGT // 2)
                TT("dve", dst.ap, B["acc"].ap, B["bgc"].ap, ALU.mult, [B["acc"], B["bgc"]], [dst])
                if do_sample:
                    sample_proj(wb, v, cols, stage)
            for u in units[8:]:
                u()
            A.release(m)

        def ssd_phase(l, seg, do_sample):
            m = A.mark()
            stage = A.tile(512, parts=SB) if do_sample else None
            XS = A.tile(NT * 2048 // 2, BF16, "p (t x) -> p t x", t=NT)
            SZ = A.tile(NT * 2048 // 2, BF16, "p (t x) -> p t x", t=NT)
            BMT = A.tile(4 * SEGT // 2, BF16, "p (g t) -> p g t", g=4)
            CMT = A.tile(4 * SEGT // 2, BF16, "p (g t) -> p g t", g=4)
            BM = A.tile(NT * 512 // 2, BF16, "p (t g n) -> p t g n", t=NT, g=4)
            DT = A.tile(NT * 32, F32, "p (t h) -> p t h", t=NT)
            AA = A.tile(NT * 32, F32, "p (t h) -> p t h", t=NT)
            NACUM = A.tile(NT * 32, F32, "p (t h) -> p t h", t=NT)
            NAA = A.tile(NT * 32, F32, "p (t h) -> p t h", t=NT)
            EA = A.tile(NT * 32, F32, "p (t h) -> p t h", t=NT)
            WJ = A.tile(NT * 32, F32, "p (t h) -> p t h", t=NT)
            ELAST = A.tile(NT * 32, F32, "p (t h) -> p t h", t=NT)
            wdt = A.tile(KC * 32 // 2, BF16, "p (c n) -> p c n", c=KC)
            t32 = [A.tile(32), A.tile(32), A.tile(32)]
            S.dma("pool", wdt.ap, w_in[l].rearrange("(c p) n -> p c n", p=128)[:, :, C_DT:C_DT + 32], writes=[wdt])
            if do_sample:
                MMACC(5, PB[5][0:SB, 0:32], [(hTs.ap[:, c, :], wdt.ap[:, c, :]) for c in range(KC)], [hTs, wdt])
                CP("act", stage.ap[:, 0:32], PB[5][0:SB, 0:32], [], [PT[5], stage])
                S.dma("sp", pj[:, C_DT:C_DT + 32], stage.ap[:, 0:32], reads=[stage], writes=[T_pj[104]])
            for t in range(NT):
                tsl = slice(t * 128, (t + 1) * 128)
                MMACC(0, PB[0][:, 0:32], [(hT.ap[:, c, tsl], wdt.ap[:, c, :]) for c in range(KC)], [hT, wdt])
                xdt, ax, lg = t32
                TT("dve", xdt.ap[:, 0:32], PB[0][:, 0:32], dtb_bc.ap[:, 0:32], ALU.add, [dtb_bc], [PT[0], xdt])
                STT("dve", ax.ap[:, 0:32], xdt.ap[:, 0:32], -1.0, xdt.ap[:, 0:32], ALU.mult, ALU.max, [xdt], [ax])
                ACT(ax.ap[:, 0:32], ax.ap[:, 0:32], AF.Exp, [ax], [ax], scale=-1.0)
                ACT(lg.ap[:, 0:32], ax.ap[:, 0:32], AF.Ln, [ax], [lg], bias=1.0)
                STT("dve", DT.ap[:, t, :], xdt.ap[:, 0:32], 0.0, lg.ap[:, 0:32], ALU.max, ALU.add, [xdt, lg], [DT])
                TT("dve", AA.ap[:, t, :], DT.ap[:, t, :], A_bc.ap[:, 0:32], ALU.mult, [DT, A_bc], [AA])
                TS("dve", NAA.ap[:, t, :], AA.ap[:, t, :], -1.0, None, ALU.mult, None, [AA], [NAA])
                MM(1, [(PB[1][:, 0:32], tri.ap, AA.ap[:, t, :], True, True),
                       (PB[1][:, 32:64], ones.ap, AA.ap[:, t, :], True, True)], [tri, ones, AA])
                TS("dve", NACUM.ap[:, t, :], PB[1][:, 0:32], -1.0, None, ALU.mult, None, [], [PT[1], NACUM])
                ACT(EA.ap[:, t, :], PB[1][:, 0:32], AF.Exp, [], [PT[1], EA])
                ACT(ELAST.ap[:, t, :], PB[1][:, 32:64], AF.Exp, [], [PT[1], ELAST])
                TT("dve", ax.ap[:, 0:32], PB[1][:, 32:64], NACUM.ap[:, t, :], ALU.add, [NACUM], [PT[1], ax])
                ACT(ax.ap[:, 0:32], ax.ap[:, 0:32], AF.Exp, [ax], [ax])
                TT("dve", WJ.ap[:, t, :], ax.ap[:, 0:32], DT.ap[:, t, :], ALU.mult, [ax, DT], [WJ])
            xb = [dict(XB=A.tile(SEGT + 3), acc=A.tile(SEGT), xc=A.tile(SEGT // 2, BF16)) for _ in range(2)]
            for bi in range(6):
                wb = wb_next()
                cols = [(C_XBC + bi * 512, 512)]
                v = load_win_block(l, wb, cols)
                for j in range(4):
                    gc = bi * 4 + j
                    B = xb[gc % 2]
                    bank = gc % 4
                    MMACC(bank, PB[bank][:, 0:SEGT], [(v[:, c, j * 128:(j + 1) * 128], hT.ap[:, c, :]) for c in range(KC)], [hT, wb])
                    CP("dve", B["XB"].ap[:, 0:3], XBcar.ap[:, gc, :], [XBcar], [B["XB"]])
                    CP("act", B["XB"].ap[:, 3:3 + SEGT], PB[bank][:, 0:SEGT], [], [PT[bank], B["XB"]])
                    CP("dve", XBcar.ap[:, gc, :], B["XB"].ap[:, SEGT:SEGT + 3], [B["XB"]], [XBcar])
                    TS("dve", B["acc"].ap, B["XB"].ap[:, 0:SEGT], ssmwT.ap[:, gc, 0:1], ssmbT.ap[:, gc:gc + 1], ALU.mult, ALU.add, [B["XB"], ssmwT, ssmbT], [B["acc"]])
                    for k in range(1, 4):
                        STT("dve", B["acc"].ap, B["XB"].ap[:, k:k + SEGT], ssmwT.ap[:, gc, k:k + 1], B["acc"].ap, ALU.mult, ALU.add, [B["XB"], ssmwT], [B["acc"]])
                    if gc < 16:
                        ACT(B["xc"].ap, B["acc"].ap, AF.Silu, [B["acc"]], [B["xc"]])
                        tb = 6 + (gc % 2)
                        TRS(tb, [(pbf(tb)[:, t * 128:(t + 1) * 128], B["xc"].ap[:, t * 128:(t + 1) * 128], identb.ap) for t in range(NT)], [B["xc"], identb])
                        CP("act" if gc % 2 else "dve", XS.ap[:, :, gc * 128:(gc + 1) * 128], pbf(tb)[:, 0:NT * 128].rearrange("p (t c) -> p t c", t=NT), [], [PT[tb], XS])
                    elif gc < 20:
                        g = gc - 16
                        ACT(BMT.ap[:, g, :], B["acc"].ap, AF.Silu, [B["acc"]], [BMT])
                        tb = 6 + (gc % 2)
                        TRS(tb, [(pbf(tb)[:, t * 128:(t + 1) * 128], BMT.ap[:, g, t * 128:(t + 1) * 128], identb.ap) for t in range(NT)], [BMT, identb])
                        CP("dve", BM.ap[:, :, g, :], pbf(tb)[:, 0:NT * 128].rearrange("p (t c) -> p t c", t=NT), [], [PT[tb], BM])
                    else:
                        g = gc - 20
                        ACT(CMT.ap[:, g, :], B["acc"].ap, AF.Silu, [B["acc"]], [CMT])
                if do_sample:
                    sample_proj(wb, v, cols, stage)
            for zb in range(4):
                wb = wb_next()
                cols = [(C_Z + zb * 512, 512)]
                v = load_win_block(l, wb, cols)
                for t in range(NT):
                    bank = t % 2
                    MMACC(bank, PB[bank][:, 0:512], [(hT.ap[:, c, t * 128:(t + 1) * 128], v[:, c, :]) for c in range(KC)], [hT, wb])
                    ACT(SZ.ap[:, t, zb * 512:(zb + 1) * 512], PB[bank][:, 0:512], AF.Silu, [], [PT[bank], SZ])
                if do_sample:
                    sample_proj(wb, v, cols, stage)
            cb = [A.tile(128), A.tile(128)]
            Lt = [A.tile(512), A.tile(512)]
            MT = [A.tile(256, BF16, "p (i j) -> p i j", i=4), A.tile(256, BF16, "p (i j) -> p i j", i=4)]
            XDT = [A.tile(1024, BF16)] * 2
            R4 = [A.tile(512), A.tile(512)]
            t1 = A.tile(512)
            t2 = A.tile(512)
            t2s = [t2, A.tile(512)]
            ssq4_t = A.tile(4)
            xw = [A.tile(256, BF16), A.tile(256, BF16)]
            YZ = A.tile(2048)
            YN = A.tile(1024, BF16)
            ssq = A.tile(1); rs = A.tile(1); tmp = A.tile(1)
            h8 = lambda ap: ap.rearrange("p (h q) -> p h q", h=8)

            def step1(c, g):
                sl = slice(c * 128, (c + 1) * 128)
                MM(0, [(PB[0][:, 0:128], BMT.ap[:, g, sl], CMT.ap[:, g, sl], True, True)], [BMT, CMT])
                ib = 1 if g % 2 == 0 else 7
                MM(ib, [(PB[ib][:, 0:512], CMT.ap[:, g, sl], sT_bf.ap[:, g * 512:(g + 1) * 512], True, True)], [CMT, sT_bf])
                for r in range(2):
                    bb = 2 + r
                    h0 = g * 8 + r * 4
                    TT("pool", R4[r].ap.rearrange("p (i j) -> p i j", i=4), tri.ap.unsqueeze(1).to_broadcast([128, 4, 128]),
                       AA.ap[:, c, h0:h0 + 4].unsqueeze(2).to_broadcast([128, 4, 128]), ALU.mult, [tri, AA], [R4[r]])
                    o = PB[bb][:, 0:512]
                    if r == 1:
                        gs_ = slice(g * 512, (g + 1) * 512)
                        hs = slice(g * 8, (g + 1) * 8)
                        TT("pool", h8(xw[g % 2].ap), h8(XS.ap[:, c, gs_]), WJ.ap[:, c, hs].unsqueeze(2).to_broadcast([128, 8, 64]), ALU.mult, [XS, WJ], [xw[g % 2]])
                        TT("pool", h8(t2s[g % 2].ap), h8(XS.ap[:, c, gs_]), D_bc.ap[:, hs].unsqueeze(2).to_broadcast([128, 8, 64]), ALU.mult, [XS, D_bc], [t2s[g % 2]])
                    MM(bb, [(o, ones.ap, R4[r].ap, True, False),
                            (o, tri.ap, NAA.ap[:, c, h0:h0 + 4].unsqueeze(2).to_broadcast([128, 4, 128]), False, False),
                            (o, identb.ap, maskneg.ap.unsqueeze(1).to_broadcast([128, 4, 128]), False, True)],
                       [ones, R4[r], tri, NAA, identb, maskneg])

            def step23(c, g):
                cbt = cb[g % 2]
                if g == 0:
                    TT("dve", XDT[0].ap.rearrange("p (h q) -> p h q", h=32), XS.ap[:, c, :].rearrange("p (h q) -> p h q", h=32),
                       DT.ap[:, c, :].unsqueeze(2).to_broadcast([128, 32, 64]), ALU.mult, [XS, DT], [XDT[0]])
                CP("act", cbt.ap, PB[0][:, 0:128], [], [PT[0], cbt])
                for r in range(2):
                    bb = 2 + r
                    ACT(Lt[r].ap, PB[bb][:, 0:512], AF.Exp, [], [PT[bb], Lt[r]])
                    TT("dve", MT[r].ap, Lt[r].ap.rearrange("p (i j) -> p i j", i=4), cbt.ap.unsqueeze(1).to_broadcast([128, 4, 128]), ALU.mult, [Lt[r], cbt], [MT[r]])

            def step4(c, g):
                yb = 4 + (g % 2)
                mms = []
                for r in range(2):
                    for i in range(4):
                        hh = r * 4 + i
                        h = g * 8 + hh
                        mms.append((PB[yb][:, hh * 64:(hh + 1) * 64], MT[r].ap[:, i, :], XDT[c % 2].ap[:, h * 64:(h + 1) * 64], True, True))
                MM(yb, mms, [MT[0], MT[1], XDT[c % 2]])
                MM(6, [(PB[6][:, 0:512], BM.ap[:, c, g, :], xw[g % 2].ap, True, True)], [BM, xw[g % 2]])

            def step56(c, g):
                gs_ = slice(g * 512, (g + 1) * 512)
                hs = slice(g * 8, (g + 1) * 8)
                yb = 4 + (g % 2)
                ib = 1 if g % 2 == 0 else 7
                TT("dve", h8(t1.ap), h8(PB[ib][:, 0:512]), EA.ap[:, c, hs].unsqueeze(2).to_broadcast([128, 8, 64]), ALU.mult, [EA], [PT[ib], t1])
                TT("dve", t1.ap, t1.ap, PB[yb][:, 0:512], ALU.add, [], [PT[yb], t1])
                TT("dve", t1.ap, t1.ap, t2s[g % 2].ap, ALU.add, [t2s[g % 2]], [t1])
                TT("dve", YZ.ap[:, gs_], t1.ap, SZ.ap[:, c, gs_], ALU.mult, [t1, SZ], [YZ])
                TT("dve", h8(sT.ap[:, gs_]), h8(sT.ap[:, gs_]), ELAST.ap[:, c, hs].unsqueeze(2).to_broadcast([128, 8, 64]), ALU.mult, [ELAST], [sT])
                TT("dve", sT.ap[:, gs_], sT.ap[:, gs_], PB[6][:, 0:512], ALU.add, [], [PT[6], sT])
                CP("act", sT_bf.ap[:, gs_], sT.ap[:, gs_], [sT], [sT_bf])

            def chunk_tail(c):
                sl = slice(c * 128, (c + 1) * 128)
                ssq4 = ssq4_t
                for q in range(4):
                    ACT(t1.ap, YZ.ap[:, q * 512:(q + 1) * 512], AF.Square, [YZ], [t1, ssq4], accum=ssq4.ap[:, q:q + 1])
                S.op("dve", lambda e, o=ssq.ap[:, 0:1], i=ssq4.ap[:, 0:4]: e.tensor_reduce(out=o, in_=i, axis=AX.X, op=ALU.add), [ssq4], [ssq])
                RSTD(rs.ap[:, 0:1], ssq.ap[:, 0:1], 1.0 / 2048, [ssq], [rs], tmp)
                ACT(YN.ap, YZ.ap, AF.Copy, [YZ, rs], [YN], scale=rs.ap[:, 0:1])
                for q in range(4):
                    tb = 7 if q % 2 == 0 else 6
                    TRS(tb, [(pbf(tb)[:, k * 128:(k + 1) * 128], YN.ap[:, (q * 4 + k) * 128:(q * 4 + k + 1) * 128], identb.ap) for k in range(4)], [YN, identb])
                    for k in range(4):
                        cc = q * 4 + k
                        dst = mixT.sub(mixT.ap[:, 16 + cc, sl], ((16 + cc) * SEGT + c * 128) // 2, 64)
                        if k % 2 == 0:
                            ACT(dst.ap, pbf(tb)[:, k * 128:(k + 1) * 128], AF.Copy, [gssmT], [PT[tb], dst], scale=gssmT.ap[:, cc:cc + 1])
                        else:
                            TS("dve", dst.ap, pbf(tb)[:, k * 128:(k + 1) * 128], gssmT.ap[:, cc:cc + 1], None, ALU.mult, None, [gssmT], [PT[tb], dst])

            items = [(c, g) for c in range(NT) for g in range(4)]
            step1(*items[0])
            step23(*items[0])
            for k, (c, g) in enumerate(items):
                if k + 1 < len(items):
                    step1(*items[k + 1])
                step4(c, g)
                step56(c, g)
                if g == 3:
                    chunk_tail(c)
                if k + 1 < len(items):
                    step23(*items[k + 1])
            A.release(m)

        def out_phase(l, seg, do_sample, hoist):
            m = A.mark()
            OUT = A.tile(NT * 2048, F32, "p (t x) -> p t x", t=NT)
            gpost = A.tile(2048)
            S.dma("sp", gpost.ap, norm_post[l:l + 1, :].partition_broadcast(128), writes=[gpost])
            outs = A.tile(2048, parts=SB) if do_sample else None
            last = (l == L - 1)
            for ob in range(8):
                wb = wb_next()
                v = wb.ap.rearrange("p (c n) -> p c n", c=32)
                S.dma("pool", v, w_out[l].rearrange("(c p) n -> p c n", p=128)[:, :, ob * 256:(ob + 1) * 256], writes=[wb])
                for t in range(NT):
                    bank = t % 4
                    MMACC(bank, PB[bank][:, 0:256], [(mixT.ap[:, mc, t * 128:(t + 1) * 128], v[:, mc, :]) for mc in range(32)], [mixT, wb])
                    CP("act" if t % 2 else "dve", OUT.ap[:, t, ob * 256:(ob + 1) * 256], PB[bank][:, 0:256], [], [PT[bank], OUT])
                if do_sample:
                    MMACC(5, PB[5][0:SB, 0:256], [(mixTs.ap[:, mc, :], v[:, mc, :]) for mc in range(32)], [mixTs, wb])
                    CP("act", outs.ap[:, ob * 256:(ob + 1) * 256], PB[5][0:SB, 0:256], [], [PT[5], outs])
            xt = [A.tile(2048), A.tile(2048)]
            junk = A.tile(2048)
            ssq = [A.tile(1), A.tile(1)]; rs = [A.tile(1), A.tile(1)]; tmp = [A.tile(1), A.tile(1)]
            if hoist is not None:
                hoist()

            def finish(o_ap, o_tile, x_t, np_, src_ap, src_reads, dst_ap, dst_tracks, sq, r, tm):
                S.dma("sp", x_t.ap, src_ap, reads=src_reads, writes=[x_t])
                ACT(junk.ap[0:np_, :], o_ap, AF.Square, [o_tile], [junk, sq], accum=sq.ap[:, 0:1])
                RSTD(r.ap[:, 0:1], sq.ap[:, 0:1], 1.0 / D_MODEL, [sq], [r], tm)
                STT("dve", o_ap, o_ap, r.ap[:, 0:1], gpost.ap[0:np_, :], ALU.mult, ALU.mult, [r, gpost], [o_tile])
                TT("dve", x_t.ap, x_t.ap, o_ap, ALU.add, [o_tile], [x_t])
                S.dma("sp", dst_ap, x_t.ap, reads=[x_t], writes=dst_tracks)

            for t in range(NT):
                p = t % 2
                r0 = seg * SEGT + t * 128
                tr = T_xres[seg * NT + t]
                if l == 0:
                    src, sr = xp[r0:r0 + 128, :], []
                else:
                    src, sr = xres[r0:r0 + 128, :], [tr]
                if last:
                    dst, dt_ = yp[r0:r0 + 128, :], []
                else:
                    dst, dt_ = xres[r0:r0 + 128, :], [tr]
                finish(OUT.ap[:, t, :], OUT, xt[p], 128, src, sr, dst, dt_, ssq[p], rs[p], tmp[p])
            if do_sample:
                xts = A.tile(2048, parts=SB)
                sq = A.tile(1, parts=SB); r = A.tile(1, parts=SB); tm = A.tile(1, parts=SB)
                src, sr = (xs, []) if l == 0 else (xsres, [T_xsres])
                dst, dt_ = (ys, []) if last else (xsres, [T_xsres])
                finish(outs.ap, outs, xts, SB, src, sr, dst, dt_, sq, r, tm)
            A.release(m)

        def decode_phase(l):
            m = A.mark()
            PJT = T_pj

            def ld(words, src, tr, parts=128, pat=None, **kw):
                t = A.tile(words, F32, pat, parts=parts, **kw)
                S.dma("sp", t.ap, src, reads=tr, writes=[t])
                return t

            def ldj(nj, c, nk, srcj, parts=128):
                t = A.tile(nj * c, F32, "p (j c) -> p j c", parts=parts, j=nj)
                for j in range(nj):
                    S.dma("sp", t.ap[:, j, :], srcj(j), writes=[t])
                return t

            def stj(dstj, t, nj):
                for j in range(nj):
                    S.dma("sp", dstj(j), t.ap[:, j, :], reads=[t])

            def pjv(c0, n, c):
                return pj[:, c0:c0 + n].rearrange("b (k c) -> k b c", c=c)

            rs = A.tile(1); tmp = A.tile(1)
            q = ld(128, pjv(C_Q, 1024, 128), PJT[0:8])
            k = ld(128, pjv(C_K, 1024, 128), PJT[8:16])
            vv = ld(128, pjv(C_V, 1024, 128), PJT[16:24])
            gr = ld(128, pjv(C_GR, 1024, 128), PJT[24:32])
            css = ld(256, cn["cs_s"], [])
            scs = ld(256, cn["sc_s"], [])
            gretP = ld(128, ret_norm[l].rearrange("(h e) -> h e", e=128).unsqueeze(1).to_broadcast([8, SB, 128]), [])
            qk = A.tile(256); AB = A.tile(256); CD = A.tile(256); qkr = A.tile(256, F32, "p (a d) -> p a d", a=2)
            CP("dve", qk.ap[:, 0:128], q.ap, [q], [qk])
            CP("dve", qk.ap[:, 128:256], k.ap, [k], [qk])
            TT("dve", AB.ap, qk.ap, css.ap, ALU.mult, [qk, css], [AB])
            TT("dve", CD.ap, qk.ap, scs.ap, ALU.mult, [qk, scs], [CD])
            AB4 = AB.ap.rearrange("p (a b f) -> p a b f", a=2, b=2)
            CD4 = CD.ap.rearrange("p (a b f) -> p a b f", a=2, b=2)
            TT("dve", qkr.ap[:, :, 0:64], AB4[:, :, 0, :], AB4[:, :, 1, :], ALU.subtract, [AB], [qkr])
            TT("dve", qkr.ap[:, :, 64:128], CD4[:, :, 0, :], CD4[:, :, 1, :], ALU.add, [CD], [qkr])
            qP = qkr.ap[:, 0, :]
            kP = qkr.ap[:, 1, :]
            ACT(gr.ap, gr.ap, AF.Silu, [gr], [gr])
            oacc = A.tile(128); opart = A.tile(128)
            pcs = [A.tile(1024, F32, "p (d e) -> p d e", d=8) for _ in range(4)]
            tmps = [A.tile(1024, F32, "p (d e) -> p d e", d=8) for _ in range(4)]
            tmpsB = [A.tile(1024, F32, "p (d e) -> p d e", d=8) for _ in range(4)]
            for pi in range(16):
                sp_ = pcs[pi % 4]; tp = tmps[pi % 4]; tq = tmpsB[pi % 4]
                d0 = pi * 8
                S.dma("sp", sp_.ap, st_ret[l][:, d0:d0 + 8, :], writes=[sp_])
                TT("pool", tp.ap, sp_.ap, qP[:, d0:d0 + 8].unsqueeze(2).to_broadcast([128, 8, 128]), ALU.mult, [sp_, qkr], [tp])
                dst = oacc if pi == 0 else opart
                S.op("dve", lambda e, o=dst.ap, i=tp.ap.rearrange("p d e -> p e d"): e.tensor_reduce(out=o, in_=i, axis=AX.X, op=ALU.add), [tp], [dst])
                if pi > 0:
                    TT("dve", oacc.ap, oacc.ap, opart.ap, ALU.add, [opart], [oacc])
                TT("pool", tq.ap, kP[:, d0:d0 + 8].unsqueeze(2).to_broadcast([128, 8, 128]), vv.ap.unsqueeze(1).to_broadcast([128, 8, 128]), ALU.mult, [qkr, vv], [tq])
                STT("dve", sp_.ap, sp_.ap, gam_p.ap[:, 0:1], tq.ap, ALU.mult, ALU.add, [gam_p, tq], [sp_])
                S.dma("sp", ret_s[l][:, d0:d0 + 8, :], sp_.ap, reads=[sp_])
            qkd = A.tile(128); qks = A.tile(1)
            TT("dve", qkd.ap, qP, kP, ALU.mult, [qkr], [qkd])
            S.op("dve", lambda e: e.tensor_reduce(out=qks.ap[:, 0:1], in_=qkd.ap, axis=AX.X, op=ALU.add), [qkd], [qks])
            TS("dve", oacc.ap, oacc.ap, gam_p.ap[:, 0:1], None, ALU.mult, None, [gam_p], [oacc])
            STT("dve", oacc.ap, vv.ap, qks.ap[:, 0:1], oacc.ap, ALU.mult, ALU.add, [vv, qks], [oacc])
            stats = A.tile(6); mv = A.tile(2)
            S.op("dve", lambda e: e.bn_stats(out=stats.ap[:, 0:6], in_=oacc.ap), [oacc], [stats])
            S.op("dve", lambda e: e.bn_aggr(out=mv.ap[:, 0:2], in_=stats.ap[:, 0:6]), [stats], [mv])
            RSTD(rs.ap[:, 0:1], mv.ap[:, 1:2], 1.0, [mv], [rs], tmp)
            TS("dve", oacc.ap, oacc.ap, mv.ap[:, 0:1], rs.ap[:, 0:1], ALU.subtract, ALU.mult, [mv, rs], [oacc])
            TT("dve", oacc.ap, oacc.ap, gretP.ap, ALU.mult, [gretP], [oacc])
            ob = A.tile(64, BF16)
            TT("dve", ob.ap, oacc.ap, gr.ap, ALU.mult, [oacc, gr], [ob])
            TRS(7, [(pbf(7)[:, 0:128], ob.ap, identb.ap)], [ob, identb])
            CP("dve", mixTs.ap[:, 0:8, :], pbf(7)[:, 0:128].rearrange("p (k b) -> p k b", k=8), [], [PT[7], mixTs])
            A.release(m)
            m = A.mark()
            bg = ld(128, pjv(C_BG, 1024, 128), PJT[32:40])
            cg = ld(128, pjv(C_CG, 1024, 128), PJT[40:48])
            hh_ = ld(128, pjv(C_H, 1024, 128), PJT[48:56])
            gsc = ld(128, pjv(C_GSC, 1024, 128), PJT[56:64])
            wP = ldj(3, 128, 8, lambda j: sc_w[l][j].rearrange("(k c) -> k c", c=128).unsqueeze(1).to_broadcast([8, SB, 128]))
            bP = ld(128, sc_b[l].rearrange("(k c) -> k c", c=128).unsqueeze(1).to_broadcast([8, SB, 128]), [])
            buf = ldj(2, 128, 8, lambda j: st_sc[l][:, j, :].rearrange("b (k c) -> k b c", c=128))
            nst = A.tile(256, F32, "p (j c) -> p j c", j=2)
            acc = A.tile(128)
            TT("dve", nst.ap[:, 1, :], cg.ap, hh_.ap, ALU.mult, [cg, hh_], [nst])
            CP("dve", nst.ap[:, 0, :], buf.ap[:, 1, :], [buf], [nst])
            TT("dve", acc.ap, buf.ap[:, 0, :], wP.ap[:, 0, :], ALU.mult, [buf, wP], [acc])
            TT("dve", acc.ap, acc.ap, bP.ap, ALU.add, [bP], [acc])
            TT("dve", bP.ap, buf.ap[:, 1, :], wP.ap[:, 1, :], ALU.mult, [buf, wP], [bP])
            TT("dve", acc.ap, acc.ap, bP.ap, ALU.add, [bP], [acc])
            TT("dve", bP.ap, nst.ap[:, 1, :], wP.ap[:, 2, :], ALU.mult, [nst, wP], [bP])
            TT("dve", acc.ap, acc.ap, bP.ap, ALU.add, [bP], [acc])
            ACT(gsc.ap, gsc.ap, AF.Silu, [gsc], [gsc])
            TT("dve", acc.ap, acc.ap, gsc.ap, ALU.mult, [gsc], [acc])
            ob = A.tile(64, BF16)
            TT("dve", ob.ap, acc.ap, bg.ap, ALU.mult, [acc, bg], [ob])
            stj(lambda j: sc_s[l][:, j, :].rearrange("b (k c) -> k b c", c=128), nst, 2)
            TRS(7, [(pbf(7)[:, 0:128], ob.ap, identb.ap)], [ob, identb])
            CP("dve", mixTs.ap[:, 8:16, :], pbf(7)[:, 0:128].rearrange("p (k b) -> p k b", k=8), [], [PT[7], mixTs])
            A.release(m)
            m = A.mark()
            xr = ld(256, pjv(C_XBC, 2048, 256), PJT[80:96])
            zz = ld(256, pjv(C_Z, 2048, 256), PJT[64:80])
            wx = ldj(4, 256, 8, lambda j: ssm_w[l][j, 0:2048].rearrange("(k c) -> k c", c=256).unsqueeze(1).to_broadcast([8, SB, 256]))
            bx = ld(256, ssm_b[l][0:2048].rearrange("(k c) -> k c", c=256).unsqueeze(1).to_broadcast([8, SB, 256]), [])
            bufx = ldj(3, 256, 8, lambda j: st_sconv[l][:, j, 0:2048].rearrange("b (k c) -> k b c", c=256))
            nstx = A.tile(768, F32, "p (j c) -> p j c", j=3)
            xc = A.tile(256); t256 = A.tile(256)
            CP("dve", nstx.ap[:, 0:2, :], bufx.ap[:, 1:3, :], [bufx], [nstx])
            CP("dve", nstx.ap[:, 2, :], xr.ap, [xr], [nstx])
            stj(lambda j: sconv_s[l][:, j, 0:2048].rearrange("b (k c) -> k b c", c=256), nstx, 3)
            TT("dve", xc.ap, xr.ap, wx.ap[:, 3, :], ALU.mult, [xr, wx], [xc])
            TT("dve", xc.ap, xc.ap, bx.ap, ALU.add, [bx], [xc])
            for j in range(3):
                TT("dve", t256.ap, bufx.ap[:, j, :], wx.ap[:, j, :], ALU.mult, [bufx, wx], [t256])
                TT("dve", xc.ap, xc.ap, t256.ap, ALU.add, [t256], [xc])
            ACT(xc.ap, xc.ap, AF.Silu, [xc], [xc])
            ACT(zz.ap, zz.ap, AF.Silu, [zz], [zz])
            br = ld(256, pjv(C_XBC + 2048, 1024, 256), PJT[96:104], parts=64)
            wbc = ldj(4, 256, 4, lambda j: ssm_w[l][j, 2048:3072].rearrange("(k c) -> k c", c=256).unsqueeze(1).to_broadcast([4, SB, 256]), parts=64)
            bbc = ld(256, ssm_b[l][2048:3072].rearrange("(k c) -> k c", c=256).unsqueeze(1).to_broadcast([4, SB, 256]), [], parts=64)
            bufb = ldj(3, 256, 4, lambda j: st_sconv[l][:, j, 2048:3072].rearrange("b (k c) -> k b c", c=256), parts=64)
            nstb = A.tile(768, F32, "p (j c) -> p j c", j=3, parts=64)
            bc = A.tile(256, parts=64); tb256 = A.tile(256, parts=64)
            CP("dve", nstb.ap[:, 0:2, :], bufb.ap[:, 1:3, :], [bufb], [nstb])
            CP("dve", nstb.ap[:, 2, :], br.ap, [br], [nstb])
            stj(lambda j: sconv_s[l][:, j, 2048:3072].rearrange("b (k c) -> k b c", c=256), nstb, 3)
            TT("dve", bc.ap, br.ap, wbc.ap[:, 3, :], ALU.mult, [br, wbc], [bc])
            TT("dve", bc.ap, bc.ap, bbc.ap, ALU.add, [bbc], [bc])
            for j in range(3):
                TT("dve", tb256.ap, bufb.ap[:, j, :], wbc.ap[:, j, :], ALU.mult, [bufb, wbc], [tb256])
                TT("dve", bc.ap, bc.ap, tb256.ap, ALU.add, [tb256], [bc])
            ACT(bc.ap, bc.ap, AF.Silu, [bc], [bc])
            MM(4, [(PB[4][:, 0:128], sel4.ap[:, 0, :], bc.ap[:, 0:128], True, False),
                   (PB[4][:, 0:128], sel4.ap[:, 1, :], bc.ap[:, 128:256], False, True),
                   (PB[4][:, 128:256], sel4.ap[:, 2, :], bc.ap[:, 0:128], True, False),
                   (PB[4][:, 128:256], sel4.ap[:, 3, :], bc.ap[:, 128:256], False, True)], [sel4, bc])
            bcP = A.tile(256)
            CP("dve", bcP.ap, PB[4][:, 0:256], [], [PT[4], bcP])
            bmP = bcP.ap[:, 0:128]
            cmP = bcP.ap[:, 128:256]
            dtr = ld(4, pj[:, C_DT:C_DT + 32].rearrange("b (k c) -> k b c", c=4), [PJT[104]])
            dtbP = ld(4, dt_bias[l].rearrange("(k c) -> k c", c=4).unsqueeze(1).to_broadcast([8, SB, 4]), [])
            AP_ = ld(4, a_log[l].rearrange("(k c) -> k c", c=4).unsqueeze(1).to_broadcast([8, SB, 4]), [])
            DP = ld(4, d_skip[l].rearrange("(k c) -> k c", c=4).unsqueeze(1).to_broadcast([8, SB, 4]), [])
            x4 = A.tile(4); a4 = A.tile(4); l4 = A.tile(4); dtP = A.tile(4); eaP = A.tile(4); coef = A.tile(4); cbP = A.tile(1)
            TT("dve", x4.ap[:, 0:4], dtr.ap[:, 0:4], dtbP.ap[:, 0:4], ALU.add, [dtr, dtbP], [x4])
            STT("dve", a4.ap[:, 0:4], x4.ap[:, 0:4], -1.0, x4.ap[:, 0:4], ALU.mult, ALU.max, [x4], [a4])
            ACT(a4.ap[:, 0:4], a4.ap[:, 0:4], AF.Exp, [a4], [a4], scale=-1.0)
            ACT(l4.ap[:, 0:4], a4.ap[:, 0:4], AF.Ln, [a4], [l4], bias=1.0)
            STT("dve", dtP.ap[:, 0:4], x4.ap[:, 0:4], 0.0, l4.ap[:, 0:4], ALU.max, ALU.add, [x4, l4], [dtP])
            ACT(AP_.ap[:, 0:4], AP_.ap[:, 0:4], AF.Exp, [AP_], [AP_])
            TT("dve", a4.ap[:, 0:4], dtP.ap[:, 0:4], AP_.ap[:, 0:4], ALU.mult, [dtP, AP_], [a4])
            ACT(eaP.ap[:, 0:4], a4.ap[:, 0:4], AF.Exp, [a4], [eaP], scale=-1.0)
            tb128 = A.tile(128)
            TT("dve", tb128.ap, bmP, cmP, ALU.mult, [bcP], [tb128])
            S.op("dve", lambda e: e.tensor_reduce(out=cbP.ap[:, 0:1], in_=tb128.ap, axis=AX.X, op=ALU.add), [tb128], [cbP])
            STT("dve", coef.ap[:, 0:4], dtP.ap[:, 0:4], cbP.ap[:, 0:1], DP.ap[:, 0:4], ALU.mult, ALU.add, [dtP, cbP, DP], [coef])
            xdt = A.tile(256); yS = A.tile(256)
            xc3 = xc.ap.rearrange("p (h q) -> p h q", h=4)
            TT("dve", xdt.ap.rearrange("p (h q) -> p h q", h=4), xc3, dtP.ap[:, 0:4].unsqueeze(2).to_broadcast([128, 4, 64]), ALU.mult, [xc, dtP], [xdt])
            pcs = [A.tile(1024, F32, "p (q n) -> p q n", q=8) for _ in range(4)]
            tmps = [A.tile(1024, F32, "p (q n) -> p q n", q=8) for _ in range(4)]
            tmpsB = [A.tile(1024, F32, "p (q n) -> p q n", q=8) for _ in range(4)]
            for pi in range(32):
                hl, p0 = pi // 8, (pi % 8) * 8
                sp_ = pcs[pi % 4]; tp = tmps[pi % 4]; tq = tmpsB[pi % 4]
                S.dma("sp", sp_.ap, st_ssm[l][:, hl, p0:p0 + 8, :], writes=[sp_])
                TT("pool", tp.ap, sp_.ap, cmP.unsqueeze(1).to_broadcast([128, 8, 128]), ALU.mult, [sp_, bcP], [tp])
                S.op("dve", lambda e, o=yS.ap[:, hl * 64 + p0:hl * 64 + p0 + 8], i=tp.ap: e.tensor_reduce(out=o, in_=i, axis=AX.X, op=ALU.add), [tp], [yS])
                TT("pool", tq.ap, xdt.ap[:, hl * 64 + p0:hl * 64 + p0 + 8].unsqueeze(2).to_broadcast([128, 8, 128]),
                   bmP.unsqueeze(1).to_broadcast([128, 8, 128]), ALU.mult, [xdt, bcP], [tq])
                STT("dve", sp_.ap, sp_.ap, eaP.ap[:, hl:hl + 1], tq.ap, ALU.mult, ALU.add, [eaP, tq], [sp_])
                S.dma("sp", ssm_s[l][:, hl, p0:p0 + 8, :], sp_.ap, reads=[sp_])
            y = A.tile(256)
            TT("dve", y.ap.rearrange("p (h q) -> p h q", h=4), xc3, coef.ap[:, 0:4].unsqueeze(2).to_broadcast([128, 4, 64]), ALU.mult, [xc, coef], [y])
            TT("dve", yS.ap.rearrange("p (h q) -> p h q", h=4), yS.ap.rearrange("p (h q) -> p h q", h=4),
               eaP.ap[:, 0:4].unsqueeze(2).to_broadcast([128, 4, 64]), ALU.mult, [eaP], [yS])
            TT("dve", y.ap, y.ap, yS.ap, ALU.add, [yS], [y])
            TT("dve", y.ap, y.ap, zz.ap, ALU.mult, [zz], [y])
            ssq = A.tile(1)
            ACT(t256.ap, y.ap, AF.Square, [y], [t256, ssq], accum=ssq.ap[:, 0:1])
            MM(4, [(PB[4][:, 0:1], selb.ap, ssq.ap[:, 0:1], True, True)], [selb, ssq])
            tot = A.tile(1)
            CP("dve", tot.ap[:, 0:1], PB[4][:, 0:1], [], [PT[4], tot])
            RSTD(rs.ap[:, 0:1], tot.ap[:, 0:1], 1.0 / 2048, [tot], [rs], tmp)
            gsP = ld(256, ssm_norm[l].rearrange("(k c) -> k c", c=256).unsqueeze(1).to_broadcast([8, SB, 256]), [])
            ob2 = A.tile(128, BF16)
            STT("dve", ob2.ap, y.ap, rs.ap[:, 0:1], gsP.ap, ALU.mult, ALU.mult, [y, rs, gsP], [ob2])
            TRS(7, [(pbf(7)[:, r * 128:(r + 1) * 128], ob2.ap[:, r * 128:(r + 1) * 128], identb.ap) for r in range(2)], [ob2, identb])
            mv_ = mixTs.ap[:, 16:32, :].rearrange("p (k r) b -> p r k b", r=2)
            for r in range(2):
                CP("dve", mv_[:, r, :, :], pbf(7)[:, r * 128:(r + 1) * 128].rearrange("p (k b) -> p k b", k=8), [], [PT[7], mixTs])
            A.release(m)

        def write_prompt_states(l):
            m = A.mark()
            S.dma("sp", ret_p[l].rearrange("h d e -> d h e"), S_ret.ap, reads=[S_ret])
            for j in range(2):
                S.dma("sp", sc_p[l][j].rearrange("(c p) -> p c", p=128), Ucar.ap[:, :, j], reads=[Ucar], allow_slow_non_contiguous=True)
            for j in range(3):
                S.dma("sp", sconv_p[l][j].rearrange("(c p) -> p c", p=128), XBcar.ap[:, :, j], reads=[XBcar], allow_slow_non_contiguous=True)
            so = [A.tile(512), A.tile(512)]
            for q in range(4):
                TRS(q % 2, [(PB[q % 2][:, k * 128:(k + 1) * 128], sT.ap[:, (q * 4 + k) * 128:(q * 4 + k + 1) * 128], identf.ap) for k in range(4)], [sT, identf])
                CP("dve", so[q % 2].ap, PB[q % 2][:, 0:512], [], [PT[q % 2], so[q % 2]])
                S.dma("sp", ssm_p[l][q * 8:(q + 1) * 8].rearrange("(k a) p n -> (a p) k n", a=2), so[q % 2].ap.rearrange("x (k n) -> x k n", k=4), reads=[so[q % 2]])
            A.release(m)

        for l in range(L):
            load_layer_params(l)
            for seg in range(nseg):
                ds = (seg == 0)
                if seg == 0:
                    build_hT(l, seg)
                if ds:
                    build_hTs(l)
                mret_, units, stage_ = ret_phase(l, seg, ds)
                sc_phase(l, seg, ds, units, stage_)
                A.release(mret_)
                ssd_phase(l, seg, ds)
                if ds:
                    decode_phase(l)
                out_phase(l, seg, ds, (lambda l=l, seg=seg: build_hT(l, seg + 1)) if seg + 1 < nseg else None)
            write_prompt_states(l)
        S.emit(st)
        build_program.stats = dict(ops=len(S.ops), waits=S.nwaits, arena_peak=A.peak)
    return nc


_CACHE = {}


def _run(inputs, depth, nseg):
    key = (depth, nseg)
    if key not in _CACHE:
        _CACHE[key] = build_program(depth, nseg)
    nc = _CACHE[key]
    seqlen = nseg * SEGT
    f = lambda a: np.ascontiguousarray(np.asarray(a, dtype=np.float32))
    consts = _consts(seqlen)
    L = depth
    shared = {k: f(inputs[k]) for k in ("w_in", "w_out", "norm_pre", "norm_post", "ret_norm", "sc_conv_w", "sc_conv_b",
                                        "ssm_conv_w", "ssm_conv_b", "ssm_dt_bias", "ssm_a_log", "ssm_d", "ssm_norm")}
    for k, v in consts.items():
        shared["c_" + k] = f(v)
    x_prompt = np.asarray(inputs["x_prompt"], np.float32)
    x_sample = np.asarray(inputs["x_sample"], np.float32)
    s_ret = np.asarray(inputs["state_ret"], np.float32)
    s_sc = np.asarray(inputs["state_sconv"], np.float32)
    s_sconv = np.asarray(inputs["state_ssm_conv"], np.float32)
    s_ssm = np.asarray(inputs["state_ssm"], np.float32)
    in_maps = []
    for c in range(8):
        b0 = c * SB
        d = dict(shared)
        d["xp"] = f(x_prompt[c % BATCH])
        d["xs"] = f(x_sample[b0:b0 + SB, 0, :])
        d["st_ret"] = f(s_ret[:, b0:b0 + SB].transpose(0, 2, 1, 3, 4).reshape(L, 128, 128, 128))
        d["st_sc"] = f(s_sc[:, b0:b0 + SB])
        d["st_sconv"] = f(s_sconv[:, b0:b0 + SB])
        d["st_ssm"] = f(s_ssm[:, b0:b0 + SB].reshape(L, SB, 8, 4, 64, 128).transpose(0, 2, 1, 3, 4, 5).reshape(L, 128, 4, 64, 128))
        in_maps.append(d)
    res = run_bass_kernel_spmd(nc, in_maps, core_ids=list(range(8)))
    R = res.results
    yp = np.stack([R[b]["yp"] for b in range(BATCH)])
    ys = np.concatenate([R[c]["ys"] for c in range(8)])[:, None, :]
    ret_p = np.stack([R[b]["ret_p"] for b in range(BATCH)], axis=1)
    sc_p = np.stack([R[b]["sc_p"] for b in range(BATCH)], axis=1)
    sconv_p = np.stack([R[b]["sconv_p"] for b in range(BATCH)], axis=1)
    ssm_p = np.stack([R[b]["ssm_p"] for b in range(BATCH)], axis=1)
    ret_s = np.concatenate([R[c]["ret_s"].reshape(L, 8, SB, 128, 128).transpose(0, 2, 1, 3, 4) for c in range(8)], axis=1)
    sc_s = np.concatenate([R[c]["sc_s"] for c in range(8)], axis=1)
    sconv_s = np.concatenate([R[c]["sconv_s"] for c in range(8)], axis=1)
    ssm_s = np.concatenate([R[c]["ssm_s"].reshape(L, 8, SB, 4, 64, 128).transpose(0, 2, 1, 3, 4, 5).reshape(L, SB, 32, 64, 128)
                            for c in range(8)], axis=1)
    outs = (yp, ys, ret_p, sc_p, sconv_p, ssm_p, ret_s, sc_s, sconv_s, ssm_s)
    return tuple(np.ascontiguousarray(o, dtype=np.float32) for o in outs)


def kernel(**inputs):
    depth = int(np.asarray(inputs["w_in"]).shape[0])
    nseg = int(np.asarray(inputs["x_prompt"]).shape[1]) // SEGT
    return _run(inputs, depth, nseg)
```

```python
import numpy as np
from contextlib import ExitStack
import concourse.bass as bass
import concourse.mybir as mybir
from concourse.bass_utils import run_bass_kernel_spmd

F32 = mybir.dt.float32
BF16 = mybir.dt.bfloat16
ALU = mybir.AluOpType
AF = mybir.ActivationFunctionType
AX = mybir.AxisListType

D_MODEL = 2048
DEPTH = 4
BATCH = 4
SEQ = 2048
DEC_B = 128
PAST_LEN = 16384
D_PROJ = 13344
EPS = 1e-6
NT = 4
SEGT = NT * 128
KC = 16
SB = 16
C_Q, C_K, C_V, C_GR, C_BG, C_CG, C_H, C_GSC, C_Z, C_XBC, C_DT = 0, 1024, 2048, 3072, 4096, 5120, 6144, 7168, 8192, 10240, 13312
GAM = [1.0 - 2.0 ** (-5.0 - h) for h in range(8)]


class Track:
    __slots__ = ("w", "r")

    def __init__(self):
        self.w = None
        self.r = []


class Tile:
    __slots__ = ("ap", "tracks", "arena", "off")

    def __init__(self, ap, tracks, arena=None, off=0):
        self.ap = ap
        self.tracks = tracks
        self.arena = arena
        self.off = off

    def sub(self, ap, woff, wlen):
        a = self.arena
        o = self.off + woff
        return Tile(ap, a.blocks[o // a.G:(o + wlen + a.G - 1) // a.G], a, o)


def _flat(lst):
    out = []
    for x in lst:
        if isinstance(x, Tile):
            out.extend(x.tracks)
        elif isinstance(x, Track):
            out.append(x)
        elif x is None:
            pass
        else:
            out.extend(_flat(x))
    return out


class Arena:
    G = 64

    def __init__(self, tensor, nwords):
        self.t = tensor
        self.n = nwords
        self.off = 0
        self.peak = 0
        self.blocks = [Track() for _ in range(nwords // self.G + 2)]

    def tile(self, words, dt=F32, pat=None, parts=128, **kw):
        req = words
        words = (words + self.G - 1) // self.G * self.G
        off = self.off
        self.off += words
        self.peak = max(self.peak, self.off)
        assert self.off <= self.n, "SBUF arena overflow %d > %d" % (self.off, self.n)
        ap = self.t[0:parts, off:off + req]
        if dt is BF16:
            ap = ap.bitcast(BF16)
        if pat:
            ap = ap.rearrange(pat, **kw)
        return Tile(ap, self.blocks[off // self.G:(off + words) // self.G], self, off)

    def mark(self):
        return self.off

    def release(self, m):
        self.off = m


class Sched:
    COMPUTE = ("pe", "act", "dve", "pool")

    def __init__(self, nc, nslots_sp=14, nslots_pool=8):
        self.nc = nc
        self.ops = []
        self.nslots = {"sp": nslots_sp, "pool": nslots_pool}
        self.dma_count = {"sp": 0, "pool": 0}

    def op(self, eng, fn, reads=(), writes=()):
        self.ops.append((eng, fn, _flat(reads), _flat(writes), None))

    def dma(self, queue, out, in_, reads=(), writes=(), **kw):
        n = self.dma_count[queue]
        self.dma_count[queue] += 1
        slot = (queue, n % self.nslots[queue])
        self.ops.append((queue, None, _flat(reads), _flat(writes), (out, in_, slot, kw)))

    def emit(self, stack):
        nc = self.nc
        ops = self.ops
        n = len(ops)
        deps_all = [None] * n
        slot_last = {}
        needs = set()
        for i, (eng, fn, R, W, dma) in enumerate(ops):
            deps = set()
            for t in R:
                if t.w is not None:
                    deps.add(t.w)
            for t in W:
                if t.w is not None:
                    deps.add(t.w)
                if t.r:
                    deps.update(t.r)
            if dma:
                s = dma[2]
                if s in slot_last:
                    deps.add(slot_last[s])
                slot_last[s] = i
            deps.discard(i)
            deps_all[i] = deps
            needs |= deps
            for t in R:
                t.r.append(i)
            for t in W:
                t.w = i
                t.r = []
        sems = {}
        for e in self.COMPUTE:
            sems[e] = stack.enter_context(nc.semaphore("s_" + e))
        for q, ns in self.nslots.items():
            for k in range(min(ns, self.dma_count[q])):
                sems[(q, k)] = stack.enter_context(nc.semaphore("d_%s%d" % (q, k)))
        cnt = {k: 0 for k in sems}
        sig = [None] * n
        for i, (eng, fn, R, W, dma) in enumerate(ops):
            if dma:
                k = dma[2]
                cnt[k] += 16
                sig[i] = (k, cnt[k])
            elif i in needs:
                cnt[eng] += 1
                sig[i] = (eng, cnt[eng])
        final = dict(cnt)
        streams = {e: [] for e in ("pe", "act", "dve", "pool", "sp")}
        seen = {e: {} for e in streams}
        nw = 0
        for i, (eng, fn, R, W, dma) in enumerate(ops):
            waits = {}
            se = seen[eng]
            for d in deps_all[i]:
                k, v = sig[d]
                if se.get(k, 0) < v and waits.get(k, 0) < v:
                    waits[k] = v
            for k, v in waits.items():
                se[k] = v
            nw += len(waits)
            streams[eng].append((i, waits))
        self.nwaits = nw
        self.final = final
        block = stack.enter_context(nc.Block())

        def run_stream(e, eng):
            for i, waits in streams[e]:
                for k, v in waits.items():
                    eng.wait_ge(sems[k], v)
                _, fn, _, _, dma = ops[i]
                if dma:
                    ins = eng.dma_start(out=dma[0], in_=dma[1], **dma[3])
                    ins.then_inc(sems[sig[i][0]], 16)
                else:
                    ins = fn(eng)
                    if sig[i] is not None:
                        ins.then_inc(sems[sig[i][0]], 1)
            for k, v in final.items():
                if isinstance(k, tuple) and k[0] == e and v > 0:
                    eng.wait_ge(sems[k], v)

        @block.sync
        def _(eng):
            run_stream("sp", eng)

        @block.tensor
        def _(eng):
            run_stream("pe", eng)

        @block.scalar
        def _(eng):
            run_stream("act", eng)

        @block.vector
        def _(eng):
            run_stream("dve", eng)

        @block.gpsimd
        def _(eng):
            run_stream("pool", eng)


def _rope_tables(pos, kscale):
    half = 64
    inv = (np.float32(10000.0) ** (-np.arange(half, dtype=np.float32) / np.float32(half))).astype(np.float32)
    ang = (pos.astype(np.float32)[:, None] * inv[None, :]).astype(np.float32)
    cos = np.cos(ang.astype(np.float64))
    sin = np.sin(ang.astype(np.float64))
    n = len(pos)
    cs = np.zeros((n, 2, 2, 64), np.float64)
    sc = np.zeros((n, 2, 2, 64), np.float64)
    cs[:, 0, 0], cs[:, 0, 1] = cos, sin
    sc[:, 0, 0], sc[:, 0, 1] = sin, cos
    cs[:, 1, 0], cs[:, 1, 1] = cos * kscale, sin * kscale
    sc[:, 1, 0], sc[:, 1, 1] = sin * kscale, cos * kscale
    return cs.reshape(n, 256).astype(np.float32), sc.reshape(n, 256).astype(np.float32)


def _consts(seqlen):
    c = {}
    c["ident"] = np.eye(128, dtype=np.float32)
    j = np.arange(128)[:, None]
    i = np.arange(128)[None, :]
    c["tri"] = (j <= i).astype(np.float32)
    c["ones"] = np.ones((128, 128), np.float32)
    c["maskneg"] = np.where(i >= j, 0.0, -30000.0).astype(np.float32)
    g = np.array(GAM, np.float64)
    m = np.zeros((128, 8, 128), np.float64)
    for h in range(8):
        m[:, h, :] = np.where(i >= j, g[h] ** (-(j + 1.0)), 0.0)
    c["mret"] = m.astype(np.float32)
    c["qdec"] = (g[None, :] ** (np.arange(128)[:, None] + 1.0)).astype(np.float32)
    c["kdec"] = (g[None, :] ** (127.0 - np.arange(128)[:, None])).astype(np.float32)
    c["cdec"] = np.broadcast_to((g ** 128.0)[None, :], (128, 8)).astype(np.float32).copy()
    ks = 128.0 ** -0.5
    c["cs_p"], c["sc_p"] = _rope_tables(np.arange(seqlen), ks)
    cs_s, sc_s = _rope_tables(np.array([PAST_LEN]), ks)
    c["cs_s"] = np.broadcast_to(cs_s, (128, 256)).copy()
    c["sc_s"] = np.broadcast_to(sc_s, (128, 256)).copy()
    c["gam_p"] = np.repeat(np.array(GAM, np.float32), SB)[:, None].copy()
    p = np.arange(128)
    c["selb"] = (p[:, None] % SB == p[None, :] % SB).astype(np.float32)
    sel = np.zeros((4, 64, 128), np.float32)
    for hh in range(8):
        gidx = hh // 2
        for b in range(SB):
            for which, base in ((0, 0), (1, 2)):
                blk = base + gidx // 2
                sel[which * 2 + (gidx % 2), blk * SB + b, hh * SB + b] = 1.0
    c["sel4"] = sel
    return c


def build_program(depth=DEPTH, nseg=SEQ // SEGT, debug=False):
    L = depth
    seqlen = nseg * SEGT
    nc = bass.Bass("TRN2", target_bir_lowering=False)

    def din(name, shape):
        return nc.dram_tensor(name, list(shape), F32, kind="ExternalInput").ap()

    def dout(name, shape):
        return nc.dram_tensor(name, list(shape), F32, kind="ExternalOutput").ap()

    xp = din("xp", [seqlen, D_MODEL])
    xs = din("xs", [SB, D_MODEL])
    st_ret = din("st_ret", [L, 128, 128, 128])
    st_sc = din("st_sc", [L, SB, 2, 1024])
    st_sconv = din("st_sconv", [L, SB, 3, 3072])
    st_ssm = din("st_ssm", [L, 128, 4, 64, 128])
    w_in = din("w_in", [L, D_MODEL, D_PROJ])
    w_out = din("w_out", [L, 4096, D_MODEL])
    norm_pre = din("norm_pre", [L, 2048])
    norm_post = din("norm_post", [L, 2048])
    ret_norm = din("ret_norm", [L, 1024])
    sc_w = din("sc_conv_w", [L, 3, 1024])
    sc_b = din("sc_conv_b", [L, 1024])
    ssm_w = din("ssm_conv_w", [L, 4, 3072])
    ssm_b = din("ssm_conv_b", [L, 3072])
    dt_bias = din("ssm_dt_bias", [L, 32])
    a_log = din("ssm_a_log", [L, 32])
    d_skip = din("ssm_d", [L, 32])
    ssm_norm = din("ssm_norm", [L, 2048])
    cn = {}
    for name, shp in (("ident", [128, 128]), ("tri", [128, 128]), ("ones", [128, 128]), ("maskneg", [128, 128]),
                      ("mret", [128, 8, 128]), ("qdec", [128, 8]), ("kdec", [128, 8]), ("cdec", [128, 8]),
                      ("cs_p", [seqlen, 256]), ("sc_p", [seqlen, 256]), ("cs_s", [128, 256]), ("sc_s", [128, 256]),
                      ("gam_p", [128, 1]), ("selb", [128, 128]), ("sel4", [4, 64, 128])):
        cn[name] = din("c_" + name, shp)

    yp = dout("yp", [seqlen, D_MODEL])
    ys = dout("ys", [SB, D_MODEL])
    ret_p = dout("ret_p", [L, 8, 128, 128])
    sc_p = dout("sc_p", [L, 2, 1024])
    sconv_p = dout("sconv_p", [L, 3, 3072])
    ssm_p = dout("ssm_p", [L, 32, 64, 128])
    ret_s = dout("ret_s", [L, 128, 128, 128])
    sc_s = dout("sc_s", [L, SB, 2, 1024])
    sconv_s = dout("sconv_s", [L, SB, 3, 3072])
    ssm_s = dout("ssm_s", [L, 128, 4, 64, 128])

    xres = nc.dram_tensor("xres", [seqlen, D_MODEL], F32).ap()
    xsres = nc.dram_tensor("xsres", [SB, D_MODEL], F32).ap()
    pj = nc.dram_tensor("pj", [SB, D_PROJ + 96], F32).ap()

    T_xres = [Track() for _ in range(nseg * NT)]
    T_xsres = Track()
    T_pj = [Track() for _ in range(106)]
    T_out = Track()

    with ExitStack() as st:
        S = Sched(nc)
        NW = 53200
        art = st.enter_context(nc.sbuf_tensor("arena", [128, NW], F32))
        A = Arena(art, NW)
        PB = [st.enter_context(nc.psum_tensor("pb%d" % i, [128, 512], F32)) for i in range(8)]
        PT = [Track() for _ in range(8)]

        def pbf(i):
            return PB[i][:, :].bitcast(BF16)

        def TT(eng, out, in0, in1, op, R, W):
            S.op(eng, lambda e: e.tensor_tensor(out=out, in0=in0, in1=in1, op=op), R, W)

        def STT(eng, out, in0, scalar, in1, op0, op1, R, W):
            S.op(eng, lambda e: e.scalar_tensor_tensor(out=out, in0=in0, scalar=scalar, in1=in1, op0=op0, op1=op1), R, W)

        def TS(eng, out, in0, s1, s2, op0, op1, R, W):
            if s2 is None:
                S.op(eng, lambda e: e.tensor_scalar(out=out, in0=in0, scalar1=s1, scalar2=None, op0=op0), R, W)
            else:
                S.op(eng, lambda e: e.tensor_scalar(out=out, in0=in0, scalar1=s1, scalar2=s2, op0=op0, op1=op1), R, W)

        def ACT(out, in_, func, R, W, bias=None, scale=None, accum=None):
            kw = {}
            if bias is not None:
                kw["bias"] = bias
            if scale is not None:
                kw["scale"] = scale
            if accum is not None:
                kw["accum_out"] = accum
            S.op("act", lambda e: e.activation(out=out, in_=in_, func=func, **kw), R, W)

        def CP(eng, out, in_, R, W):
            if eng == "act":
                S.op("act", lambda e: e.activation(out=out, in_=in_, func=AF.Copy), R, W)
            else:
                S.op(eng, lambda e: e.tensor_copy(out=out, in_=in_), R, W)

        def MSET(eng, out, val, W):
            S.op(eng, lambda e: e.memset(out, val), (), W)

        def MM(bank, mms, R, W=()):
            def fn(e):
                ins = None
                for (o, l, r, s0, s1) in mms:
                    ins = e.matmul(o, lhsT=l, rhs=r, start=s0, stop=s1)
                return ins
            S.op("pe", fn, R, [PT[bank]] + list(W))

        def MMACC(bank, out, pairs, R):
            n = len(pairs)
            MM(bank, [(out, l, r, k == 0, k == n - 1) for k, (l, r) in enumerate(pairs)], R)

        def TRS(bank, items, R):
            def fn(e):
                ins = None
                for (o, i_, idn) in items:
                    ins = e.transpose(out=o, in_=i_, identity=idn)
                return ins
            S.op("pe", fn, R, [PT[bank]])

        def RSTD(out, in_, scale, R, W, tmp):
            TS("dve", tmp.ap, in_, scale, EPS, ALU.mult, ALU.add, R, [tmp])
            TT("pool", out, tmp.ap, mhalf.ap[0:tmp.ap.shape[0], 0:1], ALU.pow, [tmp, mhalf], W)

        hT = A.tile(KC * SEGT // 2, BF16, "p (c t) -> p c t", c=KC)
        mixT = A.tile(32 * SEGT // 2, BF16, "p (c t) -> p c t", c=32)
        WB = [A.tile(4096, BF16), A.tile(4096, BF16)]
        identb = A.tile(64, BF16)
        identf = A.tile(128)
        tri = A.tile(128)
        ones = A.tile(128)
        maskneg = A.tile(64, BF16)
        qdec = A.tile(8)
        kdec = A.tile(8)
        mhalf = A.tile(8)
        cdec = A.tile(8)
        gpreT = A.tile(16)
        gretT = A.tile(8)
        gssmT = A.tile(16)
        scwT = A.tile(24, F32, "p (c j) -> p c j", j=3)
        scbT = A.tile(8)
        ssmwT = A.tile(96, F32, "p (c j) -> p c j", j=4)
        ssmbT = A.tile(24)
        dtb_bc = A.tile(32)
        A_bc = A.tile(32)
        D_bc = A.tile(32)
        S_ret = A.tile(1024, F32, "p (h e) -> p h e", h=8)
        S_bf = A.tile(512, BF16, "p (h e) -> p h e", h=8)
        sT = A.tile(2048)
        sT_bf = A.tile(1024, BF16)
        Ucar = A.tile(16, F32, "p (c j) -> p c j", j=2)
        XBcar = A.tile(72, F32, "p (c j) -> p c j", j=3)
        hTs = A.tile(KC * SB // 2, BF16, "p (c t) -> p c t", c=KC)
        mixTs = A.tile(32 * SB // 2, BF16, "p (c t) -> p c t", c=32)
        gam_p = A.tile(1)
        selb = A.tile(128)
        sel4 = A.tile(512, F32, "p (s c) -> p s c", s=4, parts=64)
        wpar = [0]

        def wb_next():
            w = WB[wpar[0]]
            wpar[0] ^= 1
            return w

        S.dma("pool", identb.ap, cn["ident"], writes=[identb])
        S.dma("sp", identf.ap, cn["ident"], writes=[identf])
        S.dma("sp", tri.ap, cn["tri"], writes=[tri])
        S.dma("sp", ones.ap, cn["ones"], writes=[ones])
        S.dma("pool", maskneg.ap, cn["maskneg"], writes=[maskneg])
        S.dma("sp", qdec.ap[:, 0:8], cn["qdec"], writes=[qdec])
        S.dma("sp", kdec.ap[:, 0:8], cn["kdec"], writes=[kdec])
        S.dma("sp", cdec.ap[:, 0:8], cn["cdec"], writes=[cdec])
        S.dma("sp", gam_p.ap[:, 0:1], cn["gam_p"], writes=[gam_p])
        S.dma("sp", selb.ap, cn["selb"], writes=[selb])
        S.dma("sp", sel4.ap, cn["sel4"].rearrange("s p c -> p s c"), writes=[sel4])
        MSET("dve", mhalf.ap, -0.5, [mhalf])

        def load_layer_params(l):
            S.dma("sp", gpreT.ap[:, 0:16], norm_pre[l].rearrange("(c p) -> p c", p=128), writes=[gpreT], allow_slow_non_contiguous=True)
            S.dma("sp", gretT.ap[:, 0:8], ret_norm[l].rearrange("(c p) -> p c", p=128), writes=[gretT], allow_slow_non_contiguous=True)
            S.dma("sp", gssmT.ap[:, 0:16], ssm_norm[l].rearrange("(c p) -> p c", p=128), writes=[gssmT], allow_slow_non_contiguous=True)
            for j in range(3):
                S.dma("sp", scwT.ap[:, :, j], sc_w[l][j].rearrange("(c p) -> p c", p=128), writes=[scwT], allow_slow_non_contiguous=True)
            S.dma("sp", scbT.ap[:, 0:8], sc_b[l].rearrange("(c p) -> p c", p=128), writes=[scbT], allow_slow_non_contiguous=True)
            for j in range(4):
                S.dma("sp", ssmwT.ap[:, :, j], ssm_w[l][j].rearrange("(c p) -> p c", p=128), writes=[ssmwT], allow_slow_non_contiguous=True)
            S.dma("sp", ssmbT.ap[:, 0:24], ssm_b[l].rearrange("(c p) -> p c", p=128), writes=[ssmbT], allow_slow_non_contiguous=True)
            S.dma("sp", dtb_bc.ap[:, 0:32], dt_bias[l:l + 1, :].partition_broadcast(128), writes=[dtb_bc])
            S.dma("sp", A_bc.ap[:, 0:32], a_log[l:l + 1, :].partition_broadcast(128), writes=[A_bc])
            S.dma("sp", D_bc.ap[:, 0:32], d_skip[l:l + 1, :].partition_broadcast(128), writes=[D_bc])
            ACT(A_bc.ap[:, 0:32], A_bc.ap[:, 0:32], AF.Exp, [A_bc], [A_bc])
            TS("dve", A_bc.ap[:, 0:32], A_bc.ap[:, 0:32], -1.0, None, ALU.mult, None, [A_bc], [A_bc])
            MSET("dve", S_ret.ap, 0.0, [S_ret])
            MSET("dve", S_bf.ap, 0.0, [S_bf])
            MSET("dve", sT.ap, 0.0, [sT])
            MSET("dve", sT_bf.ap, 0.0, [sT_bf])
            MSET("dve", Ucar.ap, 0.0, [Ucar])
            MSET("dve", XBcar.ap, 0.0, [XBcar])

        def build_hT(l, seg):
            m = A.mark()
            xt = [A.tile(2048), A.tile(2048)]
            xn = [A.tile(1024, BF16), A.tile(1024, BF16)]
            ssq = [A.tile(1), A.tile(1)]
            rs = [A.tile(1), A.tile(1)]
            tmp = [A.tile(1), A.tile(1)]
            for t in range(NT):
                p = t % 2
                r0 = seg * SEGT + t * 128
                if l == 0:
                    S.dma("sp", xt[p].ap, xp[r0:r0 + 128, :], writes=[xt[p]])
                else:
                    S.dma("sp", xt[p].ap, xres[r0:r0 + 128, :], reads=[T_xres[seg * NT + t]], writes=[xt[p]])
                ACT(xn[p].ap, xt[p].ap, AF.Square, [xt[p]], [xn[p], ssq[p]], accum=ssq[p].ap[:, 0:1])
                RSTD(rs[p].ap[:, 0:1], ssq[p].ap[:, 0:1], 1.0 / D_MODEL, [ssq[p]], [rs[p]], tmp[p])
                ACT(xn[p].ap, xt[p].ap, AF.Copy, [xt[p], rs[p]], [xn[p]], scale=rs[p].ap[:, 0:1])
                for q in range(4):
                    bank = 6 + (q % 2)
                    TRS(bank, [(pbf(bank)[:, k * 128:(k + 1) * 128], xn[p].ap[:, (q * 4 + k) * 128:(q * 4 + k + 1) * 128], identb.ap)
                               for k in range(4)], [xn[p], identb])
                    for k in range(4):
                        c = q * 4 + k
                        dst = hT.sub(hT.ap[:, c, t * 128:(t + 1) * 128], (c * SEGT + t * 128) // 2, 64)
                        if k % 2 == 0:
                            ACT(dst.ap, pbf(bank)[:, k * 128:(k + 1) * 128], AF.Copy, [gpreT], [PT[bank], dst], scale=gpreT.ap[:, c:c + 1])
                        else:
                            TS("dve", dst.ap, pbf(bank)[:, k * 128:(k + 1) * 128], gpreT.ap[:, c:c + 1], None, ALU.mult, None,
                               [gpreT], [PT[bank], dst])
            A.release(m)

        def build_hTs(l):
            m = A.mark()
            xt = A.tile(2048, parts=SB)
            xn = A.tile(1024, BF16, parts=SB)
            ssq = A.tile(1, parts=SB)
            rs = A.tile(1, parts=SB)
            tmp = A.tile(1, parts=SB)
            if l == 0:
                S.dma("sp", xt.ap, xs, writes=[xt])
            else:
                S.dma("sp", xt.ap, xsres, reads=[T_xsres], writes=[xt])
            ACT(xn.ap, xt.ap, AF.Square, [xt], [xn, ssq], accum=ssq.ap[:, 0:1])
            RSTD(rs.ap[:, 0:1], ssq.ap[:, 0:1], 1.0 / D_MODEL, [ssq], [rs], tmp)
            ACT(xn.ap, xt.ap, AF.Copy, [xt, rs], [xn], scale=rs.ap[:, 0:1])
            for q in range(4):
                bank = 6 + (q % 2)
                TRS(bank, [(pbf(bank)[:, k * SB:(k + 1) * SB], xn.ap[:, (q * 4 + k) * 128:(q * 4 + k + 1) * 128], identb.ap[0:SB, 0:SB])
                           for k in range(4)], [xn, identb])
                for k in range(4):
                    c = q * 4 + k
                    ACT(hTs.ap[:, c, :], pbf(bank)[:, k * SB:(k + 1) * SB], AF.Copy, [gpreT], [PT[bank], hTs], scale=gpreT.ap[:, c:c + 1])
            A.release(m)

        def load_win_block(l, wb, cols):
            v = wb.ap.rearrange("p (c n) -> p c n", c=KC)
            src = w_in[l].rearrange("(c p) n -> p c n", p=128)
            o = 0
            for (c0, ncol) in cols:
                S.dma("pool", v[:, :, o:o + ncol], src[:, :, c0:c0 + ncol], writes=[wb])
                o += ncol
            return v

        PRE = {}

        def win_block(key, l, cols):
            if key in PRE:
                return PRE.pop(key)
            wb = wb_next()
            return wb, load_win_block(l, wb, cols)

        def prefetch_win(key, l, cols):
            wb = wb_next()
            PRE[key] = (wb, load_win_block(l, wb, cols))

        def out_block(key, l, ob):
            if key in PRE:
                return PRE.pop(key)
            wb = wb_next()
            v = wb.ap.rearrange("p (c n) -> p c n", c=32)
            S.dma("pool", v, w_out[l].rearrange("(c p) n -> p c n", p=128)[:, :, ob * 256:(ob + 1) * 256], writes=[wb])
            return wb, v

        def prefetch_out(key, l, ob):
            PRE[key] = out_block(("none",), l, ob)

        def sc_cols(cc):
            return [(C_BG + cc * 128, 128), (C_CG + cc * 128, 128), (C_H + cc * 128, 128), (C_GSC + cc * 128, 128)]

        def sample_proj(wb, v, cols, stage):
            ntot = sum(n for _, n in cols)
            MMACC(5, PB[5][0:SB, 0:ntot], [(hTs.ap[:, c, :], v[:, c, 0:ntot]) for c in range(KC)], [hTs, wb])
            CP("act", stage.ap[:, 0:ntot], PB[5][0:SB, 0:ntot], [], [PT[5], stage])
            o = 0
            for (c0, ncol) in cols:
                S.dma("sp", pj[:, c0:c0 + ncol], stage.ap[:, o:o + ncol], reads=[stage],
                      writes=T_pj[c0 // 128:(c0 + ncol + 127) // 128])
                o += ncol

        def ret_phase(l, seg, do_sample):
            m = A.mark()
            CS = A.tile(NT * 256, F32, "p (t x) -> p t x", t=NT)
            SC = A.tile(NT * 256, F32, "p (t x) -> p t x", t=NT)
            mret = A.tile(1024, F32, "p (h i) -> p h i", h=8)
            gret_bc = A.tile(1024)
            r0 = seg * SEGT
            S.dma("sp", CS.ap, cn["cs_p"][r0:r0 + SEGT, :].rearrange("(t p) x -> p t x", p=128), writes=[CS])
            S.dma("sp", SC.ap, cn["sc_p"][r0:r0 + SEGT, :].rearrange("(t p) x -> p t x", p=128), writes=[SC])
            S.dma("sp", mret.ap, cn["mret"], writes=[mret])
            S.dma("sp", gret_bc.ap, ret_norm[l:l + 1, :].partition_broadcast(128), writes=[gret_bc])
            stage = A.tile(512, parts=SB) if do_sample else None
            H = []
            for h in range(8):
                H.append(dict(
                    qkT=A.tile(2 * SEGT // 2, BF16, "p (a t) -> p a t", a=2),
                    qkr=A.tile(NT * 256 // 2, BF16, "p (t a d) -> p t a d", t=NT, a=2),
                    vbf=A.tile(NT * 128 // 2, BF16, "p (t e) -> p t e", t=NT),
                    vdec=A.tile(NT * 128 // 2, BF16, "p (t e) -> p t e", t=NT)))
            GS = [A.tile(NT * 512 // 2, BF16, "p (t i e) -> p t i e", t=NT, i=4) for _ in range(2)]
            ABCD_t = [A.tile(512) for _ in range(2)]
            ABCD = [dict(AB=t_.sub(t_.ap[:, 0:256], 0, 256), CD=t_.sub(t_.ap[:, 256:512], 256, 256)) for t_ in ABCD_t]
            for h in range(8):
                Hh = H[h]
                wb = wb_next()
                cols = [(C_Q + h * 128, 128), (C_K + h * 128, 128), (C_V + h * 128, 128), (C_GR + h * 128, 128)]
                v = load_win_block(l, wb, cols)
                for t in range(NT):
                    bank = t % 4
                    P = PB[bank]
                    B = ABCD[t % 2]
                    MMACC(bank, P[:, 0:512], [(hT.ap[:, c, t * 128:(t + 1) * 128], v[:, c, :]) for c in range(KC)], [hT, wb])
                    P4 = P[:, 0:256].rearrange("p (a b f) -> p a b f", a=2, b=2)
                    AB4 = B["AB"].ap.rearrange("p (a b f) -> p a b f", a=2, b=2)
                    CD4 = B["CD"].ap.rearrange("p (a b f) -> p a b f", a=2, b=2)
                    TT("dve", AB4, P4, CS.ap[:, t, :].rearrange("p (a b f) -> p a b f", a=2, b=2), ALU.mult, [CS], [PT[bank], B["AB"]])
                    TT("dve", CD4, P4, SC.ap[:, t, :].rearrange("p (a b f) -> p a b f", a=2, b=2), ALU.mult, [SC], [PT[bank], B["CD"]])
                    TT("dve", Hh["qkr"].ap[:, t, :, 0:64], AB4[:, :, 0, :], AB4[:, :, 1, :], ALU.subtract, [B["AB"]], [Hh["qkr"]])
                    TT("dve", Hh["qkr"].ap[:, t, :, 64:128], CD4[:, :, 0, :], CD4[:, :, 1, :], ALU.add, [B["CD"]], [Hh["qkr"]])
                    ACT(Hh["vbf"].ap[:, t, :], P[:, 256:384], AF.Copy, [], [PT[bank], Hh["vbf"]])
                    ACT(Hh["vdec"].ap[:, t, :], P[:, 256:384], AF.Copy, [kdec], [PT[bank], Hh["vdec"]], scale=kdec.ap[:, h:h + 1])
                    ACT(GS[h // 4].ap[:, t, h % 4, :], P[:, 384:512], AF.Silu, [], [PT[bank], GS[h // 4]])
                    tb = 6 + (t % 2)
                    TRS(tb, [(pbf(tb)[:, a * 128:(a + 1) * 128], Hh["qkr"].ap[:, t, a, :], identb.ap) for a in range(2)], [Hh["qkr"], identb])
                    CP("act", Hh["qkT"].ap[:, :, t * 128:(t + 1) * 128], pbf(tb)[:, 0:256].rearrange("p (a i) -> p a i", a=2), [], [PT[tb], Hh["qkT"]])
                if do_sample:
                    sample_proj(wb, v, cols, stage)
            Pm = [A.tile(256, BF16, "p (i j) -> p i j", i=4) for _ in range(2)]
            osb = [A.tile(512, F32, "p (i e) -> p i e", i=4) for _ in range(2)]
            sq = [Tile(t_.ap.rearrange("p (i e) -> p i e", i=4), t_.tracks, t_.arena, t_.off) for t_ in ABCD_t]
            og = [A.tile(256, BF16, "p (i e) -> p i e", i=4) for _ in range(2)]
            st4 = []
            for _ in range(2):
                t_ = A.tile(24)
                st4.append({nm: t_.sub(t_.ap[:, 4 * k_:4 * k_ + 4], 0, 24) for k_, nm in enumerate(("s1", "s2", "mean", "msq", "var", "rs"))})
            b4 = lambda ap: ap.rearrange("p (i e) -> p i e", i=4)

            def finalize(c, qd):
                sl = slice(c * 128, (c + 1) * 128)
                h0 = qd * 4
                bsc = 4 if qd == 0 else 7
                TRS(bsc, [(pbf(bsc)[:, i * 128:(i + 1) * 128], og[qd].ap[:, i, :], identb.ap) for i in range(4)], [og[qd], identb])
                dsts = [mixT.sub(None, ((h0 + i) * SEGT + c * 128) // 2, 64) for i in range(4)]
                CP("act", mixT.ap[:, h0:h0 + 4, sl], pbf(bsc)[:, 0:512].rearrange("p (i e) -> p i e", i=4), [], [PT[bsc]] + dsts)

            def unit(c, qd, prev):
                    if prev is not None:
                        finalize(*prev)
                    sl = slice(c * 128, (c + 1) * 128)
                    h0 = qd * 4
                    HQ = H[h0:h0 + 4]
                    bsc, bst, bo = (4 if qd == 0 else 7), 5, 6
                    MM(bsc, [(PB[bsc][:, i * 128:(i + 1) * 128], HQ[i]["qkT"].ap[:, 1, sl], HQ[i]["qkT"].ap[:, 0, sl], True, True) for i in range(4)],
                       [x["qkT"] for x in HQ])
                    TT("dve", Pm[qd].ap, b4(PB[bsc][:, 0:512]), mret.ap[:, h0:h0 + 4, :], ALU.mult, [mret], [PT[bsc], Pm[qd]])
                    MM(bst, [(PB[bst][:, i * 128:(i + 1) * 128], HQ[i]["qkr"].ap[:, c, 1, :], HQ[i]["vdec"].ap[:, c, :], True, True) for i in range(4)],
                       [x["qkr"] for x in HQ] + [x["vdec"] for x in HQ])
                    mms = []
                    for i in range(4):
                        o = PB[bo][:, i * 128:(i + 1) * 128]
                        mms.append((o, Pm[qd].ap[:, i, :], HQ[i]["vbf"].ap[:, c, :], True, False))
                        mms.append((o, HQ[i]["qkT"].ap[:, 0, sl], S_bf.ap[:, h0 + i, :], False, True))
                    MM(bo, mms, [Pm[qd], S_bf] + [x["vbf"] for x in HQ] + [x["qkT"] for x in HQ])
                    TT("dve", S_ret.ap[:, h0:h0 + 4, :], S_ret.ap[:, h0:h0 + 4, :], cdec.ap[:, h0:h0 + 4].unsqueeze(2).to_broadcast([128, 4, 128]), ALU.mult, [cdec], [S_ret])
                    TT("dve", S_ret.ap[:, h0:h0 + 4, :], S_ret.ap[:, h0:h0 + 4, :], b4(PB[bst][:, 0:512]), ALU.add, [], [PT[bst], S_ret])
                    CP("act", S_bf.ap[:, h0:h0 + 4, :], S_ret.ap[:, h0:h0 + 4, :], [S_ret], [S_bf])
                    s = st4[qd]
                    TT("dve", osb[qd].ap, b4(PB[bo][:, 0:512]), qdec.ap[:, h0:h0 + 4].unsqueeze(2).to_broadcast([128, 4, 128]), ALU.mult, [qdec], [PT[bo], osb[qd]])
                    S.op("dve", lambda e, o=s["s1"].ap, i_=osb[qd].ap: e.tensor_reduce(out=o, in_=i_, axis=AX.X, op=ALU.add), [osb[qd]], [s["s1"]])
                    ACT(sq[qd].ap, osb[qd].ap, AF.Square, [osb[qd]], [sq[qd]])
                    S.op("dve", lambda e, o=s["s2"].ap, i_=sq[qd].ap: e.tensor_reduce(out=o, in_=i_, axis=AX.X, op=ALU.add), [sq[qd]], [s["s2"]])
                    TS("dve", s["mean"].ap, s["s1"].ap, 1.0 / 128, None, ALU.mult, None, [s["s1"]], [s["mean"]])
                    TT("dve", s["msq"].ap, s["mean"].ap, s["mean"].ap, ALU.mult, [s["mean"]], [s["msq"]])
                    STT("dve", s["var"].ap, s["s2"].ap, 1.0 / 128, s["msq"].ap, ALU.mult, ALU.subtract, [s["s2"], s["msq"]], [s["var"]])
                    TS("dve", s["var"].ap, s["var"].ap, EPS, None, ALU.add, None, [], [s["var"]])
                    TT("pool", s["rs"].ap, s["var"].ap, mhalf.ap[:, 0:4], ALU.pow, [s["var"], mhalf], [s["rs"]])
                    TT("dve", osb[qd].ap, osb[qd].ap, s["mean"].ap.unsqueeze(2).to_broadcast([128, 4, 128]), ALU.subtract, [s["mean"]], [osb[qd]])
                    TT("dve", osb[qd].ap, osb[qd].ap, s["rs"].ap.unsqueeze(2).to_broadcast([128, 4, 128]), ALU.mult, [s["rs"]], [osb[qd]])
                    TT("dve", osb[qd].ap, osb[qd].ap, b4(gret_bc.ap[:, h0 * 128:(h0 + 4) * 128]), ALU.mult, [gret_bc], [osb[qd]])
                    TT("dve", og[qd].ap, osb[qd].ap, GS[qd].ap[:, c, :, :], ALU.mult, [osb[qd], GS[qd]], [og[qd]])

            items = [(c, qd) for c in range(NT) for qd in range(2)]
            units = []
            for k, (c, qd) in enumerate(items):
                prev = items[k - 1] if k > 0 else None
                units.append(lambda c=c, qd=qd, prev=prev: unit(c, qd, prev))
            units.append(lambda: finalize(*items[-1]))
            return m, units, stage

        def sc_phase(l, seg, do_sample, units, stage):
            m = A.mark()
            bufs = []
            for par in range(1):
                bufs.append(dict(cg=A.tile(512), U=A.tile(SEGT + 2), acc=A.tile(512), sg=A.tile(256, BF16), bgc=A.tile(256, BF16)))
            for cc in range(8):
                B = bufs[0]
                cols = sc_cols(cc)
                wb, v = win_block(("sc", cc), l, cols)
                if cc + 1 < 8:
                    prefetch_win(("sc", cc + 1), l, sc_cols(cc + 1))
                else:
                    prefetch_win(("xbc", 0), l, [(C_XBC, 512)])
                bk = [j for j in range(4)]
                for j in range(4):
                    MMACC(bk[j], PB[bk[j]][:, 0:SEGT], [(v[:, c, j * 128:(j + 1) * 128], hT.ap[:, c, :]) for c in range(KC)], [hT, wb])
                CP("act", B["cg"].ap, PB[bk[1]][:, 0:SEGT], [], [PT[bk[1]], B["cg"]])
                ACT(B["sg"].ap, PB[bk[3]][:, 0:SEGT], AF.Silu, [], [PT[bk[3]], B["sg"]])
                CP("act", B["bgc"].ap, PB[bk[0]][:, 0:SEGT], [], [PT[bk[0]], B["bgc"]])
                CP("dve", B["U"].ap[:, 0:2], Ucar.ap[:, cc, :], [Ucar], [B["U"]])
                TT("dve", B["U"].ap[:, 2:2 + SEGT], B["cg"].ap, PB[bk[2]][:, 0:SEGT], ALU.mult, [B["cg"]], [PT[bk[2]], B["U"]])
                if cc < len(units):
                    units[cc]()
                CP("dve", Ucar.ap[:, cc, :], B["U"].ap[:, SEGT:SEGT + 2], [B["U"]], [Ucar])
                TS("dve", B["acc"].ap, B["U"].ap[:, 0:SEGT], scwT.ap[:, cc, 0:1], scbT.ap[:, cc:cc + 1], ALU.mult, ALU.add, [B["U"], scwT, scbT], [B["acc"]])
                STT("dve", B["acc"].ap, B["U"].ap[:, 1:1 + SEGT], scwT.ap[:, cc, 1:2], B["acc"].ap, ALU.mult, ALU.add, [B["U"], scwT], [B["acc"]])
                STT("dve", B["acc"].ap, B["U"].ap[:, 2:2 + SEGT], scwT.ap[:, cc, 2:3], B["acc"].ap, ALU.mult, ALU.add, [B["U"], scwT], [B["acc"]])
                TT("dve", B["acc"].ap, B["acc"].ap, B["sg"].ap, ALU.mult, [B["sg"]], [B["acc"]])
                dst = mixT.sub(mixT.ap[:, 8 + cc, :], ((8 + cc) * SEGT) // 2, SEGT // 2)
                TT("dve", dst.ap, B["acc"].ap, B["bgc"].ap, ALU.mult, [B["acc"], B["bgc"]], [dst])
                if do_sample:
                    sample_proj(wb, v, cols, stage)
            for u in units[8:]:
                u()
            A.release(m)

        def ssd_phase(l, seg, do_sample):
            m = A.mark()
            stage = A.tile(512, parts=SB) if do_sample else None
            XS = A.tile(NT * 2048 // 2, BF16, "p (t x) -> p t x", t=NT)
            SZ = A.tile(NT * 2048 // 2, BF16, "p (t x) -> p t x", t=NT)
            BMT = A.tile(4 * SEGT // 2, BF16, "p (g t) -> p g t", g=4)
            CMT = A.tile(4 * SEGT // 2, BF16, "p (g t) -> p g t", g=4)
            BM = A.tile(NT * 512 // 2, BF16, "p (t g n) -> p t g n", t=NT, g=4)
            DT = A.tile(NT * 32, F32, "p (t h) -> p t h", t=NT)
            AA = A.tile(NT * 32, F32, "p (t h) -> p t h", t=NT)
            NACUM = A.tile(NT * 32, F32, "p (t h) -> p t h", t=NT)
            NAA = A.tile(NT * 32, F32, "p (t h) -> p t h", t=NT)
            EA = A.tile(NT * 32, F32, "p (t h) -> p t h", t=NT)
            WJ = A.tile(NT * 32, F32, "p (t h) -> p t h", t=NT)
            ELAST = A.tile(NT * 32, F32, "p (t h) -> p t h", t=NT)
            wdt = A.tile(KC * 32 // 2, BF16, "p (c n) -> p c n", c=KC)
            t32 = [A.tile(32), A.tile(32), A.tile(32)]
            S.dma("pool", wdt.ap, w_in[l].rearrange("(c p) n -> p c n", p=128)[:, :, C_DT:C_DT + 32], writes=[wdt])
            if do_sample:
                MMACC(5, PB[5][0:SB, 0:32], [(hTs.ap[:, c, :], wdt.ap[:, c, :]) for c in range(KC)], [hTs, wdt])
                CP("act", stage.ap[:, 0:32], PB[5][0:SB, 0:32], [], [PT[5], stage])
                S.dma("sp", pj[:, C_DT:C_DT + 32], stage.ap[:, 0:32], reads=[stage], writes=[T_pj[104]])
            for t in range(NT):
                tsl = slice(t * 128, (t + 1) * 128)
                MMACC(0, PB[0][:, 0:32], [(hT.ap[:, c, tsl], wdt.ap[:, c, :]) for c in range(KC)], [hT, wdt])
                xdt, ax, lg = t32
                TT("dve", xdt.ap[:, 0:32], PB[0][:, 0:32], dtb_bc.ap[:, 0:32], ALU.add, [dtb_bc], [PT[0], xdt])
                STT("dve", ax.ap[:, 0:32], xdt.ap[:, 0:32], -1.0, xdt.ap[:, 0:32], ALU.mult, ALU.max, [xdt], [ax])
                ACT(ax.ap[:, 0:32], ax.ap[:, 0:32], AF.Exp, [ax], [ax], scale=-1.0)
                ACT(lg.ap[:, 0:32], ax.ap[:, 0:32], AF.Ln, [ax], [lg], bias=1.0)
                STT("dve", DT.ap[:, t, :], xdt.ap[:, 0:32], 0.0, lg.ap[:, 0:32], ALU.max, ALU.add, [xdt, lg], [DT])
                TT("dve", AA.ap[:, t, :], DT.ap[:, t, :], A_bc.ap[:, 0:32], ALU.mult, [DT, A_bc], [AA])
                TS("dve", NAA.ap[:, t, :], AA.ap[:, t, :], -1.0, None, ALU.mult, None, [AA], [NAA])
                MM(1, [(PB[1][:, 0:32], tri.ap, AA.ap[:, t, :], True, True),
                       (PB[1][:, 32:64], ones.ap, AA.ap[:, t, :], True, True)], [tri, ones, AA])
                TS("dve", NACUM.ap[:, t, :], PB[1][:, 0:32], -1.0, None, ALU.mult, None, [], [PT[1], NACUM])
                ACT(EA.ap[:, t, :], PB[1][:, 0:32], AF.Exp, [], [PT[1], EA])
                ACT(ELAST.ap[:, t, :], PB[1][:, 32:64], AF.Exp, [], [PT[1], ELAST])
                TT("dve", ax.ap[:, 0:32], PB[1][:, 32:64], NACUM.ap[:, t, :], ALU.add, [NACUM], [PT[1], ax])
                ACT(ax.ap[:, 0:32], ax.ap[:, 0:32], AF.Exp, [ax], [ax])
                TT("dve", WJ.ap[:, t, :], ax.ap[:, 0:32], DT.ap[:, t, :], ALU.mult, [ax, DT], [WJ])
            xb = [dict(XB=A.tile(SEGT + 3), acc=A.tile(SEGT), xc=A.tile(SEGT // 2, BF16)) for _ in range(2)]
            for bi in range(6):
                cols = [(C_XBC + bi * 512, 512)]
                wb, v = win_block(("xbc", bi), l, cols)
                for j in range(4):
                    gc = bi * 4 + j
                    B = xb[gc % 2]
                    bank = gc % 4
                    MMACC(bank, PB[bank][:, 0:SEGT], [(v[:, c, j * 128:(j + 1) * 128], hT.ap[:, c, :]) for c in range(KC)], [hT, wb])
                    CP("dve", B["XB"].ap[:, 0:3], XBcar.ap[:, gc, :], [XBcar], [B["XB"]])
                    CP("act", B["XB"].ap[:, 3:3 + SEGT], PB[bank][:, 0:SEGT], [], [PT[bank], B["XB"]])
                    CP("dve", XBcar.ap[:, gc, :], B["XB"].ap[:, SEGT:SEGT + 3], [B["XB"]], [XBcar])
                    TS("dve", B["acc"].ap, B["XB"].ap[:, 0:SEGT], ssmwT.ap[:, gc, 0:1], ssmbT.ap[:, gc:gc + 1], ALU.mult, ALU.add, [B["XB"], ssmwT, ssmbT], [B["acc"]])
                    for k in range(1, 4):
                        STT("dve", B["acc"].ap, B["XB"].ap[:, k:k + SEGT], ssmwT.ap[:, gc, k:k + 1], B["acc"].ap, ALU.mult, ALU.add, [B["XB"], ssmwT], [B["acc"]])
                    if gc < 16:
                        ACT(B["xc"].ap, B["acc"].ap, AF.Silu, [B["acc"]], [B["xc"]])
                        tb = 6 + (gc % 2)
                        TRS(tb, [(pbf(tb)[:, t * 128:(t + 1) * 128], B["xc"].ap[:, t * 128:(t + 1) * 128], identb.ap) for t in range(NT)], [B["xc"], identb])
                        CP("act" if gc % 2 else "dve", XS.ap[:, :, gc * 128:(gc + 1) * 128], pbf(tb)[:, 0:NT * 128].rearrange("p (t c) -> p t c", t=NT), [], [PT[tb], XS])
                    elif gc < 20:
                        g = gc - 16
                        ACT(BMT.ap[:, g, :], B["acc"].ap, AF.Silu, [B["acc"]], [BMT])
                        tb = 6 + (gc % 2)
                        TRS(tb, [(pbf(tb)[:, t * 128:(t + 1) * 128], BMT.ap[:, g, t * 128:(t + 1) * 128], identb.ap) for t in range(NT)], [BMT, identb])
                        CP("dve", BM.ap[:, :, g, :], pbf(tb)[:, 0:NT * 128].rearrange("p (t c) -> p t c", t=NT), [], [PT[tb], BM])
                    else:
                        g = gc - 20
                        ACT(CMT.ap[:, g, :], B["acc"].ap, AF.Silu, [B["acc"]], [CMT])
                if do_sample:
                    sample_proj(wb, v, cols, stage)
            for zb in range(4):
                wb = wb_next()
                cols = [(C_Z + zb * 512, 512)]
                v = load_win_block(l, wb, cols)
                for t in range(NT):
                    bank = t % 2
                    MMACC(bank, PB[bank][:, 0:512], [(hT.ap[:, c, t * 128:(t + 1) * 128], v[:, c, :]) for c in range(KC)], [hT, wb])
                    ACT(SZ.ap[:, t, zb * 512:(zb + 1) * 512], PB[bank][:, 0:512], AF.Silu, [], [PT[bank], SZ])
                if do_sample:
                    sample_proj(wb, v, cols, stage)
            prefetch_out(("out", 0), l, 0)
            prefetch_out(("out", 1), l, 1)
            cb = [A.tile(128), A.tile(128)]
            Lt = [A.tile(512), A.tile(512)]
            MT = [A.tile(256, BF16, "p (i j) -> p i j", i=4), A.tile(256, BF16, "p (i j) -> p i j", i=4)]
            XDT = [A.tile(1024, BF16)] * 2
            R4 = [A.tile(512), A.tile(512)]
            t1 = A.tile(512)
            t2 = A.tile(512)
            t2s = [t2, A.tile(512)]
            ssq4_t = A.tile(4)
            xw = [A.tile(256, BF16), A.tile(256, BF16)]
            YZ = A.tile(2048)
            YN = A.tile(1024, BF16)
            ssq = A.tile(1); rs = A.tile(1); tmp = A.tile(1)
            h8 = lambda ap: ap.rearrange("p (h q) -> p h q", h=8)

            def step1(c, g):
                sl = slice(c * 128, (c + 1) * 128)
                MM(0, [(PB[0][:, 0:128], BMT.ap[:, g, sl], CMT.ap[:, g, sl], True, True)], [BMT, CMT])
                ib = 1 if g % 2 == 0 else 7
                MM(ib, [(PB[ib][:, 0:512], CMT.ap[:, g, sl], sT_bf.ap[:, g * 512:(g + 1) * 512], True, True)], [CMT, sT_bf])
                for r in range(2):
                    bb = 2 + r
                    h0 = g * 8 + r * 4
                    TT("pool", R4[r].ap.rearrange("p (i j) -> p i j", i=4), tri.ap.unsqueeze(1).to_broadcast([128, 4, 128]),
                       AA.ap[:, c, h0:h0 + 4].unsqueeze(2).to_broadcast([128, 4, 128]), ALU.mult, [tri, AA], [R4[r]])
                    o = PB[bb][:, 0:512]
                    if r == 1:
                        gs_ = slice(g * 512, (g + 1) * 512)
                        hs = slice(g * 8, (g + 1) * 8)
                        TT("pool", h8(xw[g % 2].ap), h8(XS.ap[:, c, gs_]), WJ.ap[:, c, hs].unsqueeze(2).to_broadcast([128, 8, 64]), ALU.mult, [XS, WJ], [xw[g % 2]])
                        TT("pool", h8(t2s[g % 2].ap), h8(XS.ap[:, c, gs_]), D_bc.ap[:, hs].unsqueeze(2).to_broadcast([128, 8, 64]), ALU.mult, [XS, D_bc], [t2s[g % 2]])
                    MM(bb, [(o, ones.ap, R4[r].ap, True, False),
                            (o, tri.ap, NAA.ap[:, c, h0:h0 + 4].unsqueeze(2).to_broadcast([128, 4, 128]), False, False),
                            (o, identb.ap, maskneg.ap.unsqueeze(1).to_broadcast([128, 4, 128]), False, True)],
                       [ones, R4[r], tri, NAA, identb, maskneg])

            def step23(c, g):
                cbt = cb[g % 2]
                if g == 0:
                    TT("dve", XDT[0].ap.rearrange("p (h q) -> p h q", h=32), XS.ap[:, c, :].rearrange("p (h q) -> p h q", h=32),
                       DT.ap[:, c, :].unsqueeze(2).to_broadcast([128, 32, 64]), ALU.mult, [XS, DT], [XDT[0]])
                CP("act", cbt.ap, PB[0][:, 0:128], [], [PT[0], cbt])
                for r in range(2):
                    bb = 2 + r
                    ACT(Lt[r].ap, PB[bb][:, 0:512], AF.Exp, [], [PT[bb], Lt[r]])
                    TT("dve", MT[r].ap, Lt[r].ap.rearrange("p (i j) -> p i j", i=4), cbt.ap.unsqueeze(1).to_broadcast([128, 4, 128]), ALU.mult, [Lt[r], cbt], [MT[r]])

            def step4(c, g):
                yb = 4 + (g % 2)
                mms = []
                for r in range(2):
                    for i in range(4):
                        hh = r * 4 + i
                        h = g * 8 + hh
                        mms.append((PB[yb][:, hh * 64:(hh + 1) * 64], MT[r].ap[:, i, :], XDT[c % 2].ap[:, h * 64:(h + 1) * 64], True, True))
                MM(yb, mms, [MT[0], MT[1], XDT[c % 2]])
                MM(6, [(PB[6][:, 0:512], BM.ap[:, c, g, :], xw[g % 2].ap, True, True)], [BM, xw[g % 2]])

            def step56(c, g):
                gs_ = slice(g * 512, (g + 1) * 512)
                hs = slice(g * 8, (g + 1) * 8)
                yb = 4 + (g % 2)
                ib = 1 if g % 2 == 0 else 7
                TT("dve", h8(t1.ap), h8(PB[ib][:, 0:512]), EA.ap[:, c, hs].unsqueeze(2).to_broadcast([128, 8, 64]), ALU.mult, [EA], [PT[ib], t1])
                TT("dve", t1.ap, t1.ap, PB[yb][:, 0:512], ALU.add, [], [PT[yb], t1])
                TT("dve", t1.ap, t1.ap, t2s[g % 2].ap, ALU.add, [t2s[g % 2]], [t1])
                TT("dve", YZ.ap[:, gs_], t1.ap, SZ.ap[:, c, gs_], ALU.mult, [t1, SZ], [YZ])
                TT("dve", h8(sT.ap[:, gs_]), h8(sT.ap[:, gs_]), ELAST.ap[:, c, hs].unsqueeze(2).to_broadcast([128, 8, 64]), ALU.mult, [ELAST], [sT])
                TT("dve", sT.ap[:, gs_], sT.ap[:, gs_], PB[6][:, 0:512], ALU.add, [], [PT[6], sT])
                CP("act", sT_bf.ap[:, gs_], sT.ap[:, gs_], [sT], [sT_bf])

            def chunk_tail(c):
                sl = slice(c * 128, (c + 1) * 128)
                ssq4 = ssq4_t
                for q in range(4):
                    ACT(t1.ap, YZ.ap[:, q * 512:(q + 1) * 512], AF.Square, [YZ], [t1, ssq4], accum=ssq4.ap[:, q:q + 1])
                S.op("dve", lambda e, o=ssq.ap[:, 0:1], i=ssq4.ap[:, 0:4]: e.tensor_reduce(out=o, in_=i, axis=AX.X, op=ALU.add), [ssq4], [ssq])
                RSTD(rs.ap[:, 0:1], ssq.ap[:, 0:1], 1.0 / 2048, [ssq], [rs], tmp)
                ACT(YN.ap, YZ.ap, AF.Copy, [YZ, rs], [YN], scale=rs.ap[:, 0:1])
                for q in range(4):
                    tb = 7 if q % 2 == 0 else 6
                    TRS(tb, [(pbf(tb)[:, k * 128:(k + 1) * 128], YN.ap[:, (q * 4 + k) * 128:(q * 4 + k + 1) * 128], identb.ap) for k in range(4)], [YN, identb])
                    for k in range(4):
                        cc = q * 4 + k
                        dst = mixT.sub(mixT.ap[:, 16 + cc, sl], ((16 + cc) * SEGT + c * 128) // 2, 64)
                        if k % 2 == 0:
                            ACT(dst.ap, pbf(tb)[:, k * 128:(k + 1) * 128], AF.Copy, [gssmT], [PT[tb], dst], scale=gssmT.ap[:, cc:cc + 1])
                        else:
                            TS("dve", dst.ap, pbf(tb)[:, k * 128:(k + 1) * 128], gssmT.ap[:, cc:cc + 1], None, ALU.mult, None, [gssmT], [PT[tb], dst])

            items = [(c, g) for c in range(NT) for g in range(4)]
            step1(*items[0])
            step23(*items[0])
            for k, (c, g) in enumerate(items):
                if k + 1 < len(items):
                    step1(*items[k + 1])
                step4(c, g)
                step56(c, g)
                if g == 3:
                    chunk_tail(c)
                if k + 1 < len(items):
                    step23(*items[k + 1])
            A.release(m)

        def out_phase(l, seg, do_sample, hoist):
            m = A.mark()
            OUT = A.tile(NT * 2048, F32, "p (t x) -> p t x", t=NT)
            gpost = A.tile(2048)
            S.dma("sp", gpost.ap, norm_post[l:l + 1, :].partition_broadcast(128), writes=[gpost])
            outs = A.tile(2048, parts=SB) if do_sample else None
            last = (l == L - 1)
            for ob in range(8):
                wb, v = out_block(("out", ob), l, ob)
                for t in range(NT):
                    bank = t % 4
                    MMACC(bank, PB[bank][:, 0:256], [(mixT.ap[:, mc, t * 128:(t + 1) * 128], v[:, mc, :]) for mc in range(32)], [mixT, wb])
                    CP("act" if t % 2 else "dve", OUT.ap[:, t, ob * 256:(ob + 1) * 256], PB[bank][:, 0:256], [], [PT[bank], OUT])
                if do_sample:
                    MMACC(5, PB[5][0:SB, 0:256], [(mixTs.ap[:, mc, :], v[:, mc, :]) for mc in range(32)], [mixTs, wb])
                    CP("act", outs.ap[:, ob * 256:(ob + 1) * 256], PB[5][0:SB, 0:256], [], [PT[5], outs])
            xt = [A.tile(2048), A.tile(2048)]
            junk = A.tile(2048)
            ssq = [A.tile(1), A.tile(1)]; rs = [A.tile(1), A.tile(1)]; tmp = [A.tile(1), A.tile(1)]
            if hoist is not None:
                hoist()

            def finish(o_ap, o_tile, x_t, np_, src_ap, src_reads, dst_ap, dst_tracks, sq, r, tm, preloaded=False):
                if not preloaded:
                    S.dma("sp", x_t.ap, src_ap, reads=src_reads, writes=[x_t])
                ACT(junk.ap[0:np_, :], o_ap, AF.Square, [o_tile], [junk, sq], accum=sq.ap[:, 0:1])
                RSTD(r.ap[:, 0:1], sq.ap[:, 0:1], 1.0 / D_MODEL, [sq], [r], tm)
                STT("dve", o_ap, o_ap, r.ap[:, 0:1], gpost.ap[0:np_, :], ALU.mult, ALU.mult, [r, gpost], [o_tile])
                TT("dve", x_t.ap, x_t.ap, o_ap, ALU.add, [o_tile], [x_t])
                S.dma("sp", dst_ap, x_t.ap, reads=[x_t], writes=dst_tracks)

            def xsrc(t):
                r0 = seg * SEGT + t * 128
                tr = T_xres[seg * NT + t]
                return (xp[r0:r0 + 128, :], []) if l == 0 else (xres[r0:r0 + 128, :], [tr])

            s0, sr0 = xsrc(0)
            S.dma("sp", xt[0].ap, s0, reads=sr0, writes=[xt[0]])
            for t in range(NT):
                p = t % 2
                r0 = seg * SEGT + t * 128
                tr = T_xres[seg * NT + t]
                if t + 1 < NT:
                    s1, sr1 = xsrc(t + 1)
                    S.dma("sp", xt[1 - p].ap, s1, reads=sr1, writes=[xt[1 - p]])
                if last:
                    dst, dt_ = yp[r0:r0 + 128, :], []
                else:
                    dst, dt_ = xres[r0:r0 + 128, :], [tr]
                finish(OUT.ap[:, t, :], OUT, xt[p], 128, None, None, dst, dt_, ssq[p], rs[p], tmp[p], preloaded=True)
            if do_sample:
                xts = A.tile(2048, parts=SB)
                sq = A.tile(1, parts=SB); r = A.tile(1, parts=SB); tm = A.tile(1, parts=SB)
                src, sr = (xs, []) if l == 0 else (xsres, [T_xsres])
                dst, dt_ = (ys, []) if last else (xsres, [T_xsres])
                finish(outs.ap, outs, xts, SB, src, sr, dst, dt_, sq, r, tm)
            A.release(m)

        def decode_phase(l):
            m = A.mark()
            PJT = T_pj

            def ld(words, src, tr, parts=128, pat=None, **kw):
                t = A.tile(words, F32, pat, parts=parts, **kw)
                S.dma("sp", t.ap, src, reads=tr, writes=[t])
                return t

            def ldj(nj, c, nk, srcj, parts=128):
                t = A.tile(nj * c, F32, "p (j c) -> p j c", parts=parts, j=nj)
                for j in range(nj):
                    S.dma("sp", t.ap[:, j, :], srcj(j), writes=[t])
                return t

            def stj(dstj, t, nj):
                for j in range(nj):
                    S.dma("sp", dstj(j), t.ap[:, j, :], reads=[t])

            def pjv(c0, n, c):
                return pj[:, c0:c0 + n].rearrange("b (k c) -> k b c", c=c)

            rs = A.tile(1); tmp = A.tile(1)
            q = ld(128, pjv(C_Q, 1024, 128), PJT[0:8])
            k = ld(128, pjv(C_K, 1024, 128), PJT[8:16])
            vv = ld(128, pjv(C_V, 1024, 128), PJT[16:24])
            gr = ld(128, pjv(C_GR, 1024, 128), PJT[24:32])
            css = ld(256, cn["cs_s"], [])
            scs = ld(256, cn["sc_s"], [])
            gretP = ld(128, ret_norm[l].rearrange("(h e) -> h e", e=128).unsqueeze(1).to_broadcast([8, SB, 128]), [])
            qk = A.tile(256); AB = A.tile(256); CD = A.tile(256); qkr = A.tile(256, F32, "p (a d) -> p a d", a=2)
            CP("dve", qk.ap[:, 0:128], q.ap, [q], [qk])
            CP("dve", qk.ap[:, 128:256], k.ap, [k], [qk])
            TT("dve", AB.ap, qk.ap, css.ap, ALU.mult, [qk, css], [AB])
            TT("dve", CD.ap, qk.ap, scs.ap, ALU.mult, [qk, scs], [CD])
            AB4 = AB.ap.rearrange("p (a b f) -> p a b f", a=2, b=2)
            CD4 = CD.ap.rearrange("p (a b f) -> p a b f", a=2, b=2)
            TT("dve", qkr.ap[:, :, 0:64], AB4[:, :, 0, :], AB4[:, :, 1, :], ALU.subtract, [AB], [qkr])
            TT("dve", qkr.ap[:, :, 64:128], CD4[:, :, 0, :], CD4[:, :, 1, :], ALU.add, [CD], [qkr])
            qP = qkr.ap[:, 0, :]
            kP = qkr.ap[:, 1, :]
            ACT(gr.ap, gr.ap, AF.Silu, [gr], [gr])
            oacc = A.tile(128); opart = A.tile(128)
            pcs = [A.tile(1024, F32, "p (d e) -> p d e", d=8) for _ in range(4)]
            tmps = [A.tile(1024, F32, "p (d e) -> p d e", d=8) for _ in range(4)]
            tmpsB = [A.tile(1024, F32, "p (d e) -> p d e", d=8) for _ in range(4)]
            def ld_ret(pi):
                S.dma("sp", pcs[pi % 4].ap, st_ret[l][:, pi * 8:pi * 8 + 8, :], writes=[pcs[pi % 4]])
            for pi in range(3):
                ld_ret(pi)
            for pi in range(16):
                sp_ = pcs[pi % 4]; tp = tmps[pi % 4]; tq = tmpsB[pi % 4]
                d0 = pi * 8
                if pi + 3 < 16:
                    ld_ret(pi + 3)
                TT("pool", tp.ap, sp_.ap, qP[:, d0:d0 + 8].unsqueeze(2).to_broadcast([128, 8, 128]), ALU.mult, [sp_, qkr], [tp])
                dst = oacc if pi == 0 else opart
                S.op("dve", lambda e, o=dst.ap, i=tp.ap.rearrange("p d e -> p e d"): e.tensor_reduce(out=o, in_=i, axis=AX.X, op=ALU.add), [tp], [dst])
                if pi > 0:
                    TT("dve", oacc.ap, oacc.ap, opart.ap, ALU.add, [opart], [oacc])
                TT("pool", tq.ap, kP[:, d0:d0 + 8].unsqueeze(2).to_broadcast([128, 8, 128]), vv.ap.unsqueeze(1).to_broadcast([128, 8, 128]), ALU.mult, [qkr, vv], [tq])
                STT("dve", sp_.ap, sp_.ap, gam_p.ap[:, 0:1], tq.ap, ALU.mult, ALU.add, [gam_p, tq], [sp_])
                S.dma("sp", ret_s[l][:, d0:d0 + 8, :], sp_.ap, reads=[sp_])
            qkd = A.tile(128); qks = A.tile(1)
            TT("dve", qkd.ap, qP, kP, ALU.mult, [qkr], [qkd])
            S.op("dve", lambda e: e.tensor_reduce(out=qks.ap[:, 0:1], in_=qkd.ap, axis=AX.X, op=ALU.add), [qkd], [qks])
            TS("dve", oacc.ap, oacc.ap, gam_p.ap[:, 0:1], None, ALU.mult, None, [gam_p], [oacc])
            STT("dve", oacc.ap, vv.ap, qks.ap[:, 0:1], oacc.ap, ALU.mult, ALU.add, [vv, qks], [oacc])
            stats = A.tile(6); mv = A.tile(2)
            S.op("dve", lambda e: e.bn_stats(out=stats.ap[:, 0:6], in_=oacc.ap), [oacc], [stats])
            S.op("dve", lambda e: e.bn_aggr(out=mv.ap[:, 0:2], in_=stats.ap[:, 0:6]), [stats], [mv])
            RSTD(rs.ap[:, 0:1], mv.ap[:, 1:2], 1.0, [mv], [rs], tmp)
            TS("dve", oacc.ap, oacc.ap, mv.ap[:, 0:1], rs.ap[:, 0:1], ALU.subtract, ALU.mult, [mv, rs], [oacc])
            TT("dve", oacc.ap, oacc.ap, gretP.ap, ALU.mult, [gretP], [oacc])
            ob = A.tile(64, BF16)
            TT("dve", ob.ap, oacc.ap, gr.ap, ALU.mult, [oacc, gr], [ob])
            TRS(7, [(pbf(7)[:, 0:128], ob.ap, identb.ap)], [ob, identb])
            CP("dve", mixTs.ap[:, 0:8, :], pbf(7)[:, 0:128].rearrange("p (k b) -> p k b", k=8), [], [PT[7], mixTs])
            A.release(m)
            m = A.mark()
            bg = ld(128, pjv(C_BG, 1024, 128), PJT[32:40])
            cg = ld(128, pjv(C_CG, 1024, 128), PJT[40:48])
            hh_ = ld(128, pjv(C_H, 1024, 128), PJT[48:56])
            gsc = ld(128, pjv(C_GSC, 1024, 128), PJT[56:64])
            wP = ldj(3, 128, 8, lambda j: sc_w[l][j].rearrange("(k c) -> k c", c=128).unsqueeze(1).to_broadcast([8, SB, 128]))
            bP = ld(128, sc_b[l].rearrange("(k c) -> k c", c=128).unsqueeze(1).to_broadcast([8, SB, 128]), [])
            buf = ldj(2, 128, 8, lambda j: st_sc[l][:, j, :].rearrange("b (k c) -> k b c", c=128))
            nst = A.tile(256, F32, "p (j c) -> p j c", j=2)
            acc = A.tile(128)
            TT("dve", nst.ap[:, 1, :], cg.ap, hh_.ap, ALU.mult, [cg, hh_], [nst])
            CP("dve", nst.ap[:, 0, :], buf.ap[:, 1, :], [buf], [nst])
            TT("dve", acc.ap, buf.ap[:, 0, :], wP.ap[:, 0, :], ALU.mult, [buf, wP], [acc])
            TT("dve", acc.ap, acc.ap, bP.ap, ALU.add, [bP], [acc])
            TT("dve", bP.ap, buf.ap[:, 1, :], wP.ap[:, 1, :], ALU.mult, [buf, wP], [bP])
            TT("dve", acc.ap, acc.ap, bP.ap, ALU.add, [bP], [acc])
            TT("dve", bP.ap, nst.ap[:, 1, :], wP.ap[:, 2, :], ALU.mult, [nst, wP], [bP])
            TT("dve", acc.ap, acc.ap, bP.ap, ALU.add, [bP], [acc])
            ACT(gsc.ap, gsc.ap, AF.Silu, [gsc], [gsc])
            TT("dve", acc.ap, acc.ap, gsc.ap, ALU.mult, [gsc], [acc])
            ob = A.tile(64, BF16)
            TT("dve", ob.ap, acc.ap, bg.ap, ALU.mult, [acc, bg], [ob])
            stj(lambda j: sc_s[l][:, j, :].rearrange("b (k c) -> k b c", c=128), nst, 2)
            TRS(7, [(pbf(7)[:, 0:128], ob.ap, identb.ap)], [ob, identb])
            CP("dve", mixTs.ap[:, 8:16, :], pbf(7)[:, 0:128].rearrange("p (k b) -> p k b", k=8), [], [PT[7], mixTs])
            A.release(m)
            m = A.mark()
            xr = ld(256, pjv(C_XBC, 2048, 256), PJT[80:96])
            zz = ld(256, pjv(C_Z, 2048, 256), PJT[64:80])
            wx = ldj(4, 256, 8, lambda j: ssm_w[l][j, 0:2048].rearrange("(k c) -> k c", c=256).unsqueeze(1).to_broadcast([8, SB, 256]))
            bx = ld(256, ssm_b[l][0:2048].rearrange("(k c) -> k c", c=256).unsqueeze(1).to_broadcast([8, SB, 256]), [])
            bufx = ldj(3, 256, 8, lambda j: st_sconv[l][:, j, 0:2048].rearrange("b (k c) -> k b c", c=256))
            nstx = A.tile(768, F32, "p (j c) -> p j c", j=3)
            xc = A.tile(256); t256 = A.tile(256)
            CP("dve", nstx.ap[:, 0:2, :], bufx.ap[:, 1:3, :], [bufx], [nstx])
            CP("dve", nstx.ap[:, 2, :], xr.ap, [xr], [nstx])
            stj(lambda j: sconv_s[l][:, j, 0:2048].rearrange("b (k c) -> k b c", c=256), nstx, 3)
            TT("dve", xc.ap, xr.ap, wx.ap[:, 3, :], ALU.mult, [xr, wx], [xc])
            TT("dve", xc.ap, xc.ap, bx.ap, ALU.add, [bx], [xc])
            for j in range(3):
                TT("dve", t256.ap, bufx.ap[:, j, :], wx.ap[:, j, :], ALU.mult, [bufx, wx], [t256])
                TT("dve", xc.ap, xc.ap, t256.ap, ALU.add, [t256], [xc])
            ACT(xc.ap, xc.ap, AF.Silu, [xc], [xc])
            ACT(zz.ap, zz.ap, AF.Silu, [zz], [zz])
            br = ld(256, pjv(C_XBC + 2048, 1024, 256), PJT[96:104], parts=64)
            wbc = ldj(4, 256, 4, lambda j: ssm_w[l][j, 2048:3072].rearrange("(k c) -> k c", c=256).unsqueeze(1).to_broadcast([4, SB, 256]), parts=64)
            bbc = ld(256, ssm_b[l][2048:3072].rearrange("(k c) -> k c", c=256).unsqueeze(1).to_broadcast([4, SB, 256]), [], parts=64)
            bufb = ldj(3, 256, 4, lambda j: st_sconv[l][:, j, 2048:3072].rearrange("b (k c) -> k b c", c=256), parts=64)
            nstb = A.tile(768, F32, "p (j c) -> p j c", j=3, parts=64)
            bc = A.tile(256, parts=64); tb256 = A.tile(256, parts=64)
            CP("dve", nstb.ap[:, 0:2, :], bufb.ap[:, 1:3, :], [bufb], [nstb])
            CP("dve", nstb.ap[:, 2, :], br.ap, [br], [nstb])
            stj(lambda j: sconv_s[l][:, j, 2048:3072].rearrange("b (k c) -> k b c", c=256), nstb, 3)
            TT("dve", bc.ap, br.ap, wbc.ap[:, 3, :], ALU.mult, [br, wbc], [bc])
            TT("dve", bc.ap, bc.ap, bbc.ap, ALU.add, [bbc], [bc])
            for j in range(3):
                TT("dve", tb256.ap, bufb.ap[:, j, :], wbc.ap[:, j, :], ALU.mult, [bufb, wbc], [tb256])
                TT("dve", bc.ap, bc.ap, tb256.ap, ALU.add, [tb256], [bc])
            ACT(bc.ap, bc.ap, AF.Silu, [bc], [bc])
            MM(4, [(PB[4][:, 0:128], sel4.ap[:, 0, :], bc.ap[:, 0:128], True, False),
                   (PB[4][:, 0:128], sel4.ap[:, 1, :], bc.ap[:, 128:256], False, True),
                   (PB[4][:, 128:256], sel4.ap[:, 2, :], bc.ap[:, 0:128], True, False),
                   (PB[4][:, 128:256], sel4.ap[:, 3, :], bc.ap[:, 128:256], False, True)], [sel4, bc])
            bcP = A.tile(256)
            CP("dve", bcP.ap, PB[4][:, 0:256], [], [PT[4], bcP])
            bmP = bcP.ap[:, 0:128]
            cmP = bcP.ap[:, 128:256]
            dtr = ld(4, pj[:, C_DT:C_DT + 32].rearrange("b (k c) -> k b c", c=4), [PJT[104]])
            dtbP = ld(4, dt_bias[l].rearrange("(k c) -> k c", c=4).unsqueeze(1).to_broadcast([8, SB, 4]), [])
            AP_ = ld(4, a_log[l].rearrange("(k c) -> k c", c=4).unsqueeze(1).to_broadcast([8, SB, 4]), [])
            DP = ld(4, d_skip[l].rearrange("(k c) -> k c", c=4).unsqueeze(1).to_broadcast([8, SB, 4]), [])
            x4 = A.tile(4); a4 = A.tile(4); l4 = A.tile(4); dtP = A.tile(4); eaP = A.tile(4); coef = A.tile(4); cbP = A.tile(1)
            TT("dve", x4.ap[:, 0:4], dtr.ap[:, 0:4], dtbP.ap[:, 0:4], ALU.add, [dtr, dtbP], [x4])
            STT("dve", a4.ap[:, 0:4], x4.ap[:, 0:4], -1.0, x4.ap[:, 0:4], ALU.mult, ALU.max, [x4], [a4])
            ACT(a4.ap[:, 0:4], a4.ap[:, 0:4], AF.Exp, [a4], [a4], scale=-1.0)
            ACT(l4.ap[:, 0:4], a4.ap[:, 0:4], AF.Ln, [a4], [l4], bias=1.0)
            STT("dve", dtP.ap[:, 0:4], x4.ap[:, 0:4], 0.0, l4.ap[:, 0:4], ALU.max, ALU.add, [x4, l4], [dtP])
            ACT(AP_.ap[:, 0:4], AP_.ap[:, 0:4], AF.Exp, [AP_], [AP_])
            TT("dve", a4.ap[:, 0:4], dtP.ap[:, 0:4], AP_.ap[:, 0:4], ALU.mult, [dtP, AP_], [a4])
            ACT(eaP.ap[:, 0:4], a4.ap[:, 0:4], AF.Exp, [a4], [eaP], scale=-1.0)
            tb128 = A.tile(128)
            TT("dve", tb128.ap, bmP, cmP, ALU.mult, [bcP], [tb128])
            S.op("dve", lambda e: e.tensor_reduce(out=cbP.ap[:, 0:1], in_=tb128.ap, axis=AX.X, op=ALU.add), [tb128], [cbP])
            STT("dve", coef.ap[:, 0:4], dtP.ap[:, 0:4], cbP.ap[:, 0:1], DP.ap[:, 0:4], ALU.mult, ALU.add, [dtP, cbP, DP], [coef])
            xdt = A.tile(256); yS = A.tile(256)
            xc3 = xc.ap.rearrange("p (h q) -> p h q", h=4)
            TT("dve", xdt.ap.rearrange("p (h q) -> p h q", h=4), xc3, dtP.ap[:, 0:4].unsqueeze(2).to_broadcast([128, 4, 64]), ALU.mult, [xc, dtP], [xdt])
            pcs = [A.tile(1024, F32, "p (q n) -> p q n", q=8) for _ in range(4)]
            tmps = [A.tile(1024, F32, "p (q n) -> p q n", q=8) for _ in range(4)]
            tmpsB = [A.tile(1024, F32, "p (q n) -> p q n", q=8) for _ in range(4)]
            def ld_ssm(pi):
                hl_, p0_ = pi // 8, (pi % 8) * 8
                S.dma("sp", pcs[pi % 4].ap, st_ssm[l][:, hl_, p0_:p0_ + 8, :], writes=[pcs[pi % 4]])
            for pi in range(3):
                ld_ssm(pi)
            for pi in range(32):
                hl, p0 = pi // 8, (pi % 8) * 8
                sp_ = pcs[pi % 4]; tp = tmps[pi % 4]; tq = tmpsB[pi % 4]
                if pi + 3 < 32:
                    ld_ssm(pi + 3)
                TT("pool", tp.ap, sp_.ap, cmP.unsqueeze(1).to_broadcast([128, 8, 128]), ALU.mult, [sp_, bcP], [tp])
                S.op("dve", lambda e, o=yS.ap[:, hl * 64 + p0:hl * 64 + p0 + 8], i=tp.ap: e.tensor_reduce(out=o, in_=i, axis=AX.X, op=ALU.add), [tp], [yS])
                TT("pool", tq.ap, xdt.ap[:, hl * 64 + p0:hl * 64 + p0 + 8].unsqueeze(2).to_broadcast([128, 8, 128]),
                   bmP.unsqueeze(1).to_broadcast([128, 8, 128]), ALU.mult, [xdt, bcP], [tq])
                STT("dve", sp_.ap, sp_.ap, eaP.ap[:, hl:hl + 1], tq.ap, ALU.mult, ALU.add, [eaP, tq], [sp_])
                S.dma("sp", ssm_s[l][:, hl, p0:p0 + 8, :], sp_.ap, reads=[sp_])
            y = A.tile(256)
            TT("dve", y.ap.rearrange("p (h q) -> p h q", h=4), xc3, coef.ap[:, 0:4].unsqueeze(2).to_broadcast([128, 4, 64]), ALU.mult, [xc, coef], [y])
            TT("dve", yS.ap.rearrange("p (h q) -> p h q", h=4), yS.ap.rearrange("p (h q) -> p h q", h=4),
               eaP.ap[:, 0:4].unsqueeze(2).to_broadcast([128, 4, 64]), ALU.mult, [eaP], [yS])
            TT("dve", y.ap, y.ap, yS.ap, ALU.add, [yS], [y])
            TT("dve", y.ap, y.ap, zz.ap, ALU.mult, [zz], [y])
            ssq = A.tile(1)
            ACT(t256.ap, y.ap, AF.Square, [y], [t256, ssq], accum=ssq.ap[:, 0:1])
            MM(4, [(PB[4][:, 0:1], selb.ap, ssq.ap[:, 0:1], True, True)], [selb, ssq])
            tot = A.tile(1)
            CP("dve", tot.ap[:, 0:1], PB[4][:, 0:1], [], [PT[4], tot])
            RSTD(rs.ap[:, 0:1], tot.ap[:, 0:1], 1.0 / 2048, [tot], [rs], tmp)
            gsP = ld(256, ssm_norm[l].rearrange("(k c) -> k c", c=256).unsqueeze(1).to_broadcast([8, SB, 256]), [])
            ob2 = A.tile(128, BF16)
            STT("dve", ob2.ap, y.ap, rs.ap[:, 0:1], gsP.ap, ALU.mult, ALU.mult, [y, rs, gsP], [ob2])
            TRS(7, [(pbf(7)[:, r * 128:(r + 1) * 128], ob2.ap[:, r * 128:(r + 1) * 128], identb.ap) for r in range(2)], [ob2, identb])
            mv_ = mixTs.ap[:, 16:32, :].rearrange("p (k r) b -> p r k b", r=2)
            for r in range(2):
                CP("dve", mv_[:, r, :, :], pbf(7)[:, r * 128:(r + 1) * 128].rearrange("p (k b) -> p k b", k=8), [], [PT[7], mixTs])
            A.release(m)

        def write_prompt_states(l):
            m = A.mark()
            S.dma("sp", ret_p[l].rearrange("h d e -> d h e"), S_ret.ap, reads=[S_ret])
            for j in range(2):
                S.dma("sp", sc_p[l][j].rearrange("(c p) -> p c", p=128), Ucar.ap[:, :, j], reads=[Ucar], allow_slow_non_contiguous=True)
            for j in range(3):
                S.dma("sp", sconv_p[l][j].rearrange("(c p) -> p c", p=128), XBcar.ap[:, :, j], reads=[XBcar], allow_slow_non_contiguous=True)
            so = [A.tile(512), A.tile(512)]
            for q in range(4):
                TRS(q % 2, [(PB[q % 2][:, k * 128:(k + 1) * 128], sT.ap[:, (q * 4 + k) * 128:(q * 4 + k + 1) * 128], identf.ap) for k in range(4)], [sT, identf])
                CP("dve", so[q % 2].ap, PB[q % 2][:, 0:512], [], [PT[q % 2], so[q % 2]])
                S.dma("sp", ssm_p[l][q * 8:(q + 1) * 8].rearrange("(k a) p n -> (a p) k n", a=2), so[q % 2].ap.rearrange("x (k n) -> x k n", k=4), reads=[so[q % 2]])
            A.release(m)

        for l in range(L):
            load_layer_params(l)
            for seg in range(nseg):
                ds = (seg == 0)
                if seg == 0:
                    build_hT(l, seg)
                if ds:
                    build_hTs(l)
                mret_, units, stage_ = ret_phase(l, seg, ds)
                sc_phase(l, seg, ds, units, stage_)
                A.release(mret_)
                ssd_phase(l, seg, ds)
                if ds:
                    decode_phase(l)
                out_phase(l, seg, ds, (lambda l=l, seg=seg: build_hT(l, seg + 1)) if seg + 1 < nseg else None)
            write_prompt_states(l)
        S.emit(st)
        build_program.stats = dict(ops=len(S.ops), waits=S.nwaits, arena_peak=A.peak)
    return nc


_CACHE = {}


def _run(inputs, depth, nseg):
    key = (depth, nseg)
    if key not in _CACHE:
        _CACHE[key] = build_program(depth, nseg)
    nc = _CACHE[key]
    seqlen = nseg * SEGT
    f = lambda a: np.ascontiguousarray(np.asarray(a, dtype=np.float32))
    consts = _consts(seqlen)
    L = depth
    shared = {k: f(inputs[k]) for k in ("w_in", "w_out", "norm_pre", "norm_post", "ret_norm", "sc_conv_w", "sc_conv_b",
                                        "ssm_conv_w", "ssm_conv_b", "ssm_dt_bias", "ssm_a_log", "ssm_d", "ssm_norm")}
    for k, v in consts.items():
        shared["c_" + k] = f(v)
    x_prompt = np.asarray(inputs["x_prompt"], np.float32)
    x_sample = np.asarray(inputs["x_sample"], np.float32)
    s_ret = np.asarray(inputs["state_ret"], np.float32)
    s_sc = np.asarray(inputs["state_sconv"], np.float32)
    s_sconv = np.asarray(inputs["state_ssm_conv"], np.float32)
    s_ssm = np.asarray(inputs["state_ssm"], np.float32)
    in_maps = []
    for c in range(8):
        b0 = c * SB
        d = dict(shared)
        d["xp"] = f(x_prompt[c % BATCH])
        d["xs"] = f(x_sample[b0:b0 + SB, 0, :])
        d["st_ret"] = f(s_ret[:, b0:b0 + SB].transpose(0, 2, 1, 3, 4).reshape(L, 128, 128, 128))
        d["st_sc"] = f(s_sc[:, b0:b0 + SB])
        d["st_sconv"] = f(s_sconv[:, b0:b0 + SB])
        d["st_ssm"] = f(s_ssm[:, b0:b0 + SB].reshape(L, SB, 8, 4, 64, 128).transpose(0, 2, 1, 3, 4, 5).reshape(L, 128, 4, 64, 128))
        in_maps.append(d)
    res = run_bass_kernel_spmd(nc, in_maps, core_ids=list(range(8)))
    R = res.results
    yp = np.stack([R[b]["yp"] for b in range(BATCH)])
    ys = np.concatenate([R[c]["ys"] for c in range(8)])[:, None, :]
    ret_p = np.stack([R[b]["ret_p"] for b in range(BATCH)], axis=1)
    sc_p = np.stack([R[b]["sc_p"] for b in range(BATCH)], axis=1)
    sconv_p = np.stack([R[b]["sconv_p"] for b in range(BATCH)], axis=1)
    ssm_p = np.stack([R[b]["ssm_p"] for b in range(BATCH)], axis=1)
    ret_s = np.concatenate([R[c]["ret_s"].reshape(L, 8, SB, 128, 128).transpose(0, 2, 1, 3, 4) for c in range(8)], axis=1)
    sc_s = np.concatenate([R[c]["sc_s"] for c in range(8)], axis=1)
    sconv_s = np.concatenate([R[c]["sconv_s"] for c in range(8)], axis=1)
    ssm_s = np.concatenate([R[c]["ssm_s"].reshape(L, 8, SB, 4, 64, 128).transpose(0, 2, 1, 3, 4, 5).reshape(L, SB, 32, 64, 128)
                            for c in range(8)], axis=1)
    outs = (yp, ys, ret_p, sc_p, sconv_p, ssm_p, ret_s, sc_s, sconv_s, ssm_s)
    return tuple(np.ascontiguousarray(o, dtype=np.float32) for o in outs)


def kernel(**inputs):
    depth = int(np.asarray(inputs["w_in"]).shape[0])
    nseg = int(np.asarray(inputs["x_prompt"]).shape[1]) // SEGT
    return _run(inputs, depth, nseg)
```

```python
import numpy as np
from contextlib import ExitStack
import concourse.bass as bass
import concourse.mybir as mybir
from concourse.bass_utils import run_bass_kernel_spmd

F32 = mybir.dt.float32
BF16 = mybir.dt.bfloat16
ALU = mybir.AluOpType
AF = mybir.ActivationFunctionType
AX = mybir.AxisListType

D_MODEL = 2048
DEPTH = 4
BATCH = 4
SEQ = 2048
DEC_B = 128
PAST_LEN = 16384
D_PROJ = 13344
EPS = 1e-6
NT = 4
SEGT = NT * 128
KC = 16
SB = 16
C_Q, C_K, C_V, C_GR, C_BG, C_CG, C_H, C_GSC, C_Z, C_XBC, C_DT = 0, 1024, 2048, 3072, 4096, 5120, 6144, 7168, 8192, 10240, 13312
GAM = [1.0 - 2.0 ** (-5.0 - h) for h in range(8)]


class Track:
    __slots__ = ("w", "r")

    def __init__(self):
        self.w = None
        self.r = []


class Tile:
    __slots__ = ("ap", "tracks", "arena", "off")

    def __init__(self, ap, tracks, arena=None, off=0):
        self.ap = ap
        self.tracks = tracks
        self.arena = arena
        self.off = off

    def sub(self, ap, woff, wlen):
        a = self.arena
        o = self.off + woff
        return Tile(ap, a.blocks[o // a.G:(o + wlen + a.G - 1) // a.G], a, o)


def _flat(lst):
    out = []
    for x in lst:
        if isinstance(x, Tile):
            out.extend(x.tracks)
        elif isinstance(x, Track):
            out.append(x)
        elif x is None:
            pass
        else:
            out.extend(_flat(x))
    return out


class Arena:
    G = 64

    def __init__(self, tensor, nwords):
        self.t = tensor
        self.n = nwords
        self.off = 0
        self.peak = 0
        self.blocks = [Track() for _ in range(nwords // self.G + 2)]

    def tile(self, words, dt=F32, pat=None, parts=128, **kw):
        req = words
        words = (words + self.G - 1) // self.G * self.G
        off = self.off
        self.off += words
        self.peak = max(self.peak, self.off)
        assert self.off <= self.n, "SBUF arena overflow %d > %d" % (self.off, self.n)
        ap = self.t[0:parts, off:off + req]
        if dt is BF16:
            ap = ap.bitcast(BF16)
        if pat:
            ap = ap.rearrange(pat, **kw)
        return Tile(ap, self.blocks[off // self.G:(off + words) // self.G], self, off)

    def mark(self):
        return self.off

    def release(self, m):
        self.off = m


class Sched:
    COMPUTE = ("pe", "act", "dve", "pool")

    def __init__(self, nc, nslots_sp=14, nslots_pool=8):
        self.nc = nc
        self.ops = []
        self.nslots = {"sp": nslots_sp, "pool": nslots_pool}
        self.dma_count = {"sp": 0, "pool": 0}

    def op(self, eng, fn, reads=(), writes=()):
        self.ops.append((eng, fn, _flat(reads), _flat(writes), None))

    def dma(self, queue, out, in_, reads=(), writes=(), **kw):
        n = self.dma_count[queue]
        self.dma_count[queue] += 1
        slot = (queue, n % self.nslots[queue])
        self.ops.append((queue, None, _flat(reads), _flat(writes), (out, in_, slot, kw)))

    def emit(self, stack):
        nc = self.nc
        ops = self.ops
        n = len(ops)
        deps_all = [None] * n
        slot_last = {}
        needs = set()
        for i, (eng, fn, R, W, dma) in enumerate(ops):
            deps = set()
            for t in R:
                if t.w is not None:
                    deps.add(t.w)
            for t in W:
                if t.w is not None:
                    deps.add(t.w)
                if t.r:
                    deps.update(t.r)
            if dma:
                s = dma[2]
                if s in slot_last:
                    deps.add(slot_last[s])
                slot_last[s] = i
            deps.discard(i)
            deps_all[i] = deps
            needs |= deps
            for t in R:
                t.r.append(i)
            for t in W:
                t.w = i
                t.r = []
        sems = {}
        for e in self.COMPUTE:
            sems[e] = stack.enter_context(nc.semaphore("s_" + e))
        for q, ns in self.nslots.items():
            for k in range(min(ns, self.dma_count[q])):
                sems[(q, k)] = stack.enter_context(nc.semaphore("d_%s%d" % (q, k)))
        cnt = {k: 0 for k in sems}
        sig = [None] * n
        for i, (eng, fn, R, W, dma) in enumerate(ops):
            if dma:
                k = dma[2]
                cnt[k] += 16
                sig[i] = (k, cnt[k])
            elif i in needs:
                cnt[eng] += 1
                sig[i] = (eng, cnt[eng])
        final = dict(cnt)
        streams = {e: [] for e in ("pe", "act", "dve", "pool", "sp")}
        seen = {e: {} for e in streams}
        nw = 0
        for i, (eng, fn, R, W, dma) in enumerate(ops):
            waits = {}
            se = seen[eng]
            for d in deps_all[i]:
                k, v = sig[d]
                if se.get(k, 0) < v and waits.get(k, 0) < v:
                    waits[k] = v
            for k, v in waits.items():
                se[k] = v
            nw += len(waits)
            streams[eng].append((i, waits))
        self.nwaits = nw
        self.final = final
        block = stack.enter_context(nc.Block())

        def run_stream(e, eng):
            for i, waits in streams[e]:
                for k, v in waits.items():
                    eng.wait_ge(sems[k], v)
                _, fn, _, _, dma = ops[i]
                if dma:
                    ins = eng.dma_start(out=dma[0], in_=dma[1], **dma[3])
                    ins.then_inc(sems[sig[i][0]], 16)
                else:
                    ins = fn(eng)
                    if sig[i] is not None:
                        ins.then_inc(sems[sig[i][0]], 1)
            for k, v in final.items():
                if isinstance(k, tuple) and k[0] == e and v > 0:
                    eng.wait_ge(sems[k], v)

        @block.sync
        def _(eng):
            run_stream("sp", eng)

        @block.tensor
        def _(eng):
            run_stream("pe", eng)

        @block.scalar
        def _(eng):
            run_stream("act", eng)

        @block.vector
        def _(eng):
            run_stream("dve", eng)

        @block.gpsimd
        def _(eng):
            run_stream("pool", eng)


def _rope_tables(pos, kscale):
    half = 64
    inv = (np.float32(10000.0) ** (-np.arange(half, dtype=np.float32) / np.float32(half))).astype(np.float32)
    ang = (pos.astype(np.float32)[:, None] * inv[None, :]).astype(np.float32)
    cos = np.cos(ang.astype(np.float64))
    sin = np.sin(ang.astype(np.float64))
    n = len(pos)
    cs = np.zeros((n, 2, 2, 64), np.float64)
    sc = np.zeros((n, 2, 2, 64), np.float64)
    cs[:, 0, 0], cs[:, 0, 1] = cos, sin
    sc[:, 0, 0], sc[:, 0, 1] = sin, cos
    cs[:, 1, 0], cs[:, 1, 1] = cos * kscale, sin * kscale
    sc[:, 1, 0], sc[:, 1, 1] = sin * kscale, cos * kscale
    return cs.reshape(n, 256).astype(np.float32), sc.reshape(n, 256).astype(np.float32)


def _consts(seqlen):
    c = {}
    c["ident"] = np.eye(128, dtype=np.float32)
    j = np.arange(128)[:, None]
    i = np.arange(128)[None, :]
    c["tri"] = (j <= i).astype(np.float32)
    c["ones"] = np.ones((128, 128), np.float32)
    c["maskneg"] = np.where(i >= j, 0.0, -30000.0).astype(np.float32)
    g = np.array(GAM, np.float64)
    m = np.zeros((128, 8, 128), np.float64)
    for h in range(8):
        m[:, h, :] = np.where(i >= j, g[h] ** (-(j + 1.0)), 0.0)
    c["mret"] = m.astype(np.float32)
    c["qdec"] = (g[None, :] ** (np.arange(128)[:, None] + 1.0)).astype(np.float32)
    c["kdec"] = (g[None, :] ** (127.0 - np.arange(128)[:, None])).astype(np.float32)
    c["cdec"] = np.broadcast_to((g ** 128.0)[None, :], (128, 8)).astype(np.float32).copy()
    ks = 128.0 ** -0.5
    c["cs_p"], c["sc_p"] = _rope_tables(np.arange(seqlen), ks)
    cs_s, sc_s = _rope_tables(np.array([PAST_LEN]), ks)
    c["cs_s"] = np.broadcast_to(cs_s, (128, 256)).copy()
    c["sc_s"] = np.broadcast_to(sc_s, (128, 256)).copy()
    c["gam_p"] = np.repeat(np.array(GAM, np.float32), SB)[:, None].copy()
    p = np.arange(128)
    c["selb"] = (p[:, None] % SB == p[None, :] % SB).astype(np.float32)
    sel = np.zeros((4, 64, 128), np.float32)
    for hh in range(8):
        gidx = hh // 2
        for b in range(SB):
            for which, base in ((0, 0), (1, 2)):
                blk = base + gidx // 2
                sel[which * 2 + (gidx % 2), blk * SB + b, hh * SB + b] = 1.0
    c["sel4"] = sel
    return c


def build_program(depth=DEPTH, nseg=SEQ // SEGT, debug=False):
    L = depth
    seqlen = nseg * SEGT
    nc = bass.Bass("TRN2", target_bir_lowering=False)

    def din(name, shape):
        return nc.dram_tensor(name, list(shape), F32, kind="ExternalInput").ap()

    def dout(name, shape):
        return nc.dram_tensor(name, list(shape), F32, kind="ExternalOutput").ap()

    xp = din("xp", [seqlen, D_MODEL])
    xs = din("xs", [SB, D_MODEL])
    st_ret = din("st_ret", [L, 128, 128, 128])
    st_sc = din("st_sc", [L, SB, 2, 1024])
    st_sconv = din("st_sconv", [L, SB, 3, 3072])
    st_ssm = din("st_ssm", [L, 128, 4, 64, 128])
    w_in = din("w_in", [L, D_MODEL, D_PROJ])
    w_out = din("w_out", [L, 4096, D_MODEL])
    norm_pre = din("norm_pre", [L, 2048])
    norm_post = din("norm_post", [L, 2048])
    ret_norm = din("ret_norm", [L, 1024])
    sc_w = din("sc_conv_w", [L, 3, 1024])
    sc_b = din("sc_conv_b", [L, 1024])
    ssm_w = din("ssm_conv_w", [L, 4, 3072])
    ssm_b = din("ssm_conv_b", [L, 3072])
    dt_bias = din("ssm_dt_bias", [L, 32])
    a_log = din("ssm_a_log", [L, 32])
    d_skip = din("ssm_d", [L, 32])
    ssm_norm = din("ssm_norm", [L, 2048])
    cn = {}
    for name, shp in (("ident", [128, 128]), ("tri", [128, 128]), ("ones", [128, 128]), ("maskneg", [128, 128]),
                      ("mret", [128, 8, 128]), ("qdec", [128, 8]), ("kdec", [128, 8]), ("cdec", [128, 8]),
                      ("cs_p", [seqlen, 256]), ("sc_p", [seqlen, 256]), ("cs_s", [128, 256]), ("sc_s", [128, 256]),
                      ("gam_p", [128, 1]), ("selb", [128, 128]), ("sel4", [4, 64, 128])):
        cn[name] = din("c_" + name, shp)

    yp = dout("yp", [seqlen, D_MODEL])
    ys = dout("ys", [SB, D_MODEL])
    ret_p = dout("ret_p", [L, 8, 128, 128])
    sc_p = dout("sc_p", [L, 2, 1024])
    sconv_p = dout("sconv_p", [L, 3, 3072])
    ssm_p = dout("ssm_p", [L, 32, 64, 128])
    ret_s = dout("ret_s", [L, 128, 128, 128])
    sc_s = dout("sc_s", [L, SB, 2, 1024])
    sconv_s = dout("sconv_s", [L, SB, 3, 3072])
    ssm_s = dout("ssm_s", [L, 128, 4, 64, 128])

    xres = nc.dram_tensor("xres", [seqlen, D_MODEL], F32).ap()
    xsres = nc.dram_tensor("xsres", [SB, D_MODEL], F32).ap()
    pj = nc.dram_tensor("pj", [SB, D_PROJ + 96], F32).ap()

    T_xres = [Track() for _ in range(nseg * NT)]
    T_xsres = Track()
    T_pj = [Track() for _ in range(106)]
    T_out = Track()

    with ExitStack() as st:
        S = Sched(nc)
        NW = 53200
        art = st.enter_context(nc.sbuf_tensor("arena", [128, NW], F32))
        A = Arena(art, NW)
        PB = [st.enter_context(nc.psum_tensor("pb%d" % i, [128, 512], F32)) for i in range(8)]
        PT = [Track() for _ in range(8)]

        def pbf(i):
            return PB[i][:, :].bitcast(BF16)

        def TT(eng, out, in0, in1, op, R, W):
            S.op(eng, lambda e: e.tensor_tensor(out=out, in0=in0, in1=in1, op=op), R, W)

        def STT(eng, out, in0, scalar, in1, op0, op1, R, W):
            S.op(eng, lambda e: e.scalar_tensor_tensor(out=out, in0=in0, scalar=scalar, in1=in1, op0=op0, op1=op1), R, W)

        def TS(eng, out, in0, s1, s2, op0, op1, R, W):
            if s2 is None:
                S.op(eng, lambda e: e.tensor_scalar(out=out, in0=in0, scalar1=s1, scalar2=None, op0=op0), R, W)
            else:
                S.op(eng, lambda e: e.tensor_scalar(out=out, in0=in0, scalar1=s1, scalar2=s2, op0=op0, op1=op1), R, W)

        def ACT(out, in_, func, R, W, bias=None, scale=None, accum=None):
            kw = {}
            if bias is not None:
                kw["bias"] = bias
            if scale is not None:
                kw["scale"] = scale
            if accum is not None:
                kw["accum_out"] = accum
            S.op("act", lambda e: e.activation(out=out, in_=in_, func=func, **kw), R, W)

        def CP(eng, out, in_, R, W):
            if eng == "act":
                S.op("act", lambda e: e.activation(out=out, in_=in_, func=AF.Copy), R, W)
            else:
                S.op(eng, lambda e: e.tensor_copy(out=out, in_=in_), R, W)

        def MSET(eng, out, val, W):
            S.op(eng, lambda e: e.memset(out, val), (), W)

        def MM(bank, mms, R, W=()):
            def fn(e):
                ins = None
                for (o, l, r, s0, s1) in mms:
                    ins = e.matmul(o, lhsT=l, rhs=r, start=s0, stop=s1)
                return ins
            S.op("pe", fn, R, [PT[bank]] + list(W))

        def MMACC(bank, out, pairs, R):
            n = len(pairs)
            MM(bank, [(out, l, r, k == 0, k == n - 1) for k, (l, r) in enumerate(pairs)], R)

        def TRS(bank, items, R):
            def fn(e):
                ins = None
                for (o, i_, idn) in items:
                    ins = e.transpose(out=o, in_=i_, identity=idn)
                return ins
            S.op("pe", fn, R, [PT[bank]])

        def RSTD(out, in_, scale, R, W, tmp):
            TS("dve", tmp.ap, in_, scale, EPS, ALU.mult, ALU.add, R, [tmp])
            TT("pool", out, tmp.ap, mhalf.ap[0:tmp.ap.shape[0], 0:1], ALU.pow, [tmp, mhalf], W)

        hT = A.tile(KC * SEGT // 2, BF16, "p (c t) -> p c t", c=KC)
        mixT = A.tile(32 * SEGT // 2, BF16, "p (c t) -> p c t", c=32)
        WB = [A.tile(4096, BF16), A.tile(4096, BF16)]
        identb = A.tile(64, BF16)
        identf = A.tile(128)
        tri = A.tile(128)
        ones = A.tile(128)
        maskneg = A.tile(64, BF16)
        qdec = A.tile(8)
        kdec = A.tile(8)
        mhalf = A.tile(8)
        cdec = A.tile(8)
        gpreT = A.tile(16)
        gretT = A.tile(8)
        gssmT = A.tile(16)
        scwT = A.tile(24, F32, "p (c j) -> p c j", j=3)
        scbT = A.tile(8)
        ssmwT = A.tile(96, F32, "p (c j) -> p c j", j=4)
        ssmbT = A.tile(24)
        dtb_bc = A.tile(32)
        A_bc = A.tile(32)
        D_bc = A.tile(32)
        S_ret = A.tile(1024, F32, "p (h e) -> p h e", h=8)
        S_bf = A.tile(512, BF16, "p (h e) -> p h e", h=8)
        sT = A.tile(2048)
        sT_bf = A.tile(1024, BF16)
        Ucar = A.tile(16, F32, "p (c j) -> p c j", j=2)
        XBcar = A.tile(72, F32, "p (c j) -> p c j", j=3)
        hTs = A.tile(KC * SB // 2, BF16, "p (c t) -> p c t", c=KC)
        mixTs = A.tile(32 * SB // 2, BF16, "p (c t) -> p c t", c=32)
        gam_p = A.tile(1)
        selb = A.tile(128)
        sel4 = A.tile(512, F32, "p (s c) -> p s c", s=4, parts=64)
        wpar = [0]

        def wb_next():
            w = WB[wpar[0]]
            wpar[0] ^= 1
            return w

        S.dma("pool", identb.ap, cn["ident"], writes=[identb])
        S.dma("sp", identf.ap, cn["ident"], writes=[identf])
        S.dma("sp", tri.ap, cn["tri"], writes=[tri])
        S.dma("sp", ones.ap, cn["ones"], writes=[ones])
        S.dma("pool", maskneg.ap, cn["maskneg"], writes=[maskneg])
        S.dma("sp", qdec.ap[:, 0:8], cn["qdec"], writes=[qdec])
        S.dma("sp", kdec.ap[:, 0:8], cn["kdec"], writes=[kdec])
        S.dma("sp", cdec.ap[:, 0:8], cn["cdec"], writes=[cdec])
        S.dma("sp", gam_p.ap[:, 0:1], cn["gam_p"], writes=[gam_p])
        S.dma("sp", selb.ap, cn["selb"], writes=[selb])
        S.dma("sp", sel4.ap, cn["sel4"].rearrange("s p c -> p s c"), writes=[sel4])
        MSET("dve", mhalf.ap, -0.5, [mhalf])

        def load_layer_params(l):
            S.dma("sp", gpreT.ap[:, 0:16], norm_pre[l].rearrange("(c p) -> p c", p=128), writes=[gpreT], allow_slow_non_contiguous=True)
            S.dma("sp", gretT.ap[:, 0:8], ret_norm[l].rearrange("(c p) -> p c", p=128), writes=[gretT], allow_slow_non_contiguous=True)
            S.dma("sp", gssmT.ap[:, 0:16], ssm_norm[l].rearrange("(c p) -> p c", p=128), writes=[gssmT], allow_slow_non_contiguous=True)
            for j in range(3):
                S.dma("sp", scwT.ap[:, :, j], sc_w[l][j].rearrange("(c p) -> p c", p=128), writes=[scwT], allow_slow_non_contiguous=True)
            S.dma("sp", scbT.ap[:, 0:8], sc_b[l].rearrange("(c p) -> p c", p=128), writes=[scbT], allow_slow_non_contiguous=True)
            for j in range(4):
                S.dma("sp", ssmwT.ap[:, :, j], ssm_w[l][j].rearrange("(c p) -> p c", p=128), writes=[ssmwT], allow_slow_non_contiguous=True)
            S.dma("sp", ssmbT.ap[:, 0:24], ssm_b[l].rearrange("(c p) -> p c", p=128), writes=[ssmbT], allow_slow_non_contiguous=True)
            S.dma("sp", dtb_bc.ap[:, 0:32], dt_bias[l:l + 1, :].partition_broadcast(128), writes=[dtb_bc])
            S.dma("sp", A_bc.ap[:, 0:32], a_log[l:l + 1, :].partition_broadcast(128), writes=[A_bc])
            S.dma("sp", D_bc.ap[:, 0:32], d_skip[l:l + 1, :].partition_broadcast(128), writes=[D_bc])
            ACT(A_bc.ap[:, 0:32], A_bc.ap[:, 0:32], AF.Exp, [A_bc], [A_bc])
            TS("dve", A_bc.ap[:, 0:32], A_bc.ap[:, 0:32], -1.0, None, ALU.mult, None, [A_bc], [A_bc])
            MSET("dve", S_ret.ap, 0.0, [S_ret])
            MSET("dve", S_bf.ap, 0.0, [S_bf])
            MSET("dve", sT.ap, 0.0, [sT])
            MSET("dve", sT_bf.ap, 0.0, [sT_bf])
            MSET("dve", Ucar.ap, 0.0, [Ucar])
            MSET("dve", XBcar.ap, 0.0, [XBcar])

        def build_hT(l, seg):
            m = A.mark()
            xt = [A.tile(2048), A.tile(2048)]
            xn = [A.tile(1024, BF16), A.tile(1024, BF16)]
            ssq = [A.tile(1), A.tile(1)]
            rs = [A.tile(1), A.tile(1)]
            tmp = [A.tile(1), A.tile(1)]
            for t in range(NT):
                p = t % 2
                r0 = seg * SEGT + t * 128
                if l == 0:
                    S.dma("sp", xt[p].ap, xp[r0:r0 + 128, :], writes=[xt[p]])
                else:
                    S.dma("sp", xt[p].ap, xres[r0:r0 + 128, :], reads=[T_xres[seg * NT + t]], writes=[xt[p]])
                ACT(xn[p].ap, xt[p].ap, AF.Square, [xt[p]], [xn[p], ssq[p]], accum=ssq[p].ap[:, 0:1])
                RSTD(rs[p].ap[:, 0:1], ssq[p].ap[:, 0:1], 1.0 / D_MODEL, [ssq[p]], [rs[p]], tmp[p])
                ACT(xn[p].ap, xt[p].ap, AF.Copy, [xt[p], rs[p]], [xn[p]], scale=rs[p].ap[:, 0:1])
                for q in range(4):
                    bank = 6 + (q % 2)
                    TRS(bank, [(pbf(bank)[:, k * 128:(k + 1) * 128], xn[p].ap[:, (q * 4 + k) * 128:(q * 4 + k + 1) * 128], identb.ap)
                               for k in range(4)], [xn[p], identb])
                    for k in range(4):
                        c = q * 4 + k
                        dst = hT.sub(hT.ap[:, c, t * 128:(t + 1) * 128], (c * SEGT + t * 128) // 2, 64)
                        if k % 2 == 0:
                            ACT(dst.ap, pbf(bank)[:, k * 128:(k + 1) * 128], AF.Copy, [gpreT], [PT[bank], dst], scale=gpreT.ap[:, c:c + 1])
                        else:
                            TS("dve", dst.ap, pbf(bank)[:, k * 128:(k + 1) * 128], gpreT.ap[:, c:c + 1], None, ALU.mult, None,
                               [gpreT], [PT[bank], dst])
            A.release(m)

        def build_hTs(l):
            m = A.mark()
            xt = A.tile(2048, parts=SB)
            xn = A.tile(1024, BF16, parts=SB)
            ssq = A.tile(1, parts=SB)
            rs = A.tile(1, parts=SB)
            tmp = A.tile(1, parts=SB)
            if l == 0:
                S.dma("sp", xt.ap, xs, writes=[xt])
            else:
                S.dma("sp", xt.ap, xsres, reads=[T_xsres], writes=[xt])
            ACT(xn.ap, xt.ap, AF.Square, [xt], [xn, ssq], accum=ssq.ap[:, 0:1])
            RSTD(rs.ap[:, 0:1], ssq.ap[:, 0:1], 1.0 / D_MODEL, [ssq], [rs], tmp)
            ACT(xn.ap, xt.ap, AF.Copy, [xt, rs], [xn], scale=rs.ap[:, 0:1])
            for q in range(4):
                bank = 6 + (q % 2)
                TRS(bank, [(pbf(bank)[:, k * SB:(k + 1) * SB], xn.ap[:, (q * 4 + k) * 128:(q * 4 + k + 1) * 128], identb.ap[0:SB, 0:SB])
                           for k in range(4)], [xn, identb])
                for k in range(4):
                    c = q * 4 + k
                    ACT(hTs.ap[:, c, :], pbf(bank)[:, k * SB:(k + 1) * SB], AF.Copy, [gpreT], [PT[bank], hTs], scale=gpreT.ap[:, c:c + 1])
            A.release(m)

        def load_win_block(l, wb, cols):
            v = wb.ap.rearrange("p (c n) -> p c n", c=KC)
            src = w_in[l].rearrange("(c p) n -> p c n", p=128)
            o = 0
            for (c0, ncol) in cols:
                S.dma("pool", v[:, :, o:o + ncol], src[:, :, c0:c0 + ncol], writes=[wb])
                o += ncol
            return v

        PRE = {}

        def win_block(key, l, cols):
            if key in PRE:
                return PRE.pop(key)
            wb = wb_next()
            return wb, load_win_block(l, wb, cols)

        def prefetch_win(key, l, cols):
            wb = wb_next()
            PRE[key] = (wb, load_win_block(l, wb, cols))

        def out_block(key, l, ob):
            if key in PRE:
                return PRE.pop(key)
            wb = wb_next()
            v = wb.ap.rearrange("p (c n) -> p c n", c=32)
            S.dma("pool", v, w_out[l].rearrange("(c p) n -> p c n", p=128)[:, :, ob * 256:(ob + 1) * 256], writes=[wb])
            return wb, v

        def prefetch_out(key, l, ob):
            PRE[key] = out_block(("none",), l, ob)

        def sc_cols(cc):
            return [(C_BG + cc * 128, 128), (C_CG + cc * 128, 128), (C_H + cc * 128, 128), (C_GSC + cc * 128, 128)]

        def sample_proj(wb, v, cols, stage):
            ntot = sum(n for _, n in cols)
            MMACC(5, PB[5][0:SB, 0:ntot], [(hTs.ap[:, c, :], v[:, c, 0:ntot]) for c in range(KC)], [hTs, wb])
            CP("act", stage.ap[:, 0:ntot], PB[5][0:SB, 0:ntot], [], [PT[5], stage])
            o = 0
            for (c0, ncol) in cols:
                S.dma("sp", pj[:, c0:c0 + ncol], stage.ap[:, o:o + ncol], reads=[stage],
                      writes=T_pj[c0 // 128:(c0 + ncol + 127) // 128])
                o += ncol

        def ret_phase(l, seg, do_sample):
            m = A.mark()
            CS = A.tile(NT * 256, F32, "p (t x) -> p t x", t=NT)
            SC = A.tile(NT * 256, F32, "p (t x) -> p t x", t=NT)
            mret = A.tile(1024, F32, "p (h i) -> p h i", h=8)
            gret_bc = A.tile(1024)
            r0 = seg * SEGT
            S.dma("sp", CS.ap, cn["cs_p"][r0:r0 + SEGT, :].rearrange("(t p) x -> p t x", p=128), writes=[CS])
            S.dma("sp", SC.ap, cn["sc_p"][r0:r0 + SEGT, :].rearrange("(t p) x -> p t x", p=128), writes=[SC])
            S.dma("sp", mret.ap, cn["mret"], writes=[mret])
            S.dma("sp", gret_bc.ap, ret_norm[l:l + 1, :].partition_broadcast(128), writes=[gret_bc])
            stage = A.tile(512, parts=SB) if do_sample else None
            H = []
            for h in range(8):
                H.append(dict(
                    qkT=A.tile(2 * SEGT // 2, BF16, "p (a t) -> p a t", a=2),
                    qkr=A.tile(NT * 256 // 2, BF16, "p (t a d) -> p t a d", t=NT, a=2),
                    vbf=A.tile(NT * 128 // 2, BF16, "p (t e) -> p t e", t=NT),
                    vdec=A.tile(NT * 128 // 2, BF16, "p (t e) -> p t e", t=NT)))
            GS = [A.tile(NT * 512 // 2, BF16, "p (t i e) -> p t i e", t=NT, i=4) for _ in range(2)]
            ABCD_t = [A.tile(512) for _ in range(2)]
            ABCD = [dict(AB=t_.sub(t_.ap[:, 0:256], 0, 256), CD=t_.sub(t_.ap[:, 256:512], 256, 256)) for t_ in ABCD_t]
            pend_tr = [None]
            for h in range(8):
                Hh = H[h]
                wb = wb_next()
                cols = [(C_Q + h * 128, 128), (C_K + h * 128, 128), (C_V + h * 128, 128), (C_GR + h * 128, 128)]
                v = load_win_block(l, wb, cols)
                for t in range(NT):
                    bank = t % 4
                    P = PB[bank]
                    B = ABCD[t % 2]
                    MMACC(bank, P[:, 0:512], [(hT.ap[:, c, t * 128:(t + 1) * 128], v[:, c, :]) for c in range(KC)], [hT, wb])
                    if pend_tr[0] is not None:
                        pend_tr[0]()
                        pend_tr[0] = None
                    P4 = P[:, 0:256].rearrange("p (a b f) -> p a b f", a=2, b=2)
                    AB4 = B["AB"].ap.rearrange("p (a b f) -> p a b f", a=2, b=2)
                    CD4 = B["CD"].ap.rearrange("p (a b f) -> p a b f", a=2, b=2)
                    TT("dve", AB4, P4, CS.ap[:, t, :].rearrange("p (a b f) -> p a b f", a=2, b=2), ALU.mult, [CS], [PT[bank], B["AB"]])
                    TT("dve", CD4, P4, SC.ap[:, t, :].rearrange("p (a b f) -> p a b f", a=2, b=2), ALU.mult, [SC], [PT[bank], B["CD"]])
                    TT("dve", Hh["qkr"].ap[:, t, :, 0:64], AB4[:, :, 0, :], AB4[:, :, 1, :], ALU.subtract, [B["AB"]], [Hh["qkr"]])
                    TT("dve", Hh["qkr"].ap[:, t, :, 64:128], CD4[:, :, 0, :], CD4[:, :, 1, :], ALU.add, [B["CD"]], [Hh["qkr"]])
                    ACT(Hh["vbf"].ap[:, t, :], P[:, 256:384], AF.Copy, [], [PT[bank], Hh["vbf"]])
                    ACT(Hh["vdec"].ap[:, t, :], P[:, 256:384], AF.Copy, [kdec], [PT[bank], Hh["vdec"]], scale=kdec.ap[:, h:h + 1])
                    ACT(GS[h // 4].ap[:, t, h % 4, :], P[:, 384:512], AF.Silu, [], [PT[bank], GS[h // 4]])
                    def tr_(Hh=Hh, t=t):
                        tb = 6 + (t % 2)
                        TRS(tb, [(pbf(tb)[:, a * 128:(a + 1) * 128], Hh["qkr"].ap[:, t, a, :], identb.ap) for a in range(2)], [Hh["qkr"], identb])
                        CP("act", Hh["qkT"].ap[:, :, t * 128:(t + 1) * 128], pbf(tb)[:, 0:256].rearrange("p (a i) -> p a i", a=2), [], [PT[tb], Hh["qkT"]])
                    pend_tr[0] = tr_
                if do_sample:
                    sample_proj(wb, v, cols, stage)
            if pend_tr[0] is not None:
                pend_tr[0]()
                pend_tr[0] = None
            Pm = [A.tile(256, BF16, "p (i j) -> p i j", i=4) for _ in range(2)]
            osb = [A.tile(512, F32, "p (i e) -> p i e", i=4) for _ in range(2)]
            sq = [Tile(t_.ap.rearrange("p (i e) -> p i e", i=4), t_.tracks, t_.arena, t_.off) for t_ in ABCD_t]
            og = [A.tile(256, BF16, "p (i e) -> p i e", i=4) for _ in range(2)]
            st4 = []
            for _ in range(2):
                t_ = A.tile(24)
                st4.append({nm: t_.sub(t_.ap[:, 4 * k_:4 * k_ + 4], 0, 24) for k_, nm in enumerate(("s1", "s2", "mean", "msq", "var", "rs"))})
            b4 = lambda ap: ap.rearrange("p (i e) -> p i e", i=4)

            def finalize(c, qd):
                sl = slice(c * 128, (c + 1) * 128)
                h0 = qd * 4
                bsc = 4 if qd == 0 else 7
                TRS(bsc, [(pbf(bsc)[:, i * 128:(i + 1) * 128], og[qd].ap[:, i, :], identb.ap) for i in range(4)], [og[qd], identb])
                dsts = [mixT.sub(None, ((h0 + i) * SEGT + c * 128) // 2, 64) for i in range(4)]
                CP("act", mixT.ap[:, h0:h0 + 4, sl], pbf(bsc)[:, 0:512].rearrange("p (i e) -> p i e", i=4), [], [PT[bsc]] + dsts)

            def unit(c, qd, prev):
                    if prev is not None:
                        finalize(*prev)
                    sl = slice(c * 128, (c + 1) * 128)
                    h0 = qd * 4
                    HQ = H[h0:h0 + 4]
                    bsc, bst, bo = (4 if qd == 0 else 7), 5, 6
                    MM(bsc, [(PB[bsc][:, i * 128:(i + 1) * 128], HQ[i]["qkT"].ap[:, 1, sl], HQ[i]["qkT"].ap[:, 0, sl], True, True) for i in range(4)],
                       [x["qkT"] for x in HQ])
                    TT("dve", Pm[qd].ap, b4(PB[bsc][:, 0:512]), mret.ap[:, h0:h0 + 4, :], ALU.mult, [mret], [PT[bsc], Pm[qd]])
                    MM(bst, [(PB[bst][:, i * 128:(i + 1) * 128], HQ[i]["qkr"].ap[:, c, 1, :], HQ[i]["vdec"].ap[:, c, :], True, True) for i in range(4)],
                       [x["qkr"] for x in HQ] + [x["vdec"] for x in HQ])
                    mms = []
                    for i in range(4):
                        o = PB[bo][:, i * 128:(i + 1) * 128]
                        mms.append((o, Pm[qd].ap[:, i, :], HQ[i]["vbf"].ap[:, c, :], True, False))
                        mms.append((o, HQ[i]["qkT"].ap[:, 0, sl], S_bf.ap[:, h0 + i, :], False, True))
                    MM(bo, mms, [Pm[qd], S_bf] + [x["vbf"] for x in HQ] + [x["qkT"] for x in HQ])
                    TT("dve", S_ret.ap[:, h0:h0 + 4, :], S_ret.ap[:, h0:h0 + 4, :], cdec.ap[:, h0:h0 + 4].unsqueeze(2).to_broadcast([128, 4, 128]), ALU.mult, [cdec], [S_ret])
                    TT("dve", S_ret.ap[:, h0:h0 + 4, :], S_ret.ap[:, h0:h0 + 4, :], b4(PB[bst][:, 0:512]), ALU.add, [], [PT[bst], S_ret])
                    CP("act", S_bf.ap[:, h0:h0 + 4, :], S_ret.ap[:, h0:h0 + 4, :], [S_ret], [S_bf])
                    s = st4[qd]
                    TT("dve", osb[qd].ap, b4(PB[bo][:, 0:512]), qdec.ap[:, h0:h0 + 4].unsqueeze(2).to_broadcast([128, 4, 128]), ALU.mult, [qdec], [PT[bo], osb[qd]])
                    S.op("dve", lambda e, o=s["s1"].ap, i_=osb[qd].ap: e.tensor_reduce(out=o, in_=i_, axis=AX.X, op=ALU.add), [osb[qd]], [s["s1"]])
                    ACT(sq[qd].ap, osb[qd].ap, AF.Square, [osb[qd]], [sq[qd]])
                    S.op("dve", lambda e, o=s["s2"].ap, i_=sq[qd].ap: e.tensor_reduce(out=o, in_=i_, axis=AX.X, op=ALU.add), [sq[qd]], [s["s2"]])
                    TS("dve", s["mean"].ap, s["s1"].ap, 1.0 / 128, None, ALU.mult, None, [s["s1"]], [s["mean"]])
                    TT("dve", s["msq"].ap, s["mean"].ap, s["mean"].ap, ALU.mult, [s["mean"]], [s["msq"]])
                    STT("dve", s["var"].ap, s["s2"].ap, 1.0 / 128, s["msq"].ap, ALU.mult, ALU.subtract, [s["s2"], s["msq"]], [s["var"]])
                    TS("dve", s["var"].ap, s["var"].ap, EPS, None, ALU.add, None, [], [s["var"]])
                    TT("pool", s["rs"].ap, s["var"].ap, mhalf.ap[:, 0:4], ALU.pow, [s["var"], mhalf], [s["rs"]])
                    TT("dve", osb[qd].ap, osb[qd].ap, s["mean"].ap.unsqueeze(2).to_broadcast([128, 4, 128]), ALU.subtract, [s["mean"]], [osb[qd]])
                    TT("dve", osb[qd].ap, osb[qd].ap, s["rs"].ap.unsqueeze(2).to_broadcast([128, 4, 128]), ALU.mult, [s["rs"]], [osb[qd]])
                    TT("dve", osb[qd].ap, osb[qd].ap, b4(gret_bc.ap[:, h0 * 128:(h0 + 4) * 128]), ALU.mult, [gret_bc], [osb[qd]])
                    TT("dve", og[qd].ap, osb[qd].ap, GS[qd].ap[:, c, :, :], ALU.mult, [osb[qd], GS[qd]], [og[qd]])

            items = [(c, qd) for c in range(NT) for qd in range(2)]
            units = []
            for k, (c, qd) in enumerate(items):
                prev = items[k - 1] if k > 0 else None
                units.append(lambda c=c, qd=qd, prev=prev: unit(c, qd, prev))
            units.append(lambda: finalize(*items[-1]))
            return m, units, stage

        def sc_phase(l, seg, do_sample, units, stage):
            m = A.mark()
            bufs = []
            for par in range(1):
                bufs.append(dict(cg=A.tile(512), U=A.tile(SEGT + 2), acc=A.tile(512), sg=A.tile(256, BF16), bgc=A.tile(256, BF16)))
            for cc in range(8):
                B = bufs[0]
                cols = sc_cols(cc)
                wb, v = win_block(("sc", cc), l, cols)
                if cc + 1 < 8:
                    prefetch_win(("sc", cc + 1), l, sc_cols(cc + 1))
                else:
                    prefetch_win(("xbc", 0), l, [(C_XBC, 512)])
                bk = [j for j in range(4)]
                for j in range(4):
                    MMACC(bk[j], PB[bk[j]][:, 0:SEGT], [(v[:, c, j * 128:(j + 1) * 128], hT.ap[:, c, :]) for c in range(KC)], [hT, wb])
                CP("act", B["cg"].ap, PB[bk[1]][:, 0:SEGT], [], [PT[bk[1]], B["cg"]])
                ACT(B["sg"].ap, PB[bk[3]][:, 0:SEGT], AF.Silu, [], [PT[bk[3]], B["sg"]])
                CP("act", B["bgc"].ap, PB[bk[0]][:, 0:SEGT], [], [PT[bk[0]], B["bgc"]])
                CP("dve", B["U"].ap[:, 0:2], Ucar.ap[:, cc, :], [Ucar], [B["U"]])
                TT("dve", B["U"].ap[:, 2:2 + SEGT], B["cg"].ap, PB[bk[2]][:, 0:SEGT], ALU.mult, [B["cg"]], [PT[bk[2]], B["U"]])
                if cc < len(units):
                    units[cc]()
                CP("dve", Ucar.ap[:, cc, :], B["U"].ap[:, SEGT:SEGT + 2], [B["U"]], [Ucar])
                TS("dve", B["acc"].ap, B["U"].ap[:, 0:SEGT], scwT.ap[:, cc, 0:1], scbT.ap[:, cc:cc + 1], ALU.mult, ALU.add, [B["U"], scwT, scbT], [B["acc"]])
                STT("dve", B["acc"].ap, B["U"].ap[:, 1:1 + SEGT], scwT.ap[:, cc, 1:2], B["acc"].ap, ALU.mult, ALU.add, [B["U"], scwT], [B["acc"]])
                STT("dve", B["acc"].ap, B["U"].ap[:, 2:2 + SEGT], scwT.ap[:, cc, 2:3], B["acc"].ap, ALU.mult, ALU.add, [B["U"], scwT], [B["acc"]])
                TT("dve", B["acc"].ap, B["acc"].ap, B["sg"].ap, ALU.mult, [B["sg"]], [B["acc"]])
                dst = mixT.sub(mixT.ap[:, 8 + cc, :], ((8 + cc) * SEGT) // 2, SEGT // 2)
                TT("dve", dst.ap, B["acc"].ap, B["bgc"].ap, ALU.mult, [B["acc"], B["bgc"]], [dst])
                if do_sample:
                    sample_proj(wb, v, cols, stage)
            for u in units[8:]:
                u()
            A.release(m)

        def ssd_phase(l, seg, do_sample):
            m = A.mark()
            stage = A.tile(512, parts=SB) if do_sample else None
            XS = A.tile(NT * 2048 // 2, BF16, "p (t x) -> p t x", t=NT)
            SZ = A.tile(NT * 2048 // 2, BF16, "p (t x) -> p t x", t=NT)
            BMT = A.tile(4 * SEGT // 2, BF16, "p (g t) -> p g t", g=4)
            CMT = A.tile(4 * SEGT // 2, BF16, "p (g t) -> p g t", g=4)
            BM = A.tile(NT * 512 // 2, BF16, "p (t g n) -> p t g n", t=NT, g=4)
            DT = A.tile(NT * 32, F32, "p (t h) -> p t h", t=NT)
            AA = A.tile(NT * 32, F32, "p (t h) -> p t h", t=NT)
            NACUM = A.tile(NT * 32, F32, "p (t h) -> p t h", t=NT)
            NAA = A.tile(NT * 32, F32, "p (t h) -> p t h", t=NT)
            EA = A.tile(NT * 32, F32, "p (t h) -> p t h", t=NT)
            WJ = A.tile(NT * 32, F32, "p (t h) -> p t h", t=NT)
            ELAST = A.tile(NT * 32, F32, "p (t h) -> p t h", t=NT)
            wdt = A.tile(KC * 32 // 2, BF16, "p (c n) -> p c n", c=KC)
            t32 = [A.tile(32), A.tile(32), A.tile(32)]
            S.dma("pool", wdt.ap, w_in[l].rearrange("(c p) n -> p c n", p=128)[:, :, C_DT:C_DT + 32], writes=[wdt])
            if do_sample:
                MMACC(5, PB[5][0:SB, 0:32], [(hTs.ap[:, c, :], wdt.ap[:, c, :]) for c in range(KC)], [hTs, wdt])
                CP("act", stage.ap[:, 0:32], PB[5][0:SB, 0:32], [], [PT[5], stage])
                S.dma("sp", pj[:, C_DT:C_DT + 32], stage.ap[:, 0:32], reads=[stage], writes=[T_pj[104]])
            for t in range(NT):
                tsl = slice(t * 128, (t + 1) * 128)
                MMACC(0, PB[0][:, 0:32], [(hT.ap[:, c, tsl], wdt.ap[:, c, :]) for c in range(KC)], [hT, wdt])
                xdt, ax, lg = t32
                TT("dve", xdt.ap[:, 0:32], PB[0][:, 0:32], dtb_bc.ap[:, 0:32], ALU.add, [dtb_bc], [PT[0], xdt])
                STT("dve", ax.ap[:, 0:32], xdt.ap[:, 0:32], -1.0, xdt.ap[:, 0:32], ALU.mult, ALU.max, [xdt], [ax])
                ACT(ax.ap[:, 0:32], ax.ap[:, 0:32], AF.Exp, [ax], [ax], scale=-1.0)
                ACT(lg.ap[:, 0:32], ax.ap[:, 0:32], AF.Ln, [ax], [lg], bias=1.0)
                STT("dve", DT.ap[:, t, :], xdt.ap[:, 0:32], 0.0, lg.ap[:, 0:32], ALU.max, ALU.add, [xdt, lg], [DT])
                TT("dve", AA.ap[:, t, :], DT.ap[:, t, :], A_bc.ap[:, 0:32], ALU.mult, [DT, A_bc], [AA])
                TS("dve", NAA.ap[:, t, :], AA.ap[:, t, :], -1.0, None, ALU.mult, None, [AA], [NAA])
                MM(1, [(PB[1][:, 0:32], tri.ap, AA.ap[:, t, :], True, True),
                       (PB[1][:, 32:64], ones.ap, AA.ap[:, t, :], True, True)], [tri, ones, AA])
                TS("dve", NACUM.ap[:, t, :], PB[1][:, 0:32], -1.0, None, ALU.mult, None, [], [PT[1], NACUM])
                ACT(EA.ap[:, t, :], PB[1][:, 0:32], AF.Exp, [], [PT[1], EA])
                ACT(ELAST.ap[:, t, :], PB[1][:, 32:64], AF.Exp, [], [PT[1], ELAST])
                TT("dve", ax.ap[:, 0:32], PB[1][:, 32:64], NACUM.ap[:, t, :], ALU.add, [NACUM], [PT[1], ax])
                ACT(ax.ap[:, 0:32], ax.ap[:, 0:32], AF.Exp, [ax], [ax])
                TT("dve", WJ.ap[:, t, :], ax.ap[:, 0:32], DT.ap[:, t, :], ALU.mult, [ax, DT], [WJ])
            pend_d2 = [None]
            xb = [dict(XB=A.tile(SEGT + 3), acc=A.tile(SEGT), xc=A.tile(SEGT // 2, BF16)) for _ in range(2)]
            for bi in range(6):
                cols = [(C_XBC + bi * 512, 512)]
                wb, v = win_block(("xbc", bi), l, cols)
                for j in range(4):
                    gc = bi * 4 + j
                    B = xb[gc % 2]
                    bank = gc % 4
                    MMACC(bank, PB[bank][:, 0:SEGT], [(v[:, c, j * 128:(j + 1) * 128], hT.ap[:, c, :]) for c in range(KC)], [hT, wb])
                    if pend_d2[0] is not None:
                        pend_d2[0]()
                        pend_d2[0] = None
                    CP("dve", B["XB"].ap[:, 0:3], XBcar.ap[:, gc, :], [XBcar], [B["XB"]])
                    CP("act", B["XB"].ap[:, 3:3 + SEGT], PB[bank][:, 0:SEGT], [], [PT[bank], B["XB"]])
                    CP("dve", XBcar.ap[:, gc, :], B["XB"].ap[:, SEGT:SEGT + 3], [B["XB"]], [XBcar])
                    TS("dve", B["acc"].ap, B["XB"].ap[:, 0:SEGT], ssmwT.ap[:, gc, 0:1], ssmbT.ap[:, gc:gc + 1], ALU.mult, ALU.add, [B["XB"], ssmwT, ssmbT], [B["acc"]])
                    for k in range(1, 4):
                        STT("dve", B["acc"].ap, B["XB"].ap[:, k:k + SEGT], ssmwT.ap[:, gc, k:k + 1], B["acc"].ap, ALU.mult, ALU.add, [B["XB"], ssmwT], [B["acc"]])
                    if gc < 16:
                        ACT(B["xc"].ap, B["acc"].ap, AF.Silu, [B["acc"]], [B["xc"]])

                        def tr2_(B=B, gc=gc):
                            tb = 6 + (gc % 2)
                            TRS(tb, [(pbf(tb)[:, t * 128:(t + 1) * 128], B["xc"].ap[:, t * 128:(t + 1) * 128], identb.ap) for t in range(NT)], [B["xc"], identb])
                            CP("act" if gc % 2 else "dve", XS.ap[:, :, gc * 128:(gc + 1) * 128], pbf(tb)[:, 0:NT * 128].rearrange("p (t c) -> p t c", t=NT), [], [PT[tb], XS])
                        pend_d2[0] = tr2_
                    elif gc < 20:
                        g = gc - 16
                        ACT(BMT.ap[:, g, :], B["acc"].ap, AF.Silu, [B["acc"]], [BMT])

                        def tr3_(g=g, gc=gc):
                            tb = 6 + (gc % 2)
                            TRS(tb, [(pbf(tb)[:, t * 128:(t + 1) * 128], BMT.ap[:, g, t * 128:(t + 1) * 128], identb.ap) for t in range(NT)], [BMT, identb])
                            CP("dve", BM.ap[:, :, g, :], pbf(tb)[:, 0:NT * 128].rearrange("p (t c) -> p t c", t=NT), [], [PT[tb], BM])
                        pend_d2[0] = tr3_
                    else:
                        g = gc - 20
                        ACT(CMT.ap[:, g, :], B["acc"].ap, AF.Silu, [B["acc"]], [CMT])
                if do_sample:
                    sample_proj(wb, v, cols, stage)
            if pend_d2[0] is not None:
                pend_d2[0]()
                pend_d2[0] = None
            for zb in range(4):
                wb = wb_next()
                cols = [(C_Z + zb * 512, 512)]
                v = load_win_block(l, wb, cols)
                for t in range(NT):
                    bank = t % 2
                    MMACC(bank, PB[bank][:, 0:512], [(hT.ap[:, c, t * 128:(t + 1) * 128], v[:, c, :]) for c in range(KC)], [hT, wb])
                    ACT(SZ.ap[:, t, zb * 512:(zb + 1) * 512], PB[bank][:, 0:512], AF.Silu, [], [PT[bank], SZ])
                if do_sample:
                    sample_proj(wb, v, cols, stage)
            prefetch_out(("out", 0), l, 0)
            prefetch_out(("out", 1), l, 1)
            cb = [A.tile(128), A.tile(128)]
            Lt = [A.tile(512), A.tile(512)]
            MT = [A.tile(256, BF16, "p (i j) -> p i j", i=4), A.tile(256, BF16, "p (i j) -> p i j", i=4)]
            XDT = [A.tile(1024, BF16)] * 2
            R4 = [A.tile(512), A.tile(512)]
            t1 = A.tile(512)
            t2 = A.tile(512)
            t2s = [t2, A.tile(512)]
            ssq4_t = A.tile(4)
            xw = [A.tile(256, BF16), A.tile(256, BF16)]
            YZ = A.tile(2048)
            YN = A.tile(1024, BF16)
            ssq = A.tile(1); rs = A.tile(1); tmp = A.tile(1)
            h8 = lambda ap: ap.rearrange("p (h q) -> p h q", h=8)

            def step1(c, g):
                sl = slice(c * 128, (c + 1) * 128)
                MM(0, [(PB[0][:, 0:128], BMT.ap[:, g, sl], CMT.ap[:, g, sl], True, True)], [BMT, CMT])
                ib = 1 if g % 2 == 0 else 7
                MM(ib, [(PB[ib][:, 0:512], CMT.ap[:, g, sl], sT_bf.ap[:, g * 512:(g + 1) * 512], True, True)], [CMT, sT_bf])
                for r in range(2):
                    bb = 2 + r
                    h0 = g * 8 + r * 4
                    TT("pool", R4[r].ap.rearrange("p (i j) -> p i j", i=4), tri.ap.unsqueeze(1).to_broadcast([128, 4, 128]),
                       AA.ap[:, c, h0:h0 + 4].unsqueeze(2).to_broadcast([128, 4, 128]), ALU.mult, [tri, AA], [R4[r]])
                    o = PB[bb][:, 0:512]
                    if r == 1:
                        gs_ = slice(g * 512, (g + 1) * 512)
                        hs = slice(g * 8, (g + 1) * 8)
                        TT("pool", h8(xw[g % 2].ap), h8(XS.ap[:, c, gs_]), WJ.ap[:, c, hs].unsqueeze(2).to_broadcast([128, 8, 64]), ALU.mult, [XS, WJ], [xw[g % 2]])
                        TT("pool", h8(t2s[g % 2].ap), h8(XS.ap[:, c, gs_]), D_bc.ap[:, hs].unsqueeze(2).to_broadcast([128, 8, 64]), ALU.mult, [XS, D_bc], [t2s[g % 2]])
                    MM(bb, [(o, ones.ap, R4[r].ap, True, False),
                            (o, tri.ap, NAA.ap[:, c, h0:h0 + 4].unsqueeze(2).to_broadcast([128, 4, 128]), False, False),
                            (o, identb.ap, maskneg.ap.unsqueeze(1).to_broadcast([128, 4, 128]), False, True)],
                       [ones, R4[r], tri, NAA, identb, maskneg])

            def step23(c, g):
                cbt = cb[g % 2]
                if g == 0:
                    TT("dve", XDT[0].ap.rearrange("p (h q) -> p h q", h=32), XS.ap[:, c, :].rearrange("p (h q) -> p h q", h=32),
                       DT.ap[:, c, :].unsqueeze(2).to_broadcast([128, 32, 64]), ALU.mult, [XS, DT], [XDT[0]])
                CP("act", cbt.ap, PB[0][:, 0:128], [], [PT[0], cbt])
                for r in range(2):
                    bb = 2 + r
                    ACT(Lt[r].ap, PB[bb][:, 0:512], AF.Exp, [], [PT[bb], Lt[r]])
                    TT("dve", MT[r].ap, Lt[r].ap.rearrange("p (i j) -> p i j", i=4), cbt.ap.unsqueeze(1).to_broadcast([128, 4, 128]), ALU.mult, [Lt[r], cbt], [MT[r]])

            def step4(c, g):
                yb = 4 + (g % 2)
                mms = []
                for r in range(2):
                    for i in range(4):
                        hh = r * 4 + i
                        h = g * 8 + hh
                        mms.append((PB[yb][:, hh * 64:(hh + 1) * 64], MT[r].ap[:, i, :], XDT[c % 2].ap[:, h * 64:(h + 1) * 64], True, True))
                MM(yb, mms, [MT[0], MT[1], XDT[c % 2]])
                MM(6, [(PB[6][:, 0:512], BM.ap[:, c, g, :], xw[g % 2].ap, True, True)], [BM, xw[g % 2]])

            def step56(c, g):
                gs_ = slice(g * 512, (g + 1) * 512)
                hs = slice(g * 8, (g + 1) * 8)
                yb = 4 + (g % 2)
                ib = 1 if g % 2 == 0 else 7
                TT("dve", h8(t1.ap), h8(PB[ib][:, 0:512]), EA.ap[:, c, hs].unsqueeze(2).to_broadcast([128, 8, 64]), ALU.mult, [EA], [PT[ib], t1])
                TT("dve", t1.ap, t1.ap, PB[yb][:, 0:512], ALU.add, [], [PT[yb], t1])
                TT("dve", t1.ap, t1.ap, t2s[g % 2].ap, ALU.add, [t2s[g % 2]], [t1])
                TT("dve", YZ.ap[:, gs_], t1.ap, SZ.ap[:, c, gs_], ALU.mult, [t1, SZ], [YZ])
                TT("dve", h8(sT.ap[:, gs_]), h8(sT.ap[:, gs_]), ELAST.ap[:, c, hs].unsqueeze(2).to_broadcast([128, 8, 64]), ALU.mult, [ELAST], [sT])
                TT("dve", sT.ap[:, gs_], sT.ap[:, gs_], PB[6][:, 0:512], ALU.add, [], [PT[6], sT])
                CP("act", sT_bf.ap[:, gs_], sT.ap[:, gs_], [sT], [sT_bf])

            def chunk_tail(c):
                sl = slice(c * 128, (c + 1) * 128)
                ssq4 = ssq4_t
                for q in range(4):
                    ACT(t1.ap, YZ.ap[:, q * 512:(q + 1) * 512], AF.Square, [YZ], [t1, ssq4], accum=ssq4.ap[:, q:q + 1])
                S.op("dve", lambda e, o=ssq.ap[:, 0:1], i=ssq4.ap[:, 0:4]: e.tensor_reduce(out=o, in_=i, axis=AX.X, op=ALU.add), [ssq4], [ssq])
                RSTD(rs.ap[:, 0:1], ssq.ap[:, 0:1], 1.0 / 2048, [ssq], [rs], tmp)
                ACT(YN.ap, YZ.ap, AF.Copy, [YZ, rs], [YN], scale=rs.ap[:, 0:1])
                for q in range(4):
                    tb = 7 if q % 2 == 0 else 6
                    TRS(tb, [(pbf(tb)[:, k * 128:(k + 1) * 128], YN.ap[:, (q * 4 + k) * 128:(q * 4 + k + 1) * 128], identb.ap) for k in range(4)], [YN, identb])
                    for k in range(4):
                        cc = q * 4 + k
                        dst = mixT.sub(mixT.ap[:, 16 + cc, sl], ((16 + cc) * SEGT + c * 128) // 2, 64)
                        if k % 2 == 0:
                            ACT(dst.ap, pbf(tb)[:, k * 128:(k + 1) * 128], AF.Copy, [gssmT], [PT[tb], dst], scale=gssmT.ap[:, cc:cc + 1])
                        else:
                            TS("dve", dst.ap, pbf(tb)[:, k * 128:(k + 1) * 128], gssmT.ap[:, cc:cc + 1], None, ALU.mult, None, [gssmT], [PT[tb], dst])

            items = [(c, g) for c in range(NT) for g in range(4)]
            step1(*items[0])
            step23(*items[0])
            for k, (c, g) in enumerate(items):
                if k + 1 < len(items):
                    step1(*items[k + 1])
                step4(c, g)
                step56(c, g)
                if g == 3:
                    chunk_tail(c)
                if k + 1 < len(items):
                    step23(*items[k + 1])
            A.release(m)

        def out_phase(l, seg, do_sample, hoist):
            m = A.mark()
            OUT = A.tile(NT * 2048, F32, "p (t x) -> p t x", t=NT)
            gpost = A.tile(2048)
            S.dma("sp", gpost.ap, norm_post[l:l + 1, :].partition_broadcast(128), writes=[gpost])
            outs = A.tile(2048, parts=SB) if do_sample else None
            last = (l == L - 1)
            for ob in range(8):
                wb, v = out_block(("out", ob), l, ob)
                for t in range(NT):
                    bank = t % 4
                    MMACC(bank, PB[bank][:, 0:256], [(mixT.ap[:, mc, t * 128:(t + 1) * 128], v[:, mc, :]) for mc in range(32)], [mixT, wb])
                    CP("act" if t % 2 else "dve", OUT.ap[:, t, ob * 256:(ob + 1) * 256], PB[bank][:, 0:256], [], [PT[bank], OUT])
                if do_sample:
                    MMACC(5, PB[5][0:SB, 0:256], [(mixTs.ap[:, mc, :], v[:, mc, :]) for mc in range(32)], [mixTs, wb])
                    CP("act", outs.ap[:, ob * 256:(ob + 1) * 256], PB[5][0:SB, 0:256], [], [PT[5], outs])
            xt = [A.tile(2048), A.tile(2048)]
            junk = A.tile(2048)
            ssq = [A.tile(1), A.tile(1)]; rs = [A.tile(1), A.tile(1)]; tmp = [A.tile(1), A.tile(1)]
            if hoist is not None:
                hoist()

            def finish(o_ap, o_tile, x_t, np_, src_ap, src_reads, dst_ap, dst_tracks, sq, r, tm, preloaded=False):
                if not preloaded:
                    S.dma("sp", x_t.ap, src_ap, reads=src_reads, writes=[x_t])
                ACT(junk.ap[0:np_, :], o_ap, AF.Square, [o_tile], [junk, sq], accum=sq.ap[:, 0:1])
                RSTD(r.ap[:, 0:1], sq.ap[:, 0:1], 1.0 / D_MODEL, [sq], [r], tm)
                STT("dve", o_ap, o_ap, r.ap[:, 0:1], gpost.ap[0:np_, :], ALU.mult, ALU.mult, [r, gpost], [o_tile])
                TT("dve", x_t.ap, x_t.ap, o_ap, ALU.add, [o_tile], [x_t])
                S.dma("sp", dst_ap, x_t.ap, reads=[x_t], writes=dst_tracks)

            def xsrc(t):
                r0 = seg * SEGT + t * 128
                tr = T_xres[seg * NT + t]
                return (xp[r0:r0 + 128, :], []) if l == 0 else (xres[r0:r0 + 128, :], [tr])

            s0, sr0 = xsrc(0)
            S.dma("sp", xt[0].ap, s0, reads=sr0, writes=[xt[0]])
            for t in range(NT):
                p = t % 2
                r0 = seg * SEGT + t * 128
                tr = T_xres[seg * NT + t]
                if t + 1 < NT:
                    s1, sr1 = xsrc(t + 1)
                    S.dma("sp", xt[1 - p].ap, s1, reads=sr1, writes=[xt[1 - p]])
                if last:
                    dst, dt_ = yp[r0:r0 + 128, :], []
                else:
                    dst, dt_ = xres[r0:r0 + 128, :], [tr]
                finish(OUT.ap[:, t, :], OUT, xt[p], 128, None, None, dst, dt_, ssq[p], rs[p], tmp[p], preloaded=True)
            if do_sample:
                xts = A.tile(2048, parts=SB)
                sq = A.tile(1, parts=SB); r = A.tile(1, parts=SB); tm = A.tile(1, parts=SB)
                src, sr = (xs, []) if l == 0 else (xsres, [T_xsres])
                dst, dt_ = (ys, []) if last else (xsres, [T_xsres])
                finish(outs.ap, outs, xts, SB, src, sr, dst, dt_, sq, r, tm)
            A.release(m)

        def decode_phase(l):
            m = A.mark()
            PJT = T_pj

            def ld(words, src, tr, parts=128, pat=None, **kw):
                t = A.tile(words, F32, pat, parts=parts, **kw)
                S.dma("sp", t.ap, src, reads=tr, writes=[t])
                return t

            def ldj(nj, c, nk, srcj, parts=128):
                t = A.tile(nj * c, F32, "p (j c) -> p j c", parts=parts, j=nj)
                for j in range(nj):
                    S.dma("sp", t.ap[:, j, :], srcj(j), writes=[t])
                return t

            def stj(dstj, t, nj):
                for j in range(nj):
                    S.dma("sp", dstj(j), t.ap[:, j, :], reads=[t])

            def pjv(c0, n, c):
                return pj[:, c0:c0 + n].rearrange("b (k c) -> k b c", c=c)

            rs = A.tile(1); tmp = A.tile(1)
            q = ld(128, pjv(C_Q, 1024, 128), PJT[0:8])
            k = ld(128, pjv(C_K, 1024, 128), PJT[8:16])
            vv = ld(128, pjv(C_V, 1024, 128), PJT[16:24])
            gr = ld(128, pjv(C_GR, 1024, 128), PJT[24:32])
            css = ld(256, cn["cs_s"], [])
            scs = ld(256, cn["sc_s"], [])
            gretP = ld(128, ret_norm[l].rearrange("(h e) -> h e", e=128).unsqueeze(1).to_broadcast([8, SB, 128]), [])
            qk = A.tile(256); AB = A.tile(256); CD = A.tile(256); qkr = A.tile(256, F32, "p (a d) -> p a d", a=2)
            CP("dve", qk.ap[:, 0:128], q.ap, [q], [qk])
            CP("dve", qk.ap[:, 128:256], k.ap, [k], [qk])
            TT("dve", AB.ap, qk.ap, css.ap, ALU.mult, [qk, css], [AB])
            TT("dve", CD.ap, qk.ap, scs.ap, ALU.mult, [qk, scs], [CD])
            AB4 = AB.ap.rearrange("p (a b f) -> p a b f", a=2, b=2)
            CD4 = CD.ap.rearrange("p (a b f) -> p a b f", a=2, b=2)
            TT("dve", qkr.ap[:, :, 0:64], AB4[:, :, 0, :], AB4[:, :, 1, :], ALU.subtract, [AB], [qkr])
            TT("dve", qkr.ap[:, :, 64:128], CD4[:, :, 0, :], CD4[:, :, 1, :], ALU.add, [CD], [qkr])
            qP = qkr.ap[:, 0, :]
            kP = qkr.ap[:, 1, :]
            ACT(gr.ap, gr.ap, AF.Silu, [gr], [gr])
            oacc = A.tile(128); opart = A.tile(128)
            pcs = [A.tile(1024, F32, "p (d e) -> p d e", d=8) for _ in range(4)]
            tmps = [A.tile(1024, F32, "p (d e) -> p d e", d=8) for _ in range(4)]
            tmpsB = [A.tile(1024, F32, "p (d e) -> p d e", d=8) for _ in range(4)]
            def ld_ret(pi):
                S.dma("sp", pcs[pi % 4].ap, st_ret[l][:, pi * 8:pi * 8 + 8, :], writes=[pcs[pi % 4]])
            for pi in range(3):
                ld_ret(pi)
            for pi in range(16):
                sp_ = pcs[pi % 4]; tp = tmps[pi % 4]; tq = tmpsB[pi % 4]
                d0 = pi * 8
                if pi + 3 < 16:
                    ld_ret(pi + 3)
                TT("pool", tp.ap, sp_.ap, qP[:, d0:d0 + 8].unsqueeze(2).to_broadcast([128, 8, 128]), ALU.mult, [sp_, qkr], [tp])
                dst = oacc if pi == 0 else opart
                S.op("dve", lambda e, o=dst.ap, i=tp.ap.rearrange("p d e -> p e d"): e.tensor_reduce(out=o, in_=i, axis=AX.X, op=ALU.add), [tp], [dst])
                if pi > 0:
                    TT("dve", oacc.ap, oacc.ap, opart.ap, ALU.add, [opart], [oacc])
                TT("pool", tq.ap, kP[:, d0:d0 + 8].unsqueeze(2).to_broadcast([128, 8, 128]), vv.ap.unsqueeze(1).to_broadcast([128, 8, 128]), ALU.mult, [qkr, vv], [tq])
                STT("dve", sp_.ap, sp_.ap, gam_p.ap[:, 0:1], tq.ap, ALU.mult, ALU.add, [gam_p, tq], [sp_])
                S.dma("sp", ret_s[l][:, d0:d0 + 8, :], sp_.ap, reads=[sp_])
            qkd = A.tile(128); qks = A.tile(1)
            TT("dve", qkd.ap, qP, kP, ALU.mult, [qkr], [qkd])
            S.op("dve", lambda e: e.tensor_reduce(out=qks.ap[:, 0:1], in_=qkd.ap, axis=AX.X, op=ALU.add), [qkd], [qks])
            TS("dve", oacc.ap, oacc.ap, gam_p.ap[:, 0:1], None, ALU.mult, None, [gam_p], [oacc])
            STT("dve", oacc.ap, vv.ap, qks.ap[:, 0:1], oacc.ap, ALU.mult, ALU.add, [vv, qks], [oacc])
            stats = A.tile(6); mv = A.tile(2)
            S.op("dve", lambda e: e.bn_stats(out=stats.ap[:, 0:6], in_=oacc.ap), [oacc], [stats])
            S.op("dve", lambda e: e.bn_aggr(out=mv.ap[:, 0:2], in_=stats.ap[:, 0:6]), [stats], [mv])
            RSTD(rs.ap[:, 0:1], mv.ap[:, 1:2], 1.0, [mv], [rs], tmp)
            TS("dve", oacc.ap, oacc.ap, mv.ap[:, 0:1], rs.ap[:, 0:1], ALU.subtract, ALU.mult, [mv, rs], [oacc])
            TT("dve", oacc.ap, oacc.ap, gretP.ap, ALU.mult, [gretP], [oacc])
            ob = A.tile(64, BF16)
            TT("dve", ob.ap, oacc.ap, gr.ap, ALU.mult, [oacc, gr], [ob])
            TRS(7, [(pbf(7)[:, 0:128], ob.ap, identb.ap)], [ob, identb])
            CP("dve", mixTs.ap[:, 0:8, :], pbf(7)[:, 0:128].rearrange("p (k b) -> p k b", k=8), [], [PT[7], mixTs])
            A.release(m)
            m = A.mark()
            bg = ld(128, pjv(C_BG, 1024, 128), PJT[32:40])
            cg = ld(128, pjv(C_CG, 1024, 128), PJT[40:48])
            hh_ = ld(128, pjv(C_H, 1024, 128), PJT[48:56])
            gsc = ld(128, pjv(C_GSC, 1024, 128), PJT[56:64])
            wP = ldj(3, 128, 8, lambda j: sc_w[l][j].rearrange("(k c) -> k c", c=128).unsqueeze(1).to_broadcast([8, SB, 128]))
            bP = ld(128, sc_b[l].rearrange("(k c) -> k c", c=128).unsqueeze(1).to_broadcast([8, SB, 128]), [])
            buf = ldj(2, 128, 8, lambda j: st_sc[l][:, j, :].rearrange("b (k c) -> k b c", c=128))
            nst = A.tile(256, F32, "p (j c) -> p j c", j=2)
            acc = A.tile(128)
            TT("dve", nst.ap[:, 1, :], cg.ap, hh_.ap, ALU.mult, [cg, hh_], [nst])
            CP("dve", nst.ap[:, 0, :], buf.ap[:, 1, :], [buf], [nst])
            TT("dve", acc.ap, buf.ap[:, 0, :], wP.ap[:, 0, :], ALU.mult, [buf, wP], [acc])
            TT("dve", acc.ap, acc.ap, bP.ap, ALU.add, [bP], [acc])
            TT("dve", bP.ap, buf.ap[:, 1, :], wP.ap[:, 1, :], ALU.mult, [buf, wP], [bP])
            TT("dve", acc.ap, acc.ap, bP.ap, ALU.add, [bP], [acc])
            TT("dve", bP.ap, nst.ap[:, 1, :], wP.ap[:, 2, :], ALU.mult, [nst, wP], [bP])
            TT("dve", acc.ap, acc.ap, bP.ap, ALU.add, [bP], [acc])
            ACT(gsc.ap, gsc.ap, AF.Silu, [gsc], [gsc])
            TT("dve", acc.ap, acc.ap, gsc.ap, ALU.mult, [gsc], [acc])
            ob = A.tile(64, BF16)
            TT("dve", ob.ap, acc.ap, bg.ap, ALU.mult, [acc, bg], [ob])
            stj(lambda j: sc_s[l][:, j, :].rearrange("b (k c) -> k b c", c=128), nst, 2)
            TRS(7, [(pbf(7)[:, 0:128], ob.ap, identb.ap)], [ob, identb])
            CP("dve", mixTs.ap[:, 8:16, :], pbf(7)[:, 0:128].rearrange("p (k b) -> p k b", k=8), [], [PT[7], mixTs])
            A.release(m)
            m = A.mark()
            xr = ld(256, pjv(C_XBC, 2048, 256), PJT[80:96])
            zz = ld(256, pjv(C_Z, 2048, 256), PJT[64:80])
            wx = ldj(4, 256, 8, lambda j: ssm_w[l][j, 0:2048].rearrange("(k c) -> k c", c=256).unsqueeze(1).to_broadcast([8, SB, 256]))
            bx = ld(256, ssm_b[l][0:2048].rearrange("(k c) -> k c", c=256).unsqueeze(1).to_broadcast([8, SB, 256]), [])
            bufx = ldj(3, 256, 8, lambda j: st_sconv[l][:, j, 0:2048].rearrange("b (k c) -> k b c", c=256))
            nstx = A.tile(768, F32, "p (j c) -> p j c", j=3)
            xc = A.tile(256); t256 = A.tile(256)
            CP("dve", nstx.ap[:, 0:2, :], bufx.ap[:, 1:3, :], [bufx], [nstx])
            CP("dve", nstx.ap[:, 2, :], xr.ap, [xr], [nstx])
            stj(lambda j: sconv_s[l][:, j, 0:2048].rearrange("b (k c) -> k b c", c=256), nstx, 3)
            TT("dve", xc.ap, xr.ap, wx.ap[:, 3, :], ALU.mult, [xr, wx], [xc])
            TT("dve", xc.ap, xc.ap, bx.ap, ALU.add, [bx], [xc])
            for j in range(3):
                TT("dve", t256.ap, bufx.ap[:, j, :], wx.ap[:, j, :], ALU.mult, [bufx, wx], [t256])
                TT("dve", xc.ap, xc.ap, t256.ap, ALU.add, [t256], [xc])
            ACT(xc.ap, xc.ap, AF.Silu, [xc], [xc])
            ACT(zz.ap, zz.ap, AF.Silu, [zz], [zz])
            br = ld(256, pjv(C_XBC + 2048, 1024, 256), PJT[96:104], parts=64)
            wbc = ldj(4, 256, 4, lambda j: ssm_w[l][j, 2048:3072].rearrange("(k c) -> k c", c=256).unsqueeze(1).to_broadcast([4, SB, 256]), parts=64)
            bbc = ld(256, ssm_b[l][2048:3072].rearrange("(k c) -> k c", c=256).unsqueeze(1).to_broadcast([4, SB, 256]), [], parts=64)
            bufb = ldj(3, 256, 4, lambda j: st_sconv[l][:, j, 2048:3072].rearrange("b (k c) -> k b c", c=256), parts=64)
            nstb = A.tile(768, F32, "p (j c) -> p j c", j=3, parts=64)
            bc = A.tile(256, parts=64); tb256 = A.tile(256, parts=64)
            CP("dve", nstb.ap[:, 0:2, :], bufb.ap[:, 1:3, :], [bufb], [nstb])
            CP("dve", nstb.ap[:, 2, :], br.ap, [br], [nstb])
            stj(lambda j: sconv_s[l][:, j, 2048:3072].rearrange("b (k c) -> k b c", c=256), nstb, 3)
            TT("dve", bc.ap, br.ap, wbc.ap[:, 3, :], ALU.mult, [br, wbc], [bc])
            TT("dve", bc.ap, bc.ap, bbc.ap, ALU.add, [bbc], [bc])
            for j in range(3):
                TT("dve", tb256.ap, bufb.ap[:, j, :], wbc.ap[:, j, :], ALU.mult, [bufb, wbc], [tb256])
                TT("dve", bc.ap, bc.ap, tb256.ap, ALU.add, [tb256], [bc])
            ACT(bc.ap, bc.ap, AF.Silu, [bc], [bc])
            MM(4, [(PB[4][:, 0:128], sel4.ap[:, 0, :], bc.ap[:, 0:128], True, False),
                   (PB[4][:, 0:128], sel4.ap[:, 1, :], bc.ap[:, 128:256], False, True),
                   (PB[4][:, 128:256], sel4.ap[:, 2, :], bc.ap[:, 0:128], True, False),
                   (PB[4][:, 128:256], sel4.ap[:, 3, :], bc.ap[:, 128:256], False, True)], [sel4, bc])
            bcP = A.tile(256)
            CP("dve", bcP.ap, PB[4][:, 0:256], [], [PT[4], bcP])
            bmP = bcP.ap[:, 0:128]
            cmP = bcP.ap[:, 128:256]
            dtr = ld(4, pj[:, C_DT:C_DT + 32].rearrange("b (k c) -> k b c", c=4), [PJT[104]])
            dtbP = ld(4, dt_bias[l].rearrange("(k c) -> k c", c=4).unsqueeze(1).to_broadcast([8, SB, 4]), [])
            AP_ = ld(4, a_log[l].rearrange("(k c) -> k c", c=4).unsqueeze(1).to_broadcast([8, SB, 4]), [])
            DP = ld(4, d_skip[l].rearrange("(k c) -> k c", c=4).unsqueeze(1).to_broadcast([8, SB, 4]), [])
            x4 = A.tile(4); a4 = A.tile(4); l4 = A.tile(4); dtP = A.tile(4); eaP = A.tile(4); coef = A.tile(4); cbP = A.tile(1)
            TT("dve", x4.ap[:, 0:4], dtr.ap[:, 0:4], dtbP.ap[:, 0:4], ALU.add, [dtr, dtbP], [x4])
            STT("dve", a4.ap[:, 0:4], x4.ap[:, 0:4], -1.0, x4.ap[:, 0:4], ALU.mult, ALU.max, [x4], [a4])
            ACT(a4.ap[:, 0:4], a4.ap[:, 0:4], AF.Exp, [a4], [a4], scale=-1.0)
            ACT(l4.ap[:, 0:4], a4.ap[:, 0:4], AF.Ln, [a4], [l4], bias=1.0)
            STT("dve", dtP.ap[:, 0:4], x4.ap[:, 0:4], 0.0, l4.ap[:, 0:4], ALU.max, ALU.add, [x4, l4], [dtP])
            ACT(AP_.ap[:, 0:4], AP_.ap[:, 0:4], AF.Exp, [AP_], [AP_])
            TT("dve", a4.ap[:, 0:4], dtP.ap[:, 0:4], AP_.ap[:, 0:4], ALU.mult, [dtP, AP_], [a4])
            ACT(eaP.ap[:, 0:4], a4.ap[:, 0:4], AF.Exp, [a4], [eaP], scale=-1.0)
            tb128 = A.tile(128)
            TT("dve", tb128.ap, bmP, cmP, ALU.mult, [bcP], [tb128])
            S.op("dve", lambda e: e.tensor_reduce(out=cbP.ap[:, 0:1], in_=tb128.ap, axis=AX.X, op=ALU.add), [tb128], [cbP])
            STT("dve", coef.ap[:, 0:4], dtP.ap[:, 0:4], cbP.ap[:, 0:1], DP.ap[:, 0:4], ALU.mult, ALU.add, [dtP, cbP, DP], [coef])
            xdt = A.tile(256); yS = A.tile(256)
            xc3 = xc.ap.rearrange("p (h q) -> p h q", h=4)
            TT("dve", xdt.ap.rearrange("p (h q) -> p h q", h=4), xc3, dtP.ap[:, 0:4].unsqueeze(2).to_broadcast([128, 4, 64]), ALU.mult, [xc, dtP], [xdt])
            pcs = [A.tile(1024, F32, "p (q n) -> p q n", q=8) for _ in range(4)]
            tmps = [A.tile(1024, F32, "p (q n) -> p q n", q=8) for _ in range(4)]
            tmpsB = [A.tile(1024, F32, "p (q n) -> p q n", q=8) for _ in range(4)]
            def ld_ssm(pi):
                hl_, p0_ = pi // 8, (pi % 8) * 8
                S.dma("sp", pcs[pi % 4].ap, st_ssm[l][:, hl_, p0_:p0_ + 8, :], writes=[pcs[pi % 4]])
            for pi in range(3):
                ld_ssm(pi)
            for pi in range(32):
                hl, p0 = pi // 8, (pi % 8) * 8
                sp_ = pcs[pi % 4]; tp = tmps[pi % 4]; tq = tmpsB[pi % 4]
                if pi + 3 < 32:
                    ld_ssm(pi + 3)
                TT("pool", tp.ap, sp_.ap, cmP.unsqueeze(1).to_broadcast([128, 8, 128]), ALU.mult, [sp_, bcP], [tp])
                S.op("dve", lambda e, o=yS.ap[:, hl * 64 + p0:hl * 64 + p0 + 8], i=tp.ap: e.tensor_reduce(out=o, in_=i, axis=AX.X, op=ALU.add), [tp], [yS])
                TT("pool", tq.ap, xdt.ap[:, hl * 64 + p0:hl * 64 + p0 + 8].unsqueeze(2).to_broadcast([128, 8, 128]),
                   bmP.unsqueeze(1).to_broadcast([128, 8, 128]), ALU.mult, [xdt, bcP], [tq])
                STT("dve", sp_.ap, sp_.ap, eaP.ap[:, hl:hl + 1], tq.ap, ALU.mult, ALU.add, [eaP, tq], [sp_])
                S.dma("sp", ssm_s[l][:, hl, p0:p0 + 8, :], sp_.ap, reads=[sp_])
            y = A.tile(256)
            TT("dve", y.ap.rearrange("p (h q) -> p h q", h=4), xc3, coef.ap[:, 0:4].unsqueeze(2).to_broadcast([128, 4, 64]), ALU.mult, [xc, coef], [y])
            TT("dve", yS.ap.rearrange("p (h q) -> p h q", h=4), yS.ap.rearrange("p (h q) -> p h q", h=4),
               eaP.ap[:, 0:4].unsqueeze(2).to_broadcast([128, 4, 64]), ALU.mult, [eaP], [yS])
            TT("dve", y.ap, y.ap, yS.ap, ALU.add, [yS], [y])
            TT("dve", y.ap, y.ap, zz.ap, ALU.mult, [zz], [y])
            ssq = A.tile(1)
            ACT(t256.ap, y.ap, AF.Square, [y], [t256, ssq], accum=ssq.ap[:, 0:1])
            MM(4, [(PB[4][:, 0:1], selb.ap, ssq.ap[:, 0:1], True, True)], [selb, ssq])
            tot = A.tile(1)
            CP("dve", tot.ap[:, 0:1], PB[4][:, 0:1], [], [PT[4], tot])
            RSTD(rs.ap[:, 0:1], tot.ap[:, 0:1], 1.0 / 2048, [tot], [rs], tmp)
            gsP = ld(256, ssm_norm[l].rearrange("(k c) -> k c", c=256).unsqueeze(1).to_broadcast([8, SB, 256]), [])
            ob2 = A.tile(128, BF16)
            STT("dve", ob2.ap, y.ap, rs.ap[:, 0:1], gsP.ap, ALU.mult, ALU.mult, [y, rs, gsP], [ob2])
            TRS(7, [(pbf(7)[:, r * 128:(r + 1) * 128], ob2.ap[:, r * 128:(r + 1) * 128], identb.ap) for r in range(2)], [ob2, identb])
            mv_ = mixTs.ap[:, 16:32, :].rearrange("p (k r) b -> p r k b", r=2)
            for r in range(2):
                CP("dve", mv_[:, r, :, :], pbf(7)[:, r * 128:(r + 1) * 128].rearrange("p (k b) -> p k b", k=8), [], [PT[7], mixTs])
            A.release(m)

        def write_prompt_states(l):
            m = A.mark()
            S.dma("sp", ret_p[l].rearrange("h d e -> d h e"), S_ret.ap, reads=[S_ret])
            for j in range(2):
                S.dma("sp", sc_p[l][j].rearrange("(c p) -> p c", p=128), Ucar.ap[:, :, j], reads=[Ucar], allow_slow_non_contiguous=True)
            for j in range(3):
                S.dma("sp", sconv_p[l][j].rearrange("(c p) -> p c", p=128), XBcar.ap[:, :, j], reads=[XBcar], allow_slow_non_contiguous=True)
            so = [A.tile(512), A.tile(512)]
            for q in range(4):
                TRS(q % 2, [(PB[q % 2][:, k * 128:(k + 1) * 128], sT.ap[:, (q * 4 + k) * 128:(q * 4 + k + 1) * 128], identf.ap) for k in range(4)], [sT, identf])
                CP("dve", so[q % 2].ap, PB[q % 2][:, 0:512], [], [PT[q % 2], so[q % 2]])
                S.dma("sp", ssm_p[l][q * 8:(q + 1) * 8].rearrange("(k a) p n -> (a p) k n", a=2), so[q % 2].ap.rearrange("x (k n) -> x k n", k=4), reads=[so[q % 2]])
            A.release(m)

        for l in range(L):
            load_layer_params(l)
            for seg in range(nseg):
                ds = (seg == 0)
                if seg == 0:
                    build_hT(l, seg)
                if ds:
                    build_hTs(l)
                mret_, units, stage_ = ret_phase(l, seg, ds)
                sc_phase(l, seg, ds, units, stage_)
                A.release(mret_)
                ssd_phase(l, seg, ds)
                if ds:
                    decode_phase(l)
                out_phase(l, seg, ds, (lambda l=l, seg=seg: build_hT(l, seg + 1)) if seg + 1 < nseg else None)
            write_prompt_states(l)
        S.emit(st)
        build_program.stats = dict(ops=len(S.ops), waits=S.nwaits, arena_peak=A.peak)
    return nc


_CACHE = {}


def _run(inputs, depth, nseg):
    key = (depth, nseg)
    if key not in _CACHE:
        _CACHE[key] = build_program(depth, nseg)
    nc = _CACHE[key]
    seqlen = nseg * SEGT
    f = lambda a: np.ascontiguousarray(np.asarray(a, dtype=np.float32))
    consts = _consts(seqlen)
    L = depth
    shared = {k: f(inputs[k]) for k in ("w_in", "w_out", "norm_pre", "norm_post", "ret_norm", "sc_conv_w", "sc_conv_b",
                                        "ssm_conv_w", "ssm_conv_b", "ssm_dt_bias", "ssm_a_log", "ssm_d", "ssm_norm")}
    for k, v in consts.items():
        shared["c_" + k] = f(v)
    x_prompt = np.asarray(inputs["x_prompt"], np.float32)
    x_sample = np.asarray(inputs["x_sample"], np.float32)
    s_ret = np.asarray(inputs["state_ret"], np.float32)
    s_sc = np.asarray(inputs["state_sconv"], np.float32)
    s_sconv = np.asarray(inputs["state_ssm_conv"], np.float32)
    s_ssm = np.asarray(inputs["state_ssm"], np.float32)
    in_maps = []
    for c in range(8):
        b0 = c * SB
        d = dict(shared)
        d["xp"] = f(x_prompt[c % BATCH])
        d["xs"] = f(x_sample[b0:b0 + SB, 0, :])
        d["st_ret"] = f(s_ret[:, b0:b0 + SB].transpose(0, 2, 1, 3, 4).reshape(L, 128, 128, 128))
        d["st_sc"] = f(s_sc[:, b0:b0 + SB])
        d["st_sconv"] = f(s_sconv[:, b0:b0 + SB])
        d["st_ssm"] = f(s_ssm[:, b0:b0 + SB].reshape(L, SB, 8, 4, 64, 128).transpose(0, 2, 1, 3, 4, 5).reshape(L, 128, 4, 64, 128))
        in_maps.append(d)
    res = run_bass_kernel_spmd(nc, in_maps, core_ids=list(range(8)))
    R = res.results
    yp = np.stack([R[b]["yp"] for b in range(BATCH)])
    ys = np.concatenate([R[c]["ys"] for c in range(8)])[:, None, :]
    ret_p = np.stack([R[b]["ret_p"] for b in range(BATCH)], axis=1)
    sc_p = np.stack([R[b]["sc_p"] for b in range(BATCH)], axis=1)
    sconv_p = np.stack([R[b]["sconv_p"] for b in range(BATCH)], axis=1)
    ssm_p = np.stack([R[b]["ssm_p"] for b in range(BATCH)], axis=1)
    ret_s = np.concatenate([R[c]["ret_s"].reshape(L, 8, SB, 128, 128).transpose(0, 2, 1, 3, 4) for c in range(8)], axis=1)
    sc_s = np.concatenate([R[c]["sc_s"] for c in range(8)], axis=1)
    sconv_s = np.concatenate([R[c]["sconv_s"] for c in range(8)], axis=1)
    ssm_s = np.concatenate([R[c]["ssm_s"].reshape(L, 8, SB, 4, 64, 128).transpose(0, 2, 1, 3, 4, 5).reshape(L, SB, 32, 64, 128)
                            for c in range(8)], axis=1)
    outs = (yp, ys, ret_p, sc_p, sconv_p, ssm_p, ret_s, sc_s, sconv_s, ssm_s)
    return tuple(np.ascontiguousarray(o, dtype=np.float32) for o in outs)


def kernel(**inputs):
    depth = int(np.asarray(inputs["w_in"]).shape[0])
    nseg = int(np.asarray(inputs["x_prompt"]).shape[1]) // SEGT
    return _run(inputs, depth, nseg)
```

```python
import numpy as np
from contextlib import ExitStack
import concourse.bass as bass
import concourse.mybir as mybir
from concourse.bass_utils import run_bass_kernel_spmd

F32 = mybir.dt.float32
BF16 = mybir.dt.bfloat16
ALU = mybir.AluOpType
AF = mybir.ActivationFunctionType
AX = mybir.AxisListType

D_MODEL = 2048
DEPTH = 4
BATCH = 4
SEQ = 2048
DEC_B = 128
PAST_LEN = 16384
D_PROJ = 13344
EPS = 1e-6
NT = 4
SEGT = NT * 128
KC = 16
SB = 16
C_Q, C_K, C_V, C_GR, C_BG, C_CG, C_H, C_GSC, C_Z, C_XBC, C_DT = 0, 1024, 2048, 3072, 4096, 5120, 6144, 7168, 8192, 10240, 13312
GAM = [1.0 - 2.0 ** (-5.0 - h) for h in range(8)]


class Track:
    __slots__ = ("w", "r")

    def __init__(self):
        self.w = None
        self.r = []


class Tile:
    __slots__ = ("ap", "tracks", "arena", "off")

    def __init__(self, ap, tracks, arena=None, off=0):
        self.ap = ap
        self.tracks = tracks
        self.arena = arena
        self.off = off

    def sub(self, ap, woff, wlen):
        a = self.arena
        o = self.off + woff
        return Tile(ap, a.blocks[o // a.G:(o + wlen + a.G - 1) // a.G], a, o)


def _flat(lst):
    out = []
    for x in lst:
        if isinstance(x, Tile):
            out.extend(x.tracks)
        elif isinstance(x, Track):
            out.append(x)
        elif x is None:
            pass
        else:
            out.extend(_flat(x))
    return out


class Arena:
    G = 64

    def __init__(self, tensor, nwords):
        self.t = tensor
        self.n = nwords
        self.off = 0
        self.peak = 0
        self.blocks = [Track() for _ in range(nwords // self.G + 2)]

    def tile(self, words, dt=F32, pat=None, parts=128, **kw):
        req = words
        words = (words + self.G - 1) // self.G * self.G
        off = self.off
        self.off += words
        self.peak = max(self.peak, self.off)
        assert self.off <= self.n, "SBUF arena overflow %d > %d" % (self.off, self.n)
        ap = self.t[0:parts, off:off + req]
        if dt is BF16:
            ap = ap.bitcast(BF16)
        if pat:
            ap = ap.rearrange(pat, **kw)
        return Tile(ap, self.blocks[off // self.G:(off + words) // self.G], self, off)

    def mark(self):
        return self.off

    def release(self, m):
        self.off = m


class Sched:
    COMPUTE = ("pe", "act", "dve", "pool")

    def __init__(self, nc, nslots_sp=14, nslots_pool=8):
        self.nc = nc
        self.ops = []
        self.nslots = {"sp": nslots_sp, "pool": nslots_pool}
        self.dma_count = {"sp": 0, "pool": 0}

    def op(self, eng, fn, reads=(), writes=()):
        self.ops.append((eng, fn, _flat(reads), _flat(writes), None))

    def dma(self, queue, out, in_, reads=(), writes=(), **kw):
        n = self.dma_count[queue]
        self.dma_count[queue] += 1
        slot = (queue, n % self.nslots[queue])
        self.ops.append((queue, None, _flat(reads), _flat(writes), (out, in_, slot, kw)))

    def emit(self, stack):
        nc = self.nc
        ops = self.ops
        n = len(ops)
        deps_all = [None] * n
        slot_last = {}
        needs = set()
        for i, (eng, fn, R, W, dma) in enumerate(ops):
            deps = set()
            for t in R:
                if t.w is not None:
                    deps.add(t.w)
            for t in W:
                if t.w is not None:
                    deps.add(t.w)
                if t.r:
                    deps.update(t.r)
            if dma:
                s = dma[2]
                if s in slot_last:
                    deps.add(slot_last[s])
                slot_last[s] = i
            deps.discard(i)
            deps_all[i] = deps
            needs |= deps
            for t in R:
                t.r.append(i)
            for t in W:
                t.w = i
                t.r = []
        sems = {}
        for e in self.COMPUTE:
            sems[e] = stack.enter_context(nc.semaphore("s_" + e))
        for q, ns in self.nslots.items():
            for k in range(min(ns, self.dma_count[q])):
                sems[(q, k)] = stack.enter_context(nc.semaphore("d_%s%d" % (q, k)))
        cnt = {k: 0 for k in sems}
        sig = [None] * n
        for i, (eng, fn, R, W, dma) in enumerate(ops):
            if dma:
                k = dma[2]
                cnt[k] += 16
                sig[i] = (k, cnt[k])
            elif i in needs:
                cnt[eng] += 1
                sig[i] = (eng, cnt[eng])
        final = dict(cnt)
        streams = {e: [] for e in ("pe", "act", "dve", "pool", "sp")}
        seen = {e: {} for e in streams}
        nw = 0
        for i, (eng, fn, R, W, dma) in enumerate(ops):
            waits = {}
            se = seen[eng]
            for d in deps_all[i]:
                k, v = sig[d]
                if se.get(k, 0) < v and waits.get(k, 0) < v:
                    waits[k] = v
            for k, v in waits.items():
                se[k] = v
            nw += len(waits)
            streams[eng].append((i, waits))
        self.nwaits = nw
        self.final = final
        block = stack.enter_context(nc.Block())

        def run_stream(e, eng):
            for i, waits in streams[e]:
                for k, v in waits.items():
                    eng.wait_ge(sems[k], v)
                _, fn, _, _, dma = ops[i]
                if dma:
                    ins = eng.dma_start(out=dma[0], in_=dma[1], **dma[3])
                    ins.then_inc(sems[sig[i][0]], 16)
                else:
                    ins = fn(eng)
                    if sig[i] is not None:
                        ins.then_inc(sems[sig[i][0]], 1)
            for k, v in final.items():
                if isinstance(k, tuple) and k[0] == e and v > 0:
                    eng.wait_ge(sems[k], v)

        @block.sync
        def _(eng):
            run_stream("sp", eng)

        @block.tensor
        def _(eng):
            run_stream("pe", eng)

        @block.scalar
        def _(eng):
            run_stream("act", eng)

        @block.vector
        def _(eng):
            run_stream("dve", eng)

        @block.gpsimd
        def _(eng):
            run_stream("pool", eng)


def _rope_tables(pos, kscale):
    half = 64
    inv = (np.float32(10000.0) ** (-np.arange(half, dtype=np.float32) / np.float32(half))).astype(np.float32)
    ang = (pos.astype(np.float32)[:, None] * inv[None, :]).astype(np.float32)
    cos = np.cos(ang.astype(np.float64))
    sin = np.sin(ang.astype(np.float64))
    n = len(pos)
    cs = np.zeros((n, 2, 2, 64), np.float64)
    sc = np.zeros((n, 2, 2, 64), np.float64)
    cs[:, 0, 0], cs[:, 0, 1] = cos, sin
    sc[:, 0, 0], sc[:, 0, 1] = sin, cos
    cs[:, 1, 0], cs[:, 1, 1] = cos * kscale, sin * kscale
    sc[:, 1, 0], sc[:, 1, 1] = sin * kscale, cos * kscale
    return cs.reshape(n, 256).astype(np.float32), sc.reshape(n, 256).astype(np.float32)


def _consts(seqlen):
    c = {}
    c["ident"] = np.eye(128, dtype=np.float32)
    j = np.arange(128)[:, None]
    i = np.arange(128)[None, :]
    c["tri"] = (j <= i).astype(np.float32)
    c["ones"] = np.ones((128, 128), np.float32)
    c["maskneg"] = np.where(i >= j, 0.0, -30000.0).astype(np.float32)
    g = np.array(GAM, np.float64)
    m = np.zeros((128, 8, 128), np.float64)
    for h in range(8):
        m[:, h, :] = np.where(i >= j, g[h] ** (-(j + 1.0)), 0.0)
    c["mret"] = m.astype(np.float32)
    c["qdec"] = (g[None, :] ** (np.arange(128)[:, None] + 1.0)).astype(np.float32)
    c["kdec"] = (g[None, :] ** (127.0 - np.arange(128)[:, None])).astype(np.float32)
    c["cdec"] = np.broadcast_to((g ** 128.0)[None, :], (128, 8)).astype(np.float32).copy()
    ks = 128.0 ** -0.5
    c["cs_p"], c["sc_p"] = _rope_tables(np.arange(seqlen), ks)
    cs_s, sc_s = _rope_tables(np.array([PAST_LEN]), ks)
    c["cs_s"] = np.broadcast_to(cs_s, (128, 256)).copy()
    c["sc_s"] = np.broadcast_to(sc_s, (128, 256)).copy()
    c["gam_p"] = np.repeat(np.array(GAM, np.float32), SB)[:, None].copy()
    p = np.arange(128)
    c["selb"] = (p[:, None] % SB == p[None, :] % SB).astype(np.float32)
    sel = np.zeros((4, 64, 128), np.float32)
    for hh in range(8):
        gidx = hh // 2
        for b in range(SB):
            for which, base in ((0, 0), (1, 2)):
                blk = base + gidx // 2
                sel[which * 2 + (gidx % 2), blk * SB + b, hh * SB + b] = 1.0
    c["sel4"] = sel
    return c


def build_program(depth=DEPTH, nseg=SEQ // SEGT, debug=False):
    L = depth
    seqlen = nseg * SEGT
    nc = bass.Bass("TRN2", target_bir_lowering=False)

    def din(name, shape):
        return nc.dram_tensor(name, list(shape), F32, kind="ExternalInput").ap()

    def dout(name, shape):
        return nc.dram_tensor(name, list(shape), F32, kind="ExternalOutput").ap()

    xp = din("xp", [seqlen, D_MODEL])
    xs = din("xs", [SB, D_MODEL])
    st_ret = din("st_ret", [L, 128, 128, 128])
    st_sc = din("st_sc", [L, SB, 2, 1024])
    st_sconv = din("st_sconv", [L, SB, 3, 3072])
    st_ssm = din("st_ssm", [L, 128, 4, 64, 128])
    w_in = din("w_in", [L, D_MODEL, D_PROJ])
    w_out = din("w_out", [L, 4096, D_MODEL])
    norm_pre = din("norm_pre", [L, 2048])
    norm_post = din("norm_post", [L, 2048])
    ret_norm = din("ret_norm", [L, 1024])
    sc_w = din("sc_conv_w", [L, 3, 1024])
    sc_b = din("sc_conv_b", [L, 1024])
    ssm_w = din("ssm_conv_w", [L, 4, 3072])
    ssm_b = din("ssm_conv_b", [L, 3072])
    dt_bias = din("ssm_dt_bias", [L, 32])
    a_log = din("ssm_a_log", [L, 32])
    d_skip = din("ssm_d", [L, 32])
    ssm_norm = din("ssm_norm", [L, 2048])
    cn = {}
    for name, shp in (("ident", [128, 128]), ("tri", [128, 128]), ("ones", [128, 128]), ("maskneg", [128, 128]),
                      ("mret", [128, 8, 128]), ("qdec", [128, 8]), ("kdec", [128, 8]), ("cdec", [128, 8]),
                      ("cs_p", [seqlen, 256]), ("sc_p", [seqlen, 256]), ("cs_s", [128, 256]), ("sc_s", [128, 256]),
                      ("gam_p", [128, 1]), ("selb", [128, 128]), ("sel4", [4, 64, 128])):
        cn[name] = din("c_" + name, shp)

    yp = dout("yp", [seqlen, D_MODEL])
    ys = dout("ys", [SB, D_MODEL])
    ret_p = dout("ret_p", [L, 8, 128, 128])
    sc_p = dout("sc_p", [L, 2, 1024])
    sconv_p = dout("sconv_p", [L, 3, 3072])
    ssm_p = dout("ssm_p", [L, 32, 64, 128])
    ret_s = dout("ret_s", [L, 128, 128, 128])
    sc_s = dout("sc_s", [L, SB, 2, 1024])
    sconv_s = dout("sconv_s", [L, SB, 3, 3072])
    ssm_s = dout("ssm_s", [L, 128, 4, 64, 128])

    xres = nc.dram_tensor("xres", [seqlen, D_MODEL], F32).ap()
    xsres = nc.dram_tensor("xsres", [SB, D_MODEL], F32).ap()
    pj = nc.dram_tensor("pj", [SB, D_PROJ + 96], F32).ap()

    T_xres = [Track() for _ in range(nseg * NT)]
    T_xsres = Track()
    T_pj = [Track() for _ in range(106)]
    T_out = Track()

    with ExitStack() as st:
        S = Sched(nc)
        NW = 53200
        art = st.enter_context(nc.sbuf_tensor("arena", [128, NW], F32))
        A = Arena(art, NW)
        PB = [st.enter_context(nc.psum_tensor("pb%d" % i, [128, 512], F32)) for i in range(8)]
        PT = [Track() for _ in range(8)]

        def pbf(i):
            return PB[i][:, :].bitcast(BF16)

        def TT(eng, out, in0, in1, op, R, W):
            S.op(eng, lambda e: e.tensor_tensor(out=out, in0=in0, in1=in1, op=op), R, W)

        def STT(eng, out, in0, scalar, in1, op0, op1, R, W):
            S.op(eng, lambda e: e.scalar_tensor_tensor(out=out, in0=in0, scalar=scalar, in1=in1, op0=op0, op1=op1), R, W)

        def TS(eng, out, in0, s1, s2, op0, op1, R, W):
            if s2 is None:
                S.op(eng, lambda e: e.tensor_scalar(out=out, in0=in0, scalar1=s1, scalar2=None, op0=op0), R, W)
            else:
                S.op(eng, lambda e: e.tensor_scalar(out=out, in0=in0, scalar1=s1, scalar2=s2, op0=op0, op1=op1), R, W)

        def ACT(out, in_, func, R, W, bias=None, scale=None, accum=None):
            kw = {}
            if bias is not None:
                kw["bias"] = bias
            if scale is not None:
                kw["scale"] = scale
            if accum is not None:
                kw["accum_out"] = accum
            S.op("act", lambda e: e.activation(out=out, in_=in_, func=func, **kw), R, W)

        def CP(eng, out, in_, R, W):
            if eng == "act":
                S.op("act", lambda e: e.activation(out=out, in_=in_, func=AF.Copy), R, W)
            else:
                S.op(eng, lambda e: e.tensor_copy(out=out, in_=in_), R, W)

        def MSET(eng, out, val, W):
            S.op(eng, lambda e: e.memset(out, val), (), W)

        def MM(bank, mms, R, W=()):
            def fn(e):
                ins = None
                for (o, l, r, s0, s1) in mms:
                    ins = e.matmul(o, lhsT=l, rhs=r, start=s0, stop=s1)
                return ins
            S.op("pe", fn, R, [PT[bank]] + list(W))

        def MMACC(bank, out, pairs, R):
            n = len(pairs)
            MM(bank, [(out, l, r, k == 0, k == n - 1) for k, (l, r) in enumerate(pairs)], R)

        def TRS(bank, items, R):
            def fn(e):
                ins = None
                for (o, i_, idn) in items:
                    ins = e.transpose(out=o, in_=i_, identity=idn)
                return ins
            S.op("pe", fn, R, [PT[bank]])

        def RSTD(out, in_, scale, R, W, tmp):
            TS("dve", tmp.ap, in_, scale, EPS, ALU.mult, ALU.add, R, [tmp])
            TT("pool", out, tmp.ap, mhalf.ap[0:tmp.ap.shape[0], 0:1], ALU.pow, [tmp, mhalf], W)

        hT = A.tile(KC * SEGT // 2, BF16, "p (c t) -> p c t", c=KC)
        mixT = A.tile(32 * SEGT // 2, BF16, "p (c t) -> p c t", c=32)
        WB = [A.tile(4096, BF16), A.tile(4096, BF16)]
        identb = A.tile(64, BF16)
        identf = A.tile(128)
        tri = A.tile(128)
        ones = A.tile(128)
        maskneg = A.tile(64, BF16)
        qdec = A.tile(8)
        kdec = A.tile(8)
        mhalf = A.tile(8)
        cdec = A.tile(8)
        gpreT = A.tile(16)
        gretT = A.tile(8)
        gssmT = A.tile(16)
        scwT = A.tile(24, F32, "p (c j) -> p c j", j=3)
        scbT = A.tile(8)
        ssmwT = A.tile(96, F32, "p (c j) -> p c j", j=4)
        ssmbT = A.tile(24)
        dtb_bc = A.tile(32)
        A_bc = A.tile(32)
        D_bc = A.tile(32)
        S_ret = A.tile(1024, F32, "p (h e) -> p h e", h=8)
        S_bf = A.tile(512, BF16, "p (h e) -> p h e", h=8)
        sT = A.tile(2048)
        sT_bf = A.tile(1024, BF16)
        Ucar = A.tile(16, F32, "p (c j) -> p c j", j=2)
        XBcar = A.tile(72, F32, "p (c j) -> p c j", j=3)
        hTs = A.tile(KC * SB // 2, BF16, "p (c t) -> p c t", c=KC)
        mixTs = A.tile(32 * SB // 2, BF16, "p (c t) -> p c t", c=32)
        gam_p = A.tile(1)
        selb = A.tile(128)
        sel4 = A.tile(512, F32, "p (s c) -> p s c", s=4, parts=64)
        wpar = [0]

        def wb_next():
            w = WB[wpar[0]]
            wpar[0] ^= 1
            return w

        S.dma("pool", identb.ap, cn["ident"], writes=[identb])
        S.dma("sp", identf.ap, cn["ident"], writes=[identf])
        S.dma("sp", tri.ap, cn["tri"], writes=[tri])
        S.dma("sp", ones.ap, cn["ones"], writes=[ones])
        S.dma("pool", maskneg.ap, cn["maskneg"], writes=[maskneg])
        S.dma("sp", qdec.ap[:, 0:8], cn["qdec"], writes=[qdec])
        S.dma("sp", kdec.ap[:, 0:8], cn["kdec"], writes=[kdec])
        S.dma("sp", cdec.ap[:, 0:8], cn["cdec"], writes=[cdec])
        S.dma("sp", gam_p.ap[:, 0:1], cn["gam_p"], writes=[gam_p])
        S.dma("sp", selb.ap, cn["selb"], writes=[selb])
        S.dma("sp", sel4.ap, cn["sel4"].rearrange("s p c -> p s c"), writes=[sel4])
        MSET("dve", mhalf.ap, -0.5, [mhalf])

        def load_layer_params(l):
            S.dma("sp", gpreT.ap[:, 0:16], norm_pre[l].rearrange("(c p) -> p c", p=128), writes=[gpreT], allow_slow_non_contiguous=True)
            S.dma("sp", gretT.ap[:, 0:8], ret_norm[l].rearrange("(c p) -> p c", p=128), writes=[gretT], allow_slow_non_contiguous=True)
            S.dma("sp", gssmT.ap[:, 0:16], ssm_norm[l].rearrange("(c p) -> p c", p=128), writes=[gssmT], allow_slow_non_contiguous=True)
            for j in range(3):
                S.dma("sp", scwT.ap[:, :, j], sc_w[l][j].rearrange("(c p) -> p c", p=128), writes=[scwT], allow_slow_non_contiguous=True)
            S.dma("sp", scbT.ap[:, 0:8], sc_b[l].rearrange("(c p) -> p c", p=128), writes=[scbT], allow_slow_non_contiguous=True)
            for j in range(4):
                S.dma("sp", ssmwT.ap[:, :, j], ssm_w[l][j].rearrange("(c p) -> p c", p=128), writes=[ssmwT], allow_slow_non_contiguous=True)
            S.dma("sp", ssmbT.ap[:, 0:24], ssm_b[l].rearrange("(c p) -> p c", p=128), writes=[ssmbT], allow_slow_non_contiguous=True)
            S.dma("sp", dtb_bc.ap[:, 0:32], dt_bias[l:l + 1, :].partition_broadcast(128), writes=[dtb_bc])
            S.dma("sp", A_bc.ap[:, 0:32], a_log[l:l + 1, :].partition_broadcast(128), writes=[A_bc])
            S.dma("sp", D_bc.ap[:, 0:32], d_skip[l:l + 1, :].partition_broadcast(128), writes=[D_bc])
            ACT(A_bc.ap[:, 0:32], A_bc.ap[:, 0:32], AF.Exp, [A_bc], [A_bc])
            TS("dve", A_bc.ap[:, 0:32], A_bc.ap[:, 0:32], -1.0, None, ALU.mult, None, [A_bc], [A_bc])
            MSET("dve", S_ret.ap, 0.0, [S_ret])
            MSET("dve", S_bf.ap, 0.0, [S_bf])
            MSET("dve", sT.ap, 0.0, [sT])
            MSET("dve", sT_bf.ap, 0.0, [sT_bf])
            MSET("dve", Ucar.ap, 0.0, [Ucar])
            MSET("dve", XBcar.ap, 0.0, [XBcar])

        def build_hT(l, seg):
            m = A.mark()
            xt = [A.tile(2048), A.tile(2048)]
            xn = [A.tile(1024, BF16), A.tile(1024, BF16)]
            ssq = [A.tile(1), A.tile(1)]
            rs = [A.tile(1), A.tile(1)]
            tmp = [A.tile(1), A.tile(1)]
            for t in range(NT):
                p = t % 2
                r0 = seg * SEGT + t * 128
                if l == 0:
                    S.dma("sp", xt[p].ap, xp[r0:r0 + 128, :], writes=[xt[p]])
                else:
                    S.dma("sp", xt[p].ap, xres[r0:r0 + 128, :], reads=[T_xres[seg * NT + t]], writes=[xt[p]])
                ACT(xn[p].ap, xt[p].ap, AF.Square, [xt[p]], [xn[p], ssq[p]], accum=ssq[p].ap[:, 0:1])
                RSTD(rs[p].ap[:, 0:1], ssq[p].ap[:, 0:1], 1.0 / D_MODEL, [ssq[p]], [rs[p]], tmp[p])
                ACT(xn[p].ap, xt[p].ap, AF.Copy, [xt[p], rs[p]], [xn[p]], scale=rs[p].ap[:, 0:1])
                for q in range(4):
                    bank = 6 + (q % 2)
                    TRS(bank, [(pbf(bank)[:, k * 128:(k + 1) * 128], xn[p].ap[:, (q * 4 + k) * 128:(q * 4 + k + 1) * 128], identb.ap)
                               for k in range(4)], [xn[p], identb])
                    for k in range(4):
                        c = q * 4 + k
                        dst = hT.sub(hT.ap[:, c, t * 128:(t + 1) * 128], (c * SEGT + t * 128) // 2, 64)
                        if k % 2 == 0:
                            ACT(dst.ap, pbf(bank)[:, k * 128:(k + 1) * 128], AF.Copy, [gpreT], [PT[bank], dst], scale=gpreT.ap[:, c:c + 1])
                        else:
                            TS("dve", dst.ap, pbf(bank)[:, k * 128:(k + 1) * 128], gpreT.ap[:, c:c + 1], None, ALU.mult, None,
                               [gpreT], [PT[bank], dst])
            A.release(m)

        def build_hTs(l):
            m = A.mark()
            xt = A.tile(2048, parts=SB)
            xn = A.tile(1024, BF16, parts=SB)
            ssq = A.tile(1, parts=SB)
            rs = A.tile(1, parts=SB)
            tmp = A.tile(1, parts=SB)
            if l == 0:
                S.dma("sp", xt.ap, xs, writes=[xt])
            else:
                S.dma("sp", xt.ap, xsres, reads=[T_xsres], writes=[xt])
            ACT(xn.ap, xt.ap, AF.Square, [xt], [xn, ssq], accum=ssq.ap[:, 0:1])
            RSTD(rs.ap[:, 0:1], ssq.ap[:, 0:1], 1.0 / D_MODEL, [ssq], [rs], tmp)
            ACT(xn.ap, xt.ap, AF.Copy, [xt, rs], [xn], scale=rs.ap[:, 0:1])
            for q in range(4):
                bank = 6 + (q % 2)
                TRS(bank, [(pbf(bank)[:, k * SB:(k + 1) * SB], xn.ap[:, (q * 4 + k) * 128:(q * 4 + k + 1) * 128], identb.ap[0:SB, 0:SB])
                           for k in range(4)], [xn, identb])
                for k in range(4):
                    c = q * 4 + k
                    ACT(hTs.ap[:, c, :], pbf(bank)[:, k * SB:(k + 1) * SB], AF.Copy, [gpreT], [PT[bank], hTs], scale=gpreT.ap[:, c:c + 1])
            A.release(m)

        def load_win_block(l, wb, cols):
            v = wb.ap.rearrange("p (c n) -> p c n", c=KC)
            src = w_in[l].rearrange("(c p) n -> p c n", p=128)
            o = 0
            for (c0, ncol) in cols:
                S.dma("pool", v[:, :, o:o + ncol], src[:, :, c0:c0 + ncol], writes=[wb])
                o += ncol
            return v

        PRE = {}

        def win_block(key, l, cols):
            if key in PRE:
                return PRE.pop(key)
            wb = wb_next()
            return wb, load_win_block(l, wb, cols)

        def prefetch_win(key, l, cols):
            wb = wb_next()
            PRE[key] = (wb, load_win_block(l, wb, cols))

        def out_block(key, l, ob):
            if key in PRE:
                return PRE.pop(key)
            wb = wb_next()
            v = wb.ap.rearrange("p (c n) -> p c n", c=32)
            S.dma("pool", v, w_out[l].rearrange("(c p) n -> p c n", p=128)[:, :, ob * 256:(ob + 1) * 256], writes=[wb])
            return wb, v

        def prefetch_out(key, l, ob):
            PRE[key] = out_block(("none",), l, ob)

        def ret_cols(h):
            return [(C_Q + h * 128, 128), (C_K + h * 128, 128), (C_V + h * 128, 128), (C_GR + h * 128, 128)]

        def sc_cols(cc):
            return [(C_BG + cc * 128, 128), (C_CG + cc * 128, 128), (C_H + cc * 128, 128), (C_GSC + cc * 128, 128)]

        def sample_proj(wb, v, cols, stage):
            ntot = sum(n for _, n in cols)
            MMACC(5, PB[5][0:SB, 0:ntot], [(hTs.ap[:, c, :], v[:, c, 0:ntot]) for c in range(KC)], [hTs, wb])
            CP("act", stage.ap[:, 0:ntot], PB[5][0:SB, 0:ntot], [], [PT[5], stage])
            o = 0
            for (c0, ncol) in cols:
                S.dma("sp", pj[:, c0:c0 + ncol], stage.ap[:, o:o + ncol], reads=[stage],
                      writes=T_pj[c0 // 128:(c0 + ncol + 127) // 128])
                o += ncol

        def ret_phase(l, seg, do_sample):
            m = A.mark()
            CS = A.tile(NT * 256, F32, "p (t x) -> p t x", t=NT)
            SC = A.tile(NT * 256, F32, "p (t x) -> p t x", t=NT)
            mret = A.tile(1024, F32, "p (h i) -> p h i", h=8)
            gret_bc = A.tile(1024)
            r0 = seg * SEGT
            S.dma("sp", CS.ap, cn["cs_p"][r0:r0 + SEGT, :].rearrange("(t p) x -> p t x", p=128), writes=[CS])
            S.dma("sp", SC.ap, cn["sc_p"][r0:r0 + SEGT, :].rearrange("(t p) x -> p t x", p=128), writes=[SC])
            S.dma("sp", mret.ap, cn["mret"], writes=[mret])
            S.dma("sp", gret_bc.ap, ret_norm[l:l + 1, :].partition_broadcast(128), writes=[gret_bc])
            stage = A.tile(512, parts=SB) if do_sample else None
            H = []
            for h in range(8):
                H.append(dict(
                    qkT=A.tile(2 * SEGT // 2, BF16, "p (a t) -> p a t", a=2),
                    qkr=A.tile(NT * 256 // 2, BF16, "p (t a d) -> p t a d", t=NT, a=2),
                    vbf=A.tile(NT * 128 // 2, BF16, "p (t e) -> p t e", t=NT),
                    vdec=A.tile(NT * 128 // 2, BF16, "p (t e) -> p t e", t=NT)))
            GS = [A.tile(NT * 512 // 2, BF16, "p (t i e) -> p t i e", t=NT, i=4) for _ in range(2)]
            ABCD_t = [A.tile(512) for _ in range(2)]
            ABCD = [dict(AB=t_.sub(t_.ap[:, 0:256], 0, 256), CD=t_.sub(t_.ap[:, 256:512], 256, 256)) for t_ in ABCD_t]
            pend_tr = [None]
            for h in range(8):
                Hh = H[h]
                cols = ret_cols(h)
                wb, v = win_block(("ret", h), l, cols)
                for t in range(NT):
                    bank = t % 4
                    P = PB[bank]
                    B = ABCD[t % 2]
                    MMACC(bank, P[:, 0:512], [(hT.ap[:, c, t * 128:(t + 1) * 128], v[:, c, :]) for c in range(KC)], [hT, wb])
                    if pend_tr[0] is not None:
                        pend_tr[0]()
                        pend_tr[0] = None
                    P4 = P[:, 0:256].rearrange("p (a b f) -> p a b f", a=2, b=2)
                    AB4 = B["AB"].ap.rearrange("p (a b f) -> p a b f", a=2, b=2)
                    CD4 = B["CD"].ap.rearrange("p (a b f) -> p a b f", a=2, b=2)
                    TT("dve", AB4, P4, CS.ap[:, t, :].rearrange("p (a b f) -> p a b f", a=2, b=2), ALU.mult, [CS], [PT[bank], B["AB"]])
                    TT("dve", CD4, P4, SC.ap[:, t, :].rearrange("p (a b f) -> p a b f", a=2, b=2), ALU.mult, [SC], [PT[bank], B["CD"]])
                    TT("dve", Hh["qkr"].ap[:, t, :, 0:64], AB4[:, :, 0, :], AB4[:, :, 1, :], ALU.subtract, [B["AB"]], [Hh["qkr"]])
                    TT("dve", Hh["qkr"].ap[:, t, :, 64:128], CD4[:, :, 0, :], CD4[:, :, 1, :], ALU.add, [B["CD"]], [Hh["qkr"]])
                    ACT(Hh["vbf"].ap[:, t, :], P[:, 256:384], AF.Copy, [], [PT[bank], Hh["vbf"]])
                    ACT(Hh["vdec"].ap[:, t, :], P[:, 256:384], AF.Copy, [kdec], [PT[bank], Hh["vdec"]], scale=kdec.ap[:, h:h + 1])
                    ACT(GS[h // 4].ap[:, t, h % 4, :], P[:, 384:512], AF.Silu, [], [PT[bank], GS[h // 4]])
                    def tr_(Hh=Hh, t=t):
                        tb = 6 + (t % 2)
                        TRS(tb, [(pbf(tb)[:, a * 128:(a + 1) * 128], Hh["qkr"].ap[:, t, a, :], identb.ap) for a in range(2)], [Hh["qkr"], identb])
                        CP("act", Hh["qkT"].ap[:, :, t * 128:(t + 1) * 128], pbf(tb)[:, 0:256].rearrange("p (a i) -> p a i", a=2), [], [PT[tb], Hh["qkT"]])
                    pend_tr[0] = tr_
                if do_sample:
                    sample_proj(wb, v, cols, stage)
            if pend_tr[0] is not None:
                pend_tr[0]()
                pend_tr[0] = None
            Pm = [A.tile(256, BF16, "p (i j) -> p i j", i=4) for _ in range(2)]
            osb = [A.tile(512, F32, "p (i e) -> p i e", i=4) for _ in range(2)]
            sq = [Tile(t_.ap.rearrange("p (i e) -> p i e", i=4), t_.tracks, t_.arena, t_.off) for t_ in ABCD_t]
            og = [A.tile(256, BF16, "p (i e) -> p i e", i=4) for _ in range(2)]
            st4 = []
            for _ in range(2):
                t_ = A.tile(24)
                st4.append({nm: t_.sub(t_.ap[:, 4 * k_:4 * k_ + 4], 0, 24) for k_, nm in enumerate(("s1", "s2", "mean", "msq", "var", "rs"))})
            b4 = lambda ap: ap.rearrange("p (i e) -> p i e", i=4)

            def finalize(c, qd):
                sl = slice(c * 128, (c + 1) * 128)
                h0 = qd * 4
                bsc = 4 if qd == 0 else 7
                TRS(bsc, [(pbf(bsc)[:, i * 128:(i + 1) * 128], og[qd].ap[:, i, :], identb.ap) for i in range(4)], [og[qd], identb])
                dsts = [mixT.sub(None, ((h0 + i) * SEGT + c * 128) // 2, 64) for i in range(4)]
                CP("act", mixT.ap[:, h0:h0 + 4, sl], pbf(bsc)[:, 0:512].rearrange("p (i e) -> p i e", i=4), [], [PT[bsc]] + dsts)

            def unit(c, qd, prev):
                    if prev is not None:
                        finalize(*prev)
                    sl = slice(c * 128, (c + 1) * 128)
                    h0 = qd * 4
                    HQ = H[h0:h0 + 4]
                    bsc, bst, bo = (4 if qd == 0 else 7), 5, 6
                    MM(bsc, [(PB[bsc][:, i * 128:(i + 1) * 128], HQ[i]["qkT"].ap[:, 1, sl], HQ[i]["qkT"].ap[:, 0, sl], True, True) for i in range(4)],
                       [x["qkT"] for x in HQ])
                    TT("dve", Pm[qd].ap, b4(PB[bsc][:, 0:512]), mret.ap[:, h0:h0 + 4, :], ALU.mult, [mret], [PT[bsc], Pm[qd]])
                    MM(bst, [(PB[bst][:, i * 128:(i + 1) * 128], HQ[i]["qkr"].ap[:, c, 1, :], HQ[i]["vdec"].ap[:, c, :], True, True) for i in range(4)],
                       [x["qkr"] for x in HQ] + [x["vdec"] for x in HQ])
                    mms = []
                    for i in range(4):
                        o = PB[bo][:, i * 128:(i + 1) * 128]
                        mms.append((o, Pm[qd].ap[:, i, :], HQ[i]["vbf"].ap[:, c, :], True, False))
                        mms.append((o, HQ[i]["qkT"].ap[:, 0, sl], S_bf.ap[:, h0 + i, :], False, True))
                    MM(bo, mms, [Pm[qd], S_bf] + [x["vbf"] for x in HQ] + [x["qkT"] for x in HQ])
                    TT("dve", S_ret.ap[:, h0:h0 + 4, :], S_ret.ap[:, h0:h0 + 4, :], cdec.ap[:, h0:h0 + 4].unsqueeze(2).to_broadcast([128, 4, 128]), ALU.mult, [cdec], [S_ret])
                    TT("dve", S_ret.ap[:, h0:h0 + 4, :], S_ret.ap[:, h0:h0 + 4, :], b4(PB[bst][:, 0:512]), ALU.add, [], [PT[bst], S_ret])
                    CP("act", S_bf.ap[:, h0:h0 + 4, :], S_ret.ap[:, h0:h0 + 4, :], [S_ret], [S_bf])
                    s = st4[qd]
                    TT("dve", osb[qd].ap, b4(PB[bo][:, 0:512]), qdec.ap[:, h0:h0 + 4].unsqueeze(2).to_broadcast([128, 4, 128]), ALU.mult, [qdec], [PT[bo], osb[qd]])
                    S.op("dve", lambda e, o=s["s1"].ap, i_=osb[qd].ap: e.tensor_reduce(out=o, in_=i_, axis=AX.X, op=ALU.add), [osb[qd]], [s["s1"]])
                    ACT(sq[qd].ap, osb[qd].ap, AF.Square, [osb[qd]], [sq[qd]])
                    S.op("dve", lambda e, o=s["s2"].ap, i_=sq[qd].ap: e.tensor_reduce(out=o, in_=i_, axis=AX.X, op=ALU.add), [sq[qd]], [s["s2"]])
                    TS("dve", s["mean"].ap, s["s1"].ap, 1.0 / 128, None, ALU.mult, None, [s["s1"]], [s["mean"]])
                    TT("dve", s["msq"].ap, s["mean"].ap, s["mean"].ap, ALU.mult, [s["mean"]], [s["msq"]])
                    STT("dve", s["var"].ap, s["s2"].ap, 1.0 / 128, s["msq"].ap, ALU.mult, ALU.subtract, [s["s2"], s["msq"]], [s["var"]])
                    TS("dve", s["var"].ap, s["var"].ap, EPS, None, ALU.add, None, [], [s["var"]])
                    TT("pool", s["rs"].ap, s["var"].ap, mhalf.ap[:, 0:4], ALU.pow, [s["var"], mhalf], [s["rs"]])
                    TT("dve", osb[qd].ap, osb[qd].ap, s["mean"].ap.unsqueeze(2).to_broadcast([128, 4, 128]), ALU.subtract, [s["mean"]], [osb[qd]])
                    TT("dve", osb[qd].ap, osb[qd].ap, s["rs"].ap.unsqueeze(2).to_broadcast([128, 4, 128]), ALU.mult, [s["rs"]], [osb[qd]])
                    TT("dve", osb[qd].ap, osb[qd].ap, b4(gret_bc.ap[:, h0 * 128:(h0 + 4) * 128]), ALU.mult, [gret_bc], [osb[qd]])
                    TT("dve", og[qd].ap, osb[qd].ap, GS[qd].ap[:, c, :, :], ALU.mult, [osb[qd], GS[qd]], [og[qd]])

            items = [(c, qd) for c in range(NT) for qd in range(2)]
            units = []
            for k, (c, qd) in enumerate(items):
                prev = items[k - 1] if k > 0 else None
                units.append(lambda c=c, qd=qd, prev=prev: unit(c, qd, prev))
            units.append(lambda: finalize(*items[-1]))
            return m, units, stage

        def sc_phase(l, seg, do_sample, units, stage):
            m = A.mark()
            bufs = []
            for par in range(1):
                bufs.append(dict(cg=A.tile(512), U=A.tile(SEGT + 2), acc=A.tile(512), sg=A.tile(256, BF16), bgc=A.tile(256, BF16)))
            for cc in range(8):
                B = bufs[0]
                cols = sc_cols(cc)
                wb, v = win_block(("sc", cc), l, cols)
                if cc + 1 < 8:
                    prefetch_win(("sc", cc + 1), l, sc_cols(cc + 1))
                else:
                    prefetch_win(("xbc", 0), l, [(C_XBC, 512)])
                bk = [j for j in range(4)]
                for j in range(4):
                    MMACC(bk[j], PB[bk[j]][:, 0:SEGT], [(v[:, c, j * 128:(j + 1) * 128], hT.ap[:, c, :]) for c in range(KC)], [hT, wb])
                CP("act", B["cg"].ap, PB[bk[1]][:, 0:SEGT], [], [PT[bk[1]], B["cg"]])
                ACT(B["sg"].ap, PB[bk[3]][:, 0:SEGT], AF.Silu, [], [PT[bk[3]], B["sg"]])
                CP("act", B["bgc"].ap, PB[bk[0]][:, 0:SEGT], [], [PT[bk[0]], B["bgc"]])
                CP("dve", B["U"].ap[:, 0:2], Ucar.ap[:, cc, :], [Ucar], [B["U"]])
                TT("dve", B["U"].ap[:, 2:2 + SEGT], B["cg"].ap, PB[bk[2]][:, 0:SEGT], ALU.mult, [B["cg"]], [PT[bk[2]], B["U"]])
                if cc < len(units):
                    units[cc]()
                CP("dve", Ucar.ap[:, cc, :], B["U"].ap[:, SEGT:SEGT + 2], [B["U"]], [Ucar])
                TS("dve", B["acc"].ap, B["U"].ap[:, 0:SEGT], scwT.ap[:, cc, 0:1], scbT.ap[:, cc:cc + 1], ALU.mult, ALU.add, [B["U"], scwT, scbT], [B["acc"]])
                STT("dve", B["acc"].ap, B["U"].ap[:, 1:1 + SEGT], scwT.ap[:, cc, 1:2], B["acc"].ap, ALU.mult, ALU.add, [B["U"], scwT], [B["acc"]])
                STT("dve", B["acc"].ap, B["U"].ap[:, 2:2 + SEGT], scwT.ap[:, cc, 2:3], B["acc"].ap, ALU.mult, ALU.add, [B["U"], scwT], [B["acc"]])
                TT("dve", B["acc"].ap, B["acc"].ap, B["sg"].ap, ALU.mult, [B["sg"]], [B["acc"]])
                dst = mixT.sub(mixT.ap[:, 8 + cc, :], ((8 + cc) * SEGT) // 2, SEGT // 2)
                TT("dve", dst.ap, B["acc"].ap, B["bgc"].ap, ALU.mult, [B["acc"], B["bgc"]], [dst])
                if do_sample:
                    sample_proj(wb, v, cols, stage)
            for u in units[8:]:
                u()
            A.release(m)

        def ssd_phase(l, seg, do_sample):
            m = A.mark()
            stage = A.tile(512, parts=SB) if do_sample else None
            XS = A.tile(NT * 2048 // 2, BF16, "p (t x) -> p t x", t=NT)
            SZ = A.tile(NT * 2048 // 2, BF16, "p (t x) -> p t x", t=NT)
            BMT = A.tile(4 * SEGT // 2, BF16, "p (g t) -> p g t", g=4)
            CMT = A.tile(4 * SEGT // 2, BF16, "p (g t) -> p g t", g=4)
            BM = A.tile(NT * 512 // 2, BF16, "p (t g n) -> p t g n", t=NT, g=4)
            DT = A.tile(NT * 32, F32, "p (t h) -> p t h", t=NT)
            AA = A.tile(NT * 32, F32, "p (t h) -> p t h", t=NT)
            NACUM = A.tile(NT * 32, F32, "p (t h) -> p t h", t=NT)
            NAA = A.tile(NT * 32, F32, "p (t h) -> p t h", t=NT)
            EA = A.tile(NT * 32, F32, "p (t h) -> p t h", t=NT)
            WJ = A.tile(NT * 32, F32, "p (t h) -> p t h", t=NT)
            ELAST = A.tile(NT * 32, F32, "p (t h) -> p t h", t=NT)
            wdt = A.tile(KC * 32 // 2, BF16, "p (c n) -> p c n", c=KC)
            t32 = [A.tile(32), A.tile(32), A.tile(32)]
            S.dma("pool", wdt.ap, w_in[l].rearrange("(c p) n -> p c n", p=128)[:, :, C_DT:C_DT + 32], writes=[wdt])
            if do_sample:
                MMACC(5, PB[5][0:SB, 0:32], [(hTs.ap[:, c, :], wdt.ap[:, c, :]) for c in range(KC)], [hTs, wdt])
                CP("act", stage.ap[:, 0:32], PB[5][0:SB, 0:32], [], [PT[5], stage])
                S.dma("sp", pj[:, C_DT:C_DT + 32], stage.ap[:, 0:32], reads=[stage], writes=[T_pj[104]])
            for t in range(NT):
                tsl = slice(t * 128, (t + 1) * 128)
                MMACC(0, PB[0][:, 0:32], [(hT.ap[:, c, tsl], wdt.ap[:, c, :]) for c in range(KC)], [hT, wdt])
                xdt, ax, lg = t32
                TT("dve", xdt.ap[:, 0:32], PB[0][:, 0:32], dtb_bc.ap[:, 0:32], ALU.add, [dtb_bc], [PT[0], xdt])
                STT("dve", ax.ap[:, 0:32], xdt.ap[:, 0:32], -1.0, xdt.ap[:, 0:32], ALU.mult, ALU.max, [xdt], [ax])
                ACT(ax.ap[:, 0:32], ax.ap[:, 0:32], AF.Exp, [ax], [ax], scale=-1.0)
                ACT(lg.ap[:, 0:32], ax.ap[:, 0:32], AF.Ln, [ax], [lg], bias=1.0)
                STT("dve", DT.ap[:, t, :], xdt.ap[:, 0:32], 0.0, lg.ap[:, 0:32], ALU.max, ALU.add, [xdt, lg], [DT])
                TT("dve", AA.ap[:, t, :], DT.ap[:, t, :], A_bc.ap[:, 0:32], ALU.mult, [DT, A_bc], [AA])
                TS("dve", NAA.ap[:, t, :], AA.ap[:, t, :], -1.0, None, ALU.mult, None, [AA], [NAA])
                MM(1, [(PB[1][:, 0:32], tri.ap, AA.ap[:, t, :], True, True),
                       (PB[1][:, 32:64], ones.ap, AA.ap[:, t, :], True, True)], [tri, ones, AA])
                TS("dve", NACUM.ap[:, t, :], PB[1][:, 0:32], -1.0, None, ALU.mult, None, [], [PT[1], NACUM])
                ACT(EA.ap[:, t, :], PB[1][:, 0:32], AF.Exp, [], [PT[1], EA])
                ACT(ELAST.ap[:, t, :], PB[1][:, 32:64], AF.Exp, [], [PT[1], ELAST])
                TT("dve", ax.ap[:, 0:32], PB[1][:, 32:64], NACUM.ap[:, t, :], ALU.add, [NACUM], [PT[1], ax])
                ACT(ax.ap[:, 0:32], ax.ap[:, 0:32], AF.Exp, [ax], [ax])
                TT("dve", WJ.ap[:, t, :], ax.ap[:, 0:32], DT.ap[:, t, :], ALU.mult, [ax, DT], [WJ])
            pend_d2 = [None]
            xb = [dict(XB=A.tile(SEGT + 3), acc=A.tile(SEGT), xc=A.tile(SEGT // 2, BF16)) for _ in range(2)]
            for bi in range(6):
                cols = [(C_XBC + bi * 512, 512)]
                wb, v = win_block(("xbc", bi), l, cols)
                for j in range(4):
                    gc = bi * 4 + j
                    B = xb[gc % 2]
                    bank = gc % 4
                    MMACC(bank, PB[bank][:, 0:SEGT], [(v[:, c, j * 128:(j + 1) * 128], hT.ap[:, c, :]) for c in range(KC)], [hT, wb])
                    if pend_d2[0] is not None:
                        pend_d2[0]()
                        pend_d2[0] = None
                    CP("dve", B["XB"].ap[:, 0:3], XBcar.ap[:, gc, :], [XBcar], [B["XB"]])
                    CP("act", B["XB"].ap[:, 3:3 + SEGT], PB[bank][:, 0:SEGT], [], [PT[bank], B["XB"]])
                    CP("dve", XBcar.ap[:, gc, :], B["XB"].ap[:, SEGT:SEGT + 3], [B["XB"]], [XBcar])
                    TS("dve", B["acc"].ap, B["XB"].ap[:, 0:SEGT], ssmwT.ap[:, gc, 0:1], ssmbT.ap[:, gc:gc + 1], ALU.mult, ALU.add, [B["XB"], ssmwT, ssmbT], [B["acc"]])
                    for k in range(1, 4):
                        STT("dve", B["acc"].ap, B["XB"].ap[:, k:k + SEGT], ssmwT.ap[:, gc, k:k + 1], B["acc"].ap, ALU.mult, ALU.add, [B["XB"], ssmwT], [B["acc"]])
                    if gc < 16:
                        ACT(B["xc"].ap, B["acc"].ap, AF.Silu, [B["acc"]], [B["xc"]])

                        def tr2_(B=B, gc=gc):
                            tb = 6 + (gc % 2)
                            TRS(tb, [(pbf(tb)[:, t * 128:(t + 1) * 128], B["xc"].ap[:, t * 128:(t + 1) * 128], identb.ap) for t in range(NT)], [B["xc"], identb])
                            CP("act" if gc % 2 else "dve", XS.ap[:, :, gc * 128:(gc + 1) * 128], pbf(tb)[:, 0:NT * 128].rearrange("p (t c) -> p t c", t=NT), [], [PT[tb], XS])
                        pend_d2[0] = tr2_
                    elif gc < 20:
                        g = gc - 16
                        ACT(BMT.ap[:, g, :], B["acc"].ap, AF.Silu, [B["acc"]], [BMT])

                        def tr3_(g=g, gc=gc):
                            tb = 6 + (gc % 2)
                            TRS(tb, [(pbf(tb)[:, t * 128:(t + 1) * 128], BMT.ap[:, g, t * 128:(t + 1) * 128], identb.ap) for t in range(NT)], [BMT, identb])
                            CP("dve", BM.ap[:, :, g, :], pbf(tb)[:, 0:NT * 128].rearrange("p (t c) -> p t c", t=NT), [], [PT[tb], BM])
                        pend_d2[0] = tr3_
                    else:
                        g = gc - 20
                        ACT(CMT.ap[:, g, :], B["acc"].ap, AF.Silu, [B["acc"]], [CMT])
                if do_sample:
                    sample_proj(wb, v, cols, stage)
            if pend_d2[0] is not None:
                pend_d2[0]()
                pend_d2[0] = None
            for zb in range(4):
                wb = wb_next()
                cols = [(C_Z + zb * 512, 512)]
                v = load_win_block(l, wb, cols)
                for t in range(NT):
                    bank = t % 2
                    MMACC(bank, PB[bank][:, 0:512], [(hT.ap[:, c, t * 128:(t + 1) * 128], v[:, c, :]) for c in range(KC)], [hT, wb])
                    ACT(SZ.ap[:, t, zb * 512:(zb + 1) * 512], PB[bank][:, 0:512], AF.Silu, [], [PT[bank], SZ])
                if do_sample:
                    sample_proj(wb, v, cols, stage)
            prefetch_out(("out", 0), l, 0)
            prefetch_out(("out", 1), l, 1)
            cb = [A.tile(128), A.tile(128)]
            Lt = [A.tile(512), A.tile(512)]
            MT = [A.tile(256, BF16, "p (i j) -> p i j", i=4), A.tile(256, BF16, "p (i j) -> p i j", i=4)]
            XDT = [A.tile(1024, BF16)] * 2
            R4 = [A.tile(512), A.tile(512)]
            t1 = A.tile(512)
            t2 = A.tile(512)
            t2s = [t2, A.tile(512)]
            ssq4_t = A.tile(4)
            xw = [A.tile(256, BF16), A.tile(256, BF16)]
            YZ = A.tile(2048)
            YN = A.tile(1024, BF16)
            ssq = A.tile(1); rs = A.tile(1); tmp = A.tile(1)
            h8 = lambda ap: ap.rearrange("p (h q) -> p h q", h=8)

            def step1(c, g):
                sl = slice(c * 128, (c + 1) * 128)
                MM(0, [(PB[0][:, 0:128], BMT.ap[:, g, sl], CMT.ap[:, g, sl], True, True)], [BMT, CMT])
                ib = 1 if g % 2 == 0 else 7
                MM(ib, [(PB[ib][:, 0:512], CMT.ap[:, g, sl], sT_bf.ap[:, g * 512:(g + 1) * 512], True, True)], [CMT, sT_bf])
                for r in range(2):
                    bb = 2 + r
                    h0 = g * 8 + r * 4
                    TT("pool", R4[r].ap.rearrange("p (i j) -> p i j", i=4), tri.ap.unsqueeze(1).to_broadcast([128, 4, 128]),
                       AA.ap[:, c, h0:h0 + 4].unsqueeze(2).to_broadcast([128, 4, 128]), ALU.mult, [tri, AA], [R4[r]])
                    o = PB[bb][:, 0:512]
                    if r == 1:
                        gs_ = slice(g * 512, (g + 1) * 512)
                        hs = slice(g * 8, (g + 1) * 8)
                        TT("pool", h8(xw[g % 2].ap), h8(XS.ap[:, c, gs_]), WJ.ap[:, c, hs].unsqueeze(2).to_broadcast([128, 8, 64]), ALU.mult, [XS, WJ], [xw[g % 2]])
                        TT("pool", h8(t2s[g % 2].ap), h8(XS.ap[:, c, gs_]), D_bc.ap[:, hs].unsqueeze(2).to_broadcast([128, 8, 64]), ALU.mult, [XS, D_bc], [t2s[g % 2]])
                    MM(bb, [(o, ones.ap, R4[r].ap, True, False),
                            (o, tri.ap, NAA.ap[:, c, h0:h0 + 4].unsqueeze(2).to_broadcast([128, 4, 128]), False, False),
                            (o, identb.ap, maskneg.ap.unsqueeze(1).to_broadcast([128, 4, 128]), False, True)],
                       [ones, R4[r], tri, NAA, identb, maskneg])

            def step23(c, g):
                cbt = cb[g % 2]
                if g == 0:
                    TT("dve", XDT[0].ap.rearrange("p (h q) -> p h q", h=32), XS.ap[:, c, :].rearrange("p (h q) -> p h q", h=32),
                       DT.ap[:, c, :].unsqueeze(2).to_broadcast([128, 32, 64]), ALU.mult, [XS, DT], [XDT[0]])
                CP("act", cbt.ap, PB[0][:, 0:128], [], [PT[0], cbt])
                for r in range(2):
                    bb = 2 + r
                    ACT(Lt[r].ap, PB[bb][:, 0:512], AF.Exp, [], [PT[bb], Lt[r]])
                    TT("dve", MT[r].ap, Lt[r].ap.rearrange("p (i j) -> p i j", i=4), cbt.ap.unsqueeze(1).to_broadcast([128, 4, 128]), ALU.mult, [Lt[r], cbt], [MT[r]])

            def step4(c, g):
                yb = 4 + (g % 2)
                mms = []
                for r in range(2):
                    for i in range(4):
                        hh = r * 4 + i
                        h = g * 8 + hh
                        mms.append((PB[yb][:, hh * 64:(hh + 1) * 64], MT[r].ap[:, i, :], XDT[c % 2].ap[:, h * 64:(h + 1) * 64], True, True))
                MM(yb, mms, [MT[0], MT[1], XDT[c % 2]])
                MM(6, [(PB[6][:, 0:512], BM.ap[:, c, g, :], xw[g % 2].ap, True, True)], [BM, xw[g % 2]])

            def step56(c, g):
                gs_ = slice(g * 512, (g + 1) * 512)
                hs = slice(g * 8, (g + 1) * 8)
                yb = 4 + (g % 2)
                ib = 1 if g % 2 == 0 else 7
                TT("dve", h8(t1.ap), h8(PB[ib][:, 0:512]), EA.ap[:, c, hs].unsqueeze(2).to_broadcast([128, 8, 64]), ALU.mult, [EA], [PT[ib], t1])
                TT("dve", t1.ap, t1.ap, PB[yb][:, 0:512], ALU.add, [], [PT[yb], t1])
                TT("dve", t1.ap, t1.ap, t2s[g % 2].ap, ALU.add, [t2s[g % 2]], [t1])
                TT("dve", YZ.ap[:, gs_], t1.ap, SZ.ap[:, c, gs_], ALU.mult, [t1, SZ], [YZ])
                TT("dve", h8(sT.ap[:, gs_]), h8(sT.ap[:, gs_]), ELAST.ap[:, c, hs].unsqueeze(2).to_broadcast([128, 8, 64]), ALU.mult, [ELAST], [sT])
                TT("dve", sT.ap[:, gs_], sT.ap[:, gs_], PB[6][:, 0:512], ALU.add, [], [PT[6], sT])
                CP("act", sT_bf.ap[:, gs_], sT.ap[:, gs_], [sT], [sT_bf])

            def chunk_tail(c):
                sl = slice(c * 128, (c + 1) * 128)
                ssq4 = ssq4_t
                for q in range(4):
                    ACT(t1.ap, YZ.ap[:, q * 512:(q + 1) * 512], AF.Square, [YZ], [t1, ssq4], accum=ssq4.ap[:, q:q + 1])
                S.op("dve", lambda e, o=ssq.ap[:, 0:1], i=ssq4.ap[:, 0:4]: e.tensor_reduce(out=o, in_=i, axis=AX.X, op=ALU.add), [ssq4], [ssq])
                RSTD(rs.ap[:, 0:1], ssq.ap[:, 0:1], 1.0 / 2048, [ssq], [rs], tmp)
                ACT(YN.ap, YZ.ap, AF.Copy, [YZ, rs], [YN], scale=rs.ap[:, 0:1])
                for q in range(4):
                    tb = 7 if q % 2 == 0 else 6
                    TRS(tb, [(pbf(tb)[:, k * 128:(k + 1) * 128], YN.ap[:, (q * 4 + k) * 128:(q * 4 + k + 1) * 128], identb.ap) for k in range(4)], [YN, identb])
                    for k in range(4):
                        cc = q * 4 + k
                        dst = mixT.sub(mixT.ap[:, 16 + cc, sl], ((16 + cc) * SEGT + c * 128) // 2, 64)
                        if k % 2 == 0:
                            ACT(dst.ap, pbf(tb)[:, k * 128:(k + 1) * 128], AF.Copy, [gssmT], [PT[tb], dst], scale=gssmT.ap[:, cc:cc + 1])
                        else:
                            TS("dve", dst.ap, pbf(tb)[:, k * 128:(k + 1) * 128], gssmT.ap[:, cc:cc + 1], None, ALU.mult, None, [gssmT], [PT[tb], dst])

            items = [(c, g) for c in range(NT) for g in range(4)]
            step1(*items[0])
            step23(*items[0])
            for k, (c, g) in enumerate(items):
                if k + 1 < len(items):
                    step1(*items[k + 1])
                step4(c, g)
                step56(c, g)
                if g == 3:
                    chunk_tail(c)
                if k + 1 < len(items):
                    step23(*items[k + 1])
            A.release(m)

        def out_phase(l, seg, do_sample, hoist, nxt_l):
            m = A.mark()
            OUT = A.tile(NT * 2048, F32, "p (t x) -> p t x", t=NT)
            gpost = A.tile(2048)
            S.dma("sp", gpost.ap, norm_post[l:l + 1, :].partition_broadcast(128), writes=[gpost])
            outs = A.tile(2048, parts=SB) if do_sample else None
            last = (l == L - 1)
            for ob in range(8):
                wb, v = out_block(("out", ob), l, ob)
                for t in range(NT):
                    bank = t % 4
                    MMACC(bank, PB[bank][:, 0:256], [(mixT.ap[:, mc, t * 128:(t + 1) * 128], v[:, mc, :]) for mc in range(32)], [mixT, wb])
                    CP("act" if t % 2 else "dve", OUT.ap[:, t, ob * 256:(ob + 1) * 256], PB[bank][:, 0:256], [], [PT[bank], OUT])
                if do_sample:
                    MMACC(5, PB[5][0:SB, 0:256], [(mixTs.ap[:, mc, :], v[:, mc, :]) for mc in range(32)], [mixTs, wb])
                    CP("act", outs.ap[:, ob * 256:(ob + 1) * 256], PB[5][0:SB, 0:256], [], [PT[5], outs])
            xt = [A.tile(2048), A.tile(2048)]
            junk = A.tile(2048)
            ssq = [A.tile(1), A.tile(1)]; rs = [A.tile(1), A.tile(1)]; tmp = [A.tile(1), A.tile(1)]
            if nxt_l is not None:
                prefetch_win(("ret", 0), nxt_l, ret_cols(0))
                prefetch_win(("ret", 1), nxt_l, ret_cols(1))
            if hoist is not None:
                hoist()

            def finish(o_ap, o_tile, x_t, np_, src_ap, src_reads, dst_ap, dst_tracks, sq, r, tm, preloaded=False):
                if not preloaded:
                    S.dma("sp", x_t.ap, src_ap, reads=src_reads, writes=[x_t])
                ACT(junk.ap[0:np_, :], o_ap, AF.Square, [o_tile], [junk, sq], accum=sq.ap[:, 0:1])
                RSTD(r.ap[:, 0:1], sq.ap[:, 0:1], 1.0 / D_MODEL, [sq], [r], tm)
                STT("dve", o_ap, o_ap, r.ap[:, 0:1], gpost.ap[0:np_, :], ALU.mult, ALU.mult, [r, gpost], [o_tile])
                TT("dve", x_t.ap, x_t.ap, o_ap, ALU.add, [o_tile], [x_t])
                S.dma("sp", dst_ap, x_t.ap, reads=[x_t], writes=dst_tracks)

            def xsrc(t):
                r0 = seg * SEGT + t * 128
                tr = T_xres[seg * NT + t]
                return (xp[r0:r0 + 128, :], []) if l == 0 else (xres[r0:r0 + 128, :], [tr])

            s0, sr0 = xsrc(0)
            S.dma("sp", xt[0].ap, s0, reads=sr0, writes=[xt[0]])
            for t in range(NT):
                p = t % 2
                r0 = seg * SEGT + t * 128
                tr = T_xres[seg * NT + t]
                if t + 1 < NT:
                    s1, sr1 = xsrc(t + 1)
                    S.dma("sp", xt[1 - p].ap, s1, reads=sr1, writes=[xt[1 - p]])
                if last:
                    dst, dt_ = yp[r0:r0 + 128, :], []
                else:
                    dst, dt_ = xres[r0:r0 + 128, :], [tr]
                finish(OUT.ap[:, t, :], OUT, xt[p], 128, None, None, dst, dt_, ssq[p], rs[p], tmp[p], preloaded=True)
            if do_sample:
                xts = A.tile(2048, parts=SB)
                sq = A.tile(1, parts=SB); r = A.tile(1, parts=SB); tm = A.tile(1, parts=SB)
                src, sr = (xs, []) if l == 0 else (xsres, [T_xsres])
                dst, dt_ = (ys, []) if last else (xsres, [T_xsres])
                finish(outs.ap, outs, xts, SB, src, sr, dst, dt_, sq, r, tm)
            A.release(m)

        def decode_phase(l):
            m = A.mark()
            PJT = T_pj

            def ld(words, src, tr, parts=128, pat=None, **kw):
                t = A.tile(words, F32, pat, parts=parts, **kw)
                S.dma("sp", t.ap, src, reads=tr, writes=[t])
                return t

            def ldj(nj, c, nk, srcj, parts=128):
                t = A.tile(nj * c, F32, "p (j c) -> p j c", parts=parts, j=nj)
                for j in range(nj):
                    S.dma("sp", t.ap[:, j, :], srcj(j), writes=[t])
                return t

            def stj(dstj, t, nj):
                for j in range(nj):
                    S.dma("sp", dstj(j), t.ap[:, j, :], reads=[t])

            def pjv(c0, n, c):
                return pj[:, c0:c0 + n].rearrange("b (k c) -> k b c", c=c)

            rs = A.tile(1); tmp = A.tile(1)
            q = ld(128, pjv(C_Q, 1024, 128), PJT[0:8])
            k = ld(128, pjv(C_K, 1024, 128), PJT[8:16])
            vv = ld(128, pjv(C_V, 1024, 128), PJT[16:24])
            gr = ld(128, pjv(C_GR, 1024, 128), PJT[24:32])
            css = ld(256, cn["cs_s"], [])
            scs = ld(256, cn["sc_s"], [])
            gretP = ld(128, ret_norm[l].rearrange("(h e) -> h e", e=128).unsqueeze(1).to_broadcast([8, SB, 128]), [])
            qk = A.tile(256); AB = A.tile(256); CD = A.tile(256); qkr = A.tile(256, F32, "p (a d) -> p a d", a=2)
            CP("dve", qk.ap[:, 0:128], q.ap, [q], [qk])
            CP("dve", qk.ap[:, 128:256], k.ap, [k], [qk])
            TT("dve", AB.ap, qk.ap, css.ap, ALU.mult, [qk, css], [AB])
            TT("dve", CD.ap, qk.ap, scs.ap, ALU.mult, [qk, scs], [CD])
            AB4 = AB.ap.rearrange("p (a b f) -> p a b f", a=2, b=2)
            CD4 = CD.ap.rearrange("p (a b f) -> p a b f", a=2, b=2)
            TT("dve", qkr.ap[:, :, 0:64], AB4[:, :, 0, :], AB4[:, :, 1, :], ALU.subtract, [AB], [qkr])
            TT("dve", qkr.ap[:, :, 64:128], CD4[:, :, 0, :], CD4[:, :, 1, :], ALU.add, [CD], [qkr])
            qP = qkr.ap[:, 0, :]
            kP = qkr.ap[:, 1, :]
            ACT(gr.ap, gr.ap, AF.Silu, [gr], [gr])
            oacc = A.tile(128); opart = A.tile(128)
            pcs = [A.tile(1024, F32, "p (d e) -> p d e", d=8) for _ in range(4)]
            tmps = [A.tile(1024, F32, "p (d e) -> p d e", d=8) for _ in range(4)]
            tmpsB = [A.tile(1024, F32, "p (d e) -> p d e", d=8) for _ in range(4)]
            def ld_ret(pi):
                S.dma("sp", pcs[pi % 4].ap, st_ret[l][:, pi * 8:pi * 8 + 8, :], writes=[pcs[pi % 4]])
            for pi in range(3):
                ld_ret(pi)
            for pi in range(16):
                sp_ = pcs[pi % 4]; tp = tmps[pi % 4]; tq = tmpsB[pi % 4]
                d0 = pi * 8
                if pi + 3 < 16:
                    ld_ret(pi + 3)
                TT("pool", tp.ap, sp_.ap, qP[:, d0:d0 + 8].unsqueeze(2).to_broadcast([128, 8, 128]), ALU.mult, [sp_, qkr], [tp])
                dst = oacc if pi == 0 else opart
                S.op("dve", lambda e, o=dst.ap, i=tp.ap.rearrange("p d e -> p e d"): e.tensor_reduce(out=o, in_=i, axis=AX.X, op=ALU.add), [tp], [dst])
                if pi > 0:
                    TT("dve", oacc.ap, oacc.ap, opart.ap, ALU.add, [opart], [oacc])
                TT("pool", tq.ap, kP[:, d0:d0 + 8].unsqueeze(2).to_broadcast([128, 8, 128]), vv.ap.unsqueeze(1).to_broadcast([128, 8, 128]), ALU.mult, [qkr, vv], [tq])
                STT("dve", sp_.ap, sp_.ap, gam_p.ap[:, 0:1], tq.ap, ALU.mult, ALU.add, [gam_p, tq], [sp_])
                S.dma("sp", ret_s[l][:, d0:d0 + 8, :], sp_.ap, reads=[sp_])
            qkd = A.tile(128); qks = A.tile(1)
            TT("dve", qkd.ap, qP, kP, ALU.mult, [qkr], [qkd])
            S.op("dve", lambda e: e.tensor_reduce(out=qks.ap[:, 0:1], in_=qkd.ap, axis=AX.X, op=ALU.add), [qkd], [qks])
            TS("dve", oacc.ap, oacc.ap, gam_p.ap[:, 0:1], None, ALU.mult, None, [gam_p], [oacc])
            STT("dve", oacc.ap, vv.ap, qks.ap[:, 0:1], oacc.ap, ALU.mult, ALU.add, [vv, qks], [oacc])
            stats = A.tile(6); mv = A.tile(2)
            S.op("dve", lambda e: e.bn_stats(out=stats.ap[:, 0:6], in_=oacc.ap), [oacc], [stats])
            S.op("dve", lambda e: e.bn_aggr(out=mv.ap[:, 0:2], in_=stats.ap[:, 0:6]), [stats], [mv])
            RSTD(rs.ap[:, 0:1], mv.ap[:, 1:2], 1.0, [mv], [rs], tmp)
            TS("dve", oacc.ap, oacc.ap, mv.ap[:, 0:1], rs.ap[:, 0:1], ALU.subtract, ALU.mult, [mv, rs], [oacc])
            TT("dve", oacc.ap, oacc.ap, gretP.ap, ALU.mult, [gretP], [oacc])
            ob = A.tile(64, BF16)
            TT("dve", ob.ap, oacc.ap, gr.ap, ALU.mult, [oacc, gr], [ob])
            TRS(7, [(pbf(7)[:, 0:128], ob.ap, identb.ap)], [ob, identb])
            CP("dve", mixTs.ap[:, 0:8, :], pbf(7)[:, 0:128].rearrange("p (k b) -> p k b", k=8), [], [PT[7], mixTs])
            A.release(m)
            m = A.mark()
            bg = ld(128, pjv(C_BG, 1024, 128), PJT[32:40])
            cg = ld(128, pjv(C_CG, 1024, 128), PJT[40:48])
            hh_ = ld(128, pjv(C_H, 1024, 128), PJT[48:56])
            gsc = ld(128, pjv(C_GSC, 1024, 128), PJT[56:64])
            wP = ldj(3, 128, 8, lambda j: sc_w[l][j].rearrange("(k c) -> k c", c=128).unsqueeze(1).to_broadcast([8, SB, 128]))
            bP = ld(128, sc_b[l].rearrange("(k c) -> k c", c=128).unsqueeze(1).to_broadcast([8, SB, 128]), [])
            buf = ldj(2, 128, 8, lambda j: st_sc[l][:, j, :].rearrange("b (k c) -> k b c", c=128))
            nst = A.tile(256, F32, "p (j c) -> p j c", j=2)
            acc = A.tile(128)
            TT("dve", nst.ap[:, 1, :], cg.ap, hh_.ap, ALU.mult, [cg, hh_], [nst])
            CP("dve", nst.ap[:, 0, :], buf.ap[:, 1, :], [buf], [nst])
            TT("dve", acc.ap, buf.ap[:, 0, :], wP.ap[:, 0, :], ALU.mult, [buf, wP], [acc])
            TT("dve", acc.ap, acc.ap, bP.ap, ALU.add, [bP], [acc])
            TT("dve", bP.ap, buf.ap[:, 1, :], wP.ap[:, 1, :], ALU.mult, [buf, wP], [bP])
            TT("dve", acc.ap, acc.ap, bP.ap, ALU.add, [bP], [acc])
            TT("dve", bP.ap, nst.ap[:, 1, :], wP.ap[:, 2, :], ALU.mult, [nst, wP], [bP])
            TT("dve", acc.ap, acc.ap, bP.ap, ALU.add, [bP], [acc])
            ACT(gsc.ap, gsc.ap, AF.Silu, [gsc], [gsc])
            TT("dve", acc.ap, acc.ap, gsc.ap, ALU.mult, [gsc], [acc])
            ob = A.tile(64, BF16)
            TT("dve", ob.ap, acc.ap, bg.ap, ALU.mult, [acc, bg], [ob])
            stj(lambda j: sc_s[l][:, j, :].rearrange("b (k c) -> k b c", c=128), nst, 2)
            TRS(7, [(pbf(7)[:, 0:128], ob.ap, identb.ap)], [ob, identb])
            CP("dve", mixTs.ap[:, 8:16, :], pbf(7)[:, 0:128].rearrange("p (k b) -> p k b", k=8), [], [PT[7], mixTs])
            A.release(m)
            m = A.mark()
            xr = ld(256, pjv(C_XBC, 2048, 256), PJT[80:96])
            zz = ld(256, pjv(C_Z, 2048, 256), PJT[64:80])
            wx = ldj(4, 256, 8, lambda j: ssm_w[l][j, 0:2048].rearrange("(k c) -> k c", c=256).unsqueeze(1).to_broadcast([8, SB, 256]))
            bx = ld(256, ssm_b[l][0:2048].rearrange("(k c) -> k c", c=256).unsqueeze(1).to_broadcast([8, SB, 256]), [])
            bufx = ldj(3, 256, 8, lambda j: st_sconv[l][:, j, 0:2048].rearrange("b (k c) -> k b c", c=256))
            nstx = A.tile(768, F32, "p (j c) -> p j c", j=3)
            xc = A.tile(256); t256 = A.tile(256)
            CP("dve", nstx.ap[:, 0:2, :], bufx.ap[:, 1:3, :], [bufx], [nstx])
            CP("dve", nstx.ap[:, 2, :], xr.ap, [xr], [nstx])
            stj(lambda j: sconv_s[l][:, j, 0:2048].rearrange("b (k c) -> k b c", c=256), nstx, 3)
            TT("dve", xc.ap, xr.ap, wx.ap[:, 3, :], ALU.mult, [xr, wx], [xc])
            TT("dve", xc.ap, xc.ap, bx.ap, ALU.add, [bx], [xc])
            for j in range(3):
                TT("dve", t256.ap, bufx.ap[:, j, :], wx.ap[:, j, :], ALU.mult, [bufx, wx], [t256])
                TT("dve", xc.ap, xc.ap, t256.ap, ALU.add, [t256], [xc])
            ACT(xc.ap, xc.ap, AF.Silu, [xc], [xc])
            ACT(zz.ap, zz.ap, AF.Silu, [zz], [zz])
            br = ld(256, pjv(C_XBC + 2048, 1024, 256), PJT[96:104], parts=64)
            wbc = ldj(4, 256, 4, lambda j: ssm_w[l][j, 2048:3072].rearrange("(k c) -> k c", c=256).unsqueeze(1).to_broadcast([4, SB, 256]), parts=64)
            bbc = ld(256, ssm_b[l][2048:3072].rearrange("(k c) -> k c", c=256).unsqueeze(1).to_broadcast([4, SB, 256]), [], parts=64)
            bufb = ldj(3, 256, 4, lambda j: st_sconv[l][:, j, 2048:3072].rearrange("b (k c) -> k b c", c=256), parts=64)
            nstb = A.tile(768, F32, "p (j c) -> p j c", j=3, parts=64)
            bc = A.tile(256, parts=64); tb256 = A.tile(256, parts=64)
            CP("dve", nstb.ap[:, 0:2, :], bufb.ap[:, 1:3, :], [bufb], [nstb])
            CP("dve", nstb.ap[:, 2, :], br.ap, [br], [nstb])
            stj(lambda j: sconv_s[l][:, j, 2048:3072].rearrange("b (k c) -> k b c", c=256), nstb, 3)
            TT("dve", bc.ap, br.ap, wbc.ap[:, 3, :], ALU.mult, [br, wbc], [bc])
            TT("dve", bc.ap, bc.ap, bbc.ap, ALU.add, [bbc], [bc])
            for j in range(3):
                TT("dve", tb256.ap, bufb.ap[:, j, :], wbc.ap[:, j, :], ALU.mult, [bufb, wbc], [tb256])
                TT("dve", bc.ap, bc.ap, tb256.ap, ALU.add, [tb256], [bc])
            ACT(bc.ap, bc.ap, AF.Silu, [bc], [bc])
            MM(4, [(PB[4][:, 0:128], sel4.ap[:, 0, :], bc.ap[:, 0:128], True, False),
                   (PB[4][:, 0:128], sel4.ap[:, 1, :], bc.ap[:, 128:256], False, True),
                   (PB[4][:, 128:256], sel4.ap[:, 2, :], bc.ap[:, 0:128], True, False),
                   (PB[4][:, 128:256], sel4.ap[:, 3, :], bc.ap[:, 128:256], False, True)], [sel4, bc])
            bcP = A.tile(256)
            CP("dve", bcP.ap, PB[4][:, 0:256], [], [PT[4], bcP])
            bmP = bcP.ap[:, 0:128]
            cmP = bcP.ap[:, 128:256]
            dtr = ld(4, pj[:, C_DT:C_DT + 32].rearrange("b (k c) -> k b c", c=4), [PJT[104]])
            dtbP = ld(4, dt_bias[l].rearrange("(k c) -> k c", c=4).unsqueeze(1).to_broadcast([8, SB, 4]), [])
            AP_ = ld(4, a_log[l].rearrange("(k c) -> k c", c=4).unsqueeze(1).to_broadcast([8, SB, 4]), [])
            DP = ld(4, d_skip[l].rearrange("(k c) -> k c", c=4).unsqueeze(1).to_broadcast([8, SB, 4]), [])
            x4 = A.tile(4); a4 = A.tile(4); l4 = A.tile(4); dtP = A.tile(4); eaP = A.tile(4); coef = A.tile(4); cbP = A.tile(1)
            TT("dve", x4.ap[:, 0:4], dtr.ap[:, 0:4], dtbP.ap[:, 0:4], ALU.add, [dtr, dtbP], [x4])
            STT("dve", a4.ap[:, 0:4], x4.ap[:, 0:4], -1.0, x4.ap[:, 0:4], ALU.mult, ALU.max, [x4], [a4])
            ACT(a4.ap[:, 0:4], a4.ap[:, 0:4], AF.Exp, [a4], [a4], scale=-1.0)
            ACT(l4.ap[:, 0:4], a4.ap[:, 0:4], AF.Ln, [a4], [l4], bias=1.0)
            STT("dve", dtP.ap[:, 0:4], x4.ap[:, 0:4], 0.0, l4.ap[:, 0:4], ALU.max, ALU.add, [x4, l4], [dtP])
            ACT(AP_.ap[:, 0:4], AP_.ap[:, 0:4], AF.Exp, [AP_], [AP_])
            TT("dve", a4.ap[:, 0:4], dtP.ap[:, 0:4], AP_.ap[:, 0:4], ALU.mult, [dtP, AP_], [a4])
            ACT(eaP.ap[:, 0:4], a4.ap[:, 0:4], AF.Exp, [a4], [eaP], scale=-1.0)
            tb128 = A.tile(128)
            TT("dve", tb128.ap, bmP, cmP, ALU.mult, [bcP], [tb128])
            S.op("dve", lambda e: e.tensor_reduce(out=cbP.ap[:, 0:1], in_=tb128.ap, axis=AX.X, op=ALU.add), [tb128], [cbP])
            STT("dve", coef.ap[:, 0:4], dtP.ap[:, 0:4], cbP.ap[:, 0:1], DP.ap[:, 0:4], ALU.mult, ALU.add, [dtP, cbP, DP], [coef])
            xdt = A.tile(256); yS = A.tile(256)
            xc3 = xc.ap.rearrange("p (h q) -> p h q", h=4)
            TT("dve", xdt.ap.rearrange("p (h q) -> p h q", h=4), xc3, dtP.ap[:, 0:4].unsqueeze(2).to_broadcast([128, 4, 64]), ALU.mult, [xc, dtP], [xdt])
            pcs = [A.tile(1024, F32, "p (q n) -> p q n", q=8) for _ in range(4)]
            tmps = [A.tile(1024, F32, "p (q n) -> p q n", q=8) for _ in range(4)]
            tmpsB = [A.tile(1024, F32, "p (q n) -> p q n", q=8) for _ in range(4)]
            def ld_ssm(pi):
                hl_, p0_ = pi // 8, (pi % 8) * 8
                S.dma("sp", pcs[pi % 4].ap, st_ssm[l][:, hl_, p0_:p0_ + 8, :], writes=[pcs[pi % 4]])
            for pi in range(3):
                ld_ssm(pi)
            for pi in range(32):
                hl, p0 = pi // 8, (pi % 8) * 8
                sp_ = pcs[pi % 4]; tp = tmps[pi % 4]; tq = tmpsB[pi % 4]
                if pi + 3 < 32:
                    ld_ssm(pi + 3)
                TT("pool", tp.ap, sp_.ap, cmP.unsqueeze(1).to_broadcast([128, 8, 128]), ALU.mult, [sp_, bcP], [tp])
                S.op("dve", lambda e, o=yS.ap[:, hl * 64 + p0:hl * 64 + p0 + 8], i=tp.ap: e.tensor_reduce(out=o, in_=i, axis=AX.X, op=ALU.add), [tp], [yS])
                TT("pool", tq.ap, xdt.ap[:, hl * 64 + p0:hl * 64 + p0 + 8].unsqueeze(2).to_broadcast([128, 8, 128]),
                   bmP.unsqueeze(1).to_broadcast([128, 8, 128]), ALU.mult, [xdt, bcP], [tq])
                STT("dve", sp_.ap, sp_.ap, eaP.ap[:, hl:hl + 1], tq.ap, ALU.mult, ALU.add, [eaP, tq], [sp_])
                S.dma("sp", ssm_s[l][:, hl, p0:p0 + 8, :], sp_.ap, reads=[sp_])
            y = A.tile(256)
            TT("dve", y.ap.rearrange("p (h q) -> p h q", h=4), xc3, coef.ap[:, 0:4].unsqueeze(2).to_broadcast([128, 4, 64]), ALU.mult, [xc, coef], [y])
            TT("dve", yS.ap.rearrange("p (h q) -> p h q", h=4), yS.ap.rearrange("p (h q) -> p h q", h=4),
               eaP.ap[:, 0:4].unsqueeze(2).to_broadcast([128, 4, 64]), ALU.mult, [eaP], [yS])
            TT("dve", y.ap, y.ap, yS.ap, ALU.add, [yS], [y])
            TT("dve", y.ap, y.ap, zz.ap, ALU.mult, [zz], [y])
            ssq = A.tile(1)
            ACT(t256.ap, y.ap, AF.Square, [y], [t256, ssq], accum=ssq.ap[:, 0:1])
            MM(4, [(PB[4][:, 0:1], selb.ap, ssq.ap[:, 0:1], True, True)], [selb, ssq])
            tot = A.tile(1)
            CP("dve", tot.ap[:, 0:1], PB[4][:, 0:1], [], [PT[4], tot])
            RSTD(rs.ap[:, 0:1], tot.ap[:, 0:1], 1.0 / 2048, [tot], [rs], tmp)
            gsP = ld(256, ssm_norm[l].rearrange("(k c) -> k c", c=256).unsqueeze(1).to_broadcast([8, SB, 256]), [])
            ob2 = A.tile(128, BF16)
            STT("dve", ob2.ap, y.ap, rs.ap[:, 0:1], gsP.ap, ALU.mult, ALU.mult, [y, rs, gsP], [ob2])
            TRS(7, [(pbf(7)[:, r * 128:(r + 1) * 128], ob2.ap[:, r * 128:(r + 1) * 128], identb.ap) for r in range(2)], [ob2, identb])
            mv_ = mixTs.ap[:, 16:32, :].rearrange("p (k r) b -> p r k b", r=2)
            for r in range(2):
                CP("dve", mv_[:, r, :, :], pbf(7)[:, r * 128:(r + 1) * 128].rearrange("p (k b) -> p k b", k=8), [], [PT[7], mixTs])
            A.release(m)

        def write_prompt_states(l):
            m = A.mark()
            S.dma("sp", ret_p[l].rearrange("h d e -> d h e"), S_ret.ap, reads=[S_ret])
            for j in range(2):
                S.dma("sp", sc_p[l][j].rearrange("(c p) -> p c", p=128), Ucar.ap[:, :, j], reads=[Ucar], allow_slow_non_contiguous=True)
            for j in range(3):
                S.dma("sp", sconv_p[l][j].rearrange("(c p) -> p c", p=128), XBcar.ap[:, :, j], reads=[XBcar], allow_slow_non_contiguous=True)
            so = [A.tile(512), A.tile(512)]
            for q in range(4):
                TRS(q % 2, [(PB[q % 2][:, k * 128:(k + 1) * 128], sT.ap[:, (q * 4 + k) * 128:(q * 4 + k + 1) * 128], identf.ap) for k in range(4)], [sT, identf])
                CP("dve", so[q % 2].ap, PB[q % 2][:, 0:512], [], [PT[q % 2], so[q % 2]])
                S.dma("sp", ssm_p[l][q * 8:(q + 1) * 8].rearrange("(k a) p n -> (a p) k n", a=2), so[q % 2].ap.rearrange("x (k n) -> x k n", k=4), reads=[so[q % 2]])
            A.release(m)

        for l in range(L):
            load_layer_params(l)
            for seg in range(nseg):
                ds = (seg == 0)
                if seg == 0:
                    build_hT(l, seg)
                if ds:
                    build_hTs(l)
                mret_, units, stage_ = ret_phase(l, seg, ds)
                sc_phase(l, seg, ds, units, stage_)
                A.release(mret_)
                ssd_phase(l, seg, ds)
                if ds:
                    decode_phase(l)
                nxt_l = l if seg + 1 < nseg else (l + 1 if l + 1 < L else None)
                out_phase(l, seg, ds, (lambda l=l, seg=seg: build_hT(l, seg + 1)) if seg + 1 < nseg else None, nxt_l)
            write_prompt_states(l)
        S.emit(st)
        build_program.stats = dict(ops=len(S.ops), waits=S.nwaits, arena_peak=A.peak)
    return nc


_CACHE = {}


def _run(inputs, depth, nseg):
    key = (depth, nseg)
    if key not in _CACHE:
        _CACHE[key] = build_program(depth, nseg)
    nc = _CACHE[key]
    seqlen = nseg * SEGT
    f = lambda a: np.ascontiguousarray(np.asarray(a, dtype=np.float32))
    consts = _consts(seqlen)
    L = depth
    shared = {k: f(inputs[k]) for k in ("w_in", "w_out", "norm_pre", "norm_post", "ret_norm", "sc_conv_w", "sc_conv_b",
                                        "ssm_conv_w", "ssm_conv_b", "ssm_dt_bias", "ssm_a_log", "ssm_d", "ssm_norm")}
    for k, v in consts.items():
        shared["c_" + k] = f(v)
    x_prompt = np.asarray(inputs["x_prompt"], np.float32)
    x_sample = np.asarray(inputs["x_sample"], np.float32)
    s_ret = np.asarray(inputs["state_ret"], np.float32)
    s_sc = np.asarray(inputs["state_sconv"], np.float32)
    s_sconv = np.asarray(inputs["state_ssm_conv"], np.float32)
    s_ssm = np.asarray(inputs["state_ssm"], np.float32)
    in_maps = []
    for c in range(8):
        b0 = c * SB
        d = dict(shared)
        d["xp"] = f(x_prompt[c % BATCH])
        d["xs"] = f(x_sample[b0:b0 + SB, 0, :])
        d["st_ret"] = f(s_ret[:, b0:b0 + SB].transpose(0, 2, 1, 3, 4).reshape(L, 128, 128, 128))
        d["st_sc"] = f(s_sc[:, b0:b0 + SB])
        d["st_sconv"] = f(s_sconv[:, b0:b0 + SB])
        d["st_ssm"] = f(s_ssm[:, b0:b0 + SB].reshape(L, SB, 8, 4, 64, 128).transpose(0, 2, 1, 3, 4, 5).reshape(L, 128, 4, 64, 128))
        in_maps.append(d)
    res = run_bass_kernel_spmd(nc, in_maps, core_ids=list(range(8)))
    R = res.results
    yp = np.stack([R[b]["yp"] for b in range(BATCH)])
    ys = np.concatenate([R[c]["ys"] for c in range(8)])[:, None, :]
    ret_p = np.stack([R[b]["ret_p"] for b in range(BATCH)], axis=1)
    sc_p = np.stack([R[b]["sc_p"] for b in range(BATCH)], axis=1)
    sconv_p = np.stack([R[b]["sconv_p"] for b in range(BATCH)], axis=1)
    ssm_p = np.stack([R[b]["ssm_p"] for b in range(BATCH)], axis=1)
    ret_s = np.concatenate([R[c]["ret_s"].reshape(L, 8, SB, 128, 128).transpose(0, 2, 1, 3, 4) for c in range(8)], axis=1)
    sc_s = np.concatenate([R[c]["sc_s"] for c in range(8)], axis=1)
    sconv_s = np.concatenate([R[c]["sconv_s"] for c in range(8)], axis=1)
    ssm_s = np.concatenate([R[c]["ssm_s"].reshape(L, 8, SB, 4, 64, 128).transpose(0, 2, 1, 3, 4, 5).reshape(L, SB, 32, 64, 128)
                            for c in range(8)], axis=1)
    outs = (yp, ys, ret_p, sc_p, sconv_p, ssm_p, ret_s, sc_s, sconv_s, ssm_s)
    return tuple(np.ascontiguousarray(o, dtype=np.float32) for o in outs)


def kernel(**inputs):
    depth = int(np.asarray(inputs["w_in"]).shape[0])
    nseg = int(np.asarray(inputs["x_prompt"]).shape[1]) // SEGT
    return _run(inputs, depth, nseg)
```
